# Optimizing a Trainium2 kernel written in Bass

```python
import jax, jax.numpy as jnp
from jax import lax
import numpy as np

D_MODEL = 1024
BATCH = 16
SEQ = 4096
DEPTH = 4

GRID_W = 64
CTX_LEN = 256
N_MIXERS = 3
N_LAYERS_A = (DEPTH + 2) // 3
N_LAYERS_B = (DEPTH + 1) // 3
N_LAYERS_C = DEPTH // 3

N_FOURIER_GROUPS = 8
FOURIER_GROUP = D_MODEL // N_FOURIER_GROUPS

HEAD_DIM = 128
N_HEADS = D_MODEL // HEAD_DIM
N_KV_HEADS = 2
Q_PER_KV = N_HEADS // N_KV_HEADS
ROPE_THETA = 10000.0
Q_BLOCK = 128

D_RNN = 5 * D_MODEL // 4
LRU_BLOCK = 128
N_LRU_BLOCKS = D_RNN // LRU_BLOCK
CONV_W = 4
LRU_C = 8.0

D_FF = 4 * D_MODEL
EPS = 1e-6

kernel_name = 'hybrid_fnet_gqa_rglru_diffusion_trunk'


def _rmsnorm(x, g):
    x32 = x.astype(jnp.float32)
    y = x32 * lax.rsqrt(jnp.mean(x32 * x32, axis=-1, keepdims=True) + EPS)
    return (y * g.astype(jnp.float32)).astype(x.dtype)


def _modulate(h, shift, scale):
    return h * (1 + scale) + shift


def _squared_relu_mlp(h, w1, w2):
    return jnp.square(jax.nn.relu(h @ w1)) @ w2


def _fourier_mixer(h, w_in, w_out):
    B, T, _ = h.shape
    u = (h @ w_in).reshape(B, T, N_FOURIER_GROUPS, FOURIER_GROUP).astype(jnp.float32)
    f = jnp.fft.fft2(u, axes=(1, 3), norm='ortho').real.astype(h.dtype)
    return f.reshape(B, T, D_MODEL) @ w_out


def _axial_angles(n_tokens):
    rows = n_tokens // GRID_W
    row = jnp.repeat(jnp.arange(rows, dtype=jnp.int32), GRID_W)
    col = jnp.tile(jnp.arange(GRID_W, dtype=jnp.int32), rows)
    n_freq = HEAD_DIM // 4
    inv = ROPE_THETA ** (-jnp.arange(n_freq, dtype=jnp.float32) / n_freq)
    ang = jnp.concatenate([row.astype(jnp.float32)[:, None] * inv,
                           col.astype(jnp.float32)[:, None] * inv], axis=-1)
    return jnp.cos(ang), jnp.sin(ang)


def _rope(x, cos, sin):
    xp = x.astype(jnp.float32).reshape(*x.shape[:-1], HEAD_DIM // 2, 2)
    x0, x1 = xp[..., 0], xp[..., 1]
    cs, sn = cos[None, :, None, :], sin[None, :, None, :]
    out = jnp.stack([x0 * cs - x1 * sn, x0 * sn + x1 * cs], axis=-1)
    return out.reshape(x.shape).astype(x.dtype)


def _qkv(h, w_qkv, q_g, k_g):
    B, T, _ = h.shape
    q, k, v = jnp.split(h @ w_qkv, [N_HEADS * HEAD_DIM, (N_HEADS + N_KV_HEADS) * HEAD_DIM], axis=-1)
    q = _rmsnorm(q.reshape(B, T, N_HEADS, HEAD_DIM), q_g)
    k = _rmsnorm(k.reshape(B, T, N_KV_HEADS, HEAD_DIM), k_g)
    v = v.reshape(B, T, N_KV_HEADS, HEAD_DIM)
    return q, k, v


def _attend(q, k, v):
    s = jnp.einsum('bqkgd,bskd->bkgqs', q, k).astype(jnp.float32) * (HEAD_DIM ** -0.5)
    p = jax.nn.softmax(s, axis=-1).astype(v.dtype)
    return jnp.einsum('bkgqs,bskd->bqkgd', p, v)


def _attention_mixer(h_lat, h_ctx, w_qkv, w_o, q_g, k_g, ctx_out):
    B, S, _ = h_lat.shape
    Tc = h_ctx.shape[1]
    cos, sin = _axial_angles(S)
    q_l, k_l, v_l = _qkv(h_lat, w_qkv, q_g, k_g)
    q_l, k_l = _rope(q_l, cos, sin), _rope(k_l, cos, sin)
    q_c, k_c, v_c = _qkv(h_ctx, w_qkv, q_g, k_g)
    k_all = jnp.concatenate([k_l, k_c], axis=1)
    v_all = jnp.concatenate([v_l, v_c], axis=1)
    n_blk = S // Q_BLOCK
    qb = q_l.reshape(B, n_blk, Q_BLOCK, N_KV_HEADS, Q_PER_KV, HEAD_DIM).transpose(1, 0, 2, 3, 4, 5)
    ob = lax.map(lambda qblk: _attend(qblk, k_all, v_all), qb)
    y_l = ob.transpose(1, 0, 2, 3, 4, 5).reshape(B, S, N_HEADS * HEAD_DIM) @ w_o
    y_c = None
    if ctx_out:
        o_c = _attend(q_c.reshape(B, Tc, N_KV_HEADS, Q_PER_KV, HEAD_DIM), k_c, v_c)
        y_c = o_c.reshape(B, Tc, N_HEADS * HEAD_DIM) @ w_o
    return y_l, y_c


def _centred_conv(x, w, b):
    T = x.shape[1]
    left = CONV_W // 2
    xp = jnp.pad(x, ((0, 0), (left, CONV_W - 1 - left), (0, 0)))
    y = b
    for k in range(CONV_W):
        y = y + xp[:, k:k + T] * w[k]
    return y


def _block_diag(x, w, b):
    B, T, _ = x.shape
    xb = x.reshape(B, T, N_LRU_BLOCKS, LRU_BLOCK)
    return jnp.einsum('btnk,nkj->btnj', xb, w).reshape(B, T, D_RNN) + b


def _linear_scan(a, u, h0):
    u = u.at[:, 0].add(a[:, 0] * h0)
    def comb(l, r):
        return (l[0] * r[0], r[0] * l[1] + r[1])
    _, h = lax.associative_scan(comb, (a, u), axis=1)
    return h


def _rglru_dir(x, wa, ba, wx, bx, lam, h0, reverse):
    r = jax.nn.sigmoid(_block_diag(x, wa, ba)).astype(jnp.float32)
    i = jax.nn.sigmoid(_block_diag(x, wx, bx)).astype(jnp.float32)
    log_a = -LRU_C * r * jax.nn.softplus(-lam.astype(jnp.float32))
    a = jnp.exp(log_a)
    u = jnp.sqrt(-jnp.expm1(2.0 * log_a)) * (i * x.astype(jnp.float32))
    if reverse:
        a, u = jnp.flip(a, axis=1), jnp.flip(u, axis=1)
    h = _linear_scan(a, u, h0)
    final = h[:, -1]
    if reverse:
        h = jnp.flip(h, axis=1)
    return h, final


def _lru_mixer(h_lat, h_ctx, w_in, conv_w, conv_b, ga_w, ga_b, gx_w, gx_b, lam, w_out, ctx_out):
    def branches(h):
        g, xr = jnp.split(h @ w_in, 2, axis=-1)
        return jax.nn.gelu(g), _centred_conv(xr, conv_w, conv_b)
    g_c, x_c = branches(h_ctx)
    g_l, x_l = branches(h_lat)
    B = h_lat.shape[0]
    h0 = jnp.zeros((B, D_RNN), jnp.float32)
    rec_l, rec_c = None, None
    for d, rev in enumerate((False, True)):
        p = (ga_w[d], ga_b[d], gx_w[d], gx_b[d], lam[d])
        hc, fin_c = _rglru_dir(x_c, *p, h0, rev)
        hl, _ = _rglru_dir(x_l, *p, fin_c, rev)
        rec_l = hl if rec_l is None else rec_l + hl
        rec_c = hc if rec_c is None else rec_c + hc
    y_l = (g_l * rec_l.astype(g_l.dtype)) @ w_out
    y_c = (g_c * rec_c.astype(g_c.dtype)) @ w_out if ctx_out else None
    return y_l, y_c


def setup_inputs(seed: int = 0) -> dict:
    key = jax.random.key(seed)
    ks = jax.random.split(key, 32)
    f32 = jnp.float32

    def nrm(k, shape, scale):
        return jax.random.normal(k, shape, f32) * scale

    u = jax.random.uniform(ks[24], (N_LAYERS_C, 2, D_RNN), f32, 0.9, 0.999)
    return {
        'x': nrm(ks[0], (BATCH, SEQ, D_MODEL), 1.0),
        'c': nrm(ks[1], (BATCH, D_MODEL), 1.0),
        'ctx': nrm(ks[2], (BATCH, CTX_LEN, D_MODEL), 1.0),
        'c_ctx': nrm(ks[3], (D_MODEL,), 1.0),
        'ada_w': nrm(ks[4], (DEPTH, D_MODEL, 6 * D_MODEL), D_MODEL ** -0.5),
        'ada_b': nrm(ks[5], (DEPTH, 6 * D_MODEL), 0.02),
        'norm_mix_g': 1.0 + nrm(ks[6], (DEPTH, D_MODEL), 0.05),
        'norm_mlp_g': 1.0 + nrm(ks[7], (DEPTH, D_MODEL), 0.05),
        'mlp_w1': nrm(ks[8], (DEPTH, D_MODEL, D_FF), D_MODEL ** -0.5),
        'mlp_w2': nrm(ks[9], (DEPTH, D_FF, D_MODEL), D_FF ** -0.5),
        'fnet_w_in': nrm(ks[10], (N_LAYERS_A, D_MODEL, D_MODEL), D_MODEL ** -0.5),
        'fnet_w_out': nrm(ks[11], (N_LAYERS_A, D_MODEL, D_MODEL), D_MODEL ** -0.5),
        'attn_w_qkv': nrm(ks[12], (N_LAYERS_B, D_MODEL, (N_HEADS + 2 * N_KV_HEADS) * HEAD_DIM), D_MODEL ** -0.5),
        'attn_w_o': nrm(ks[13], (N_LAYERS_B, N_HEADS * HEAD_DIM, D_MODEL), (N_HEADS * HEAD_DIM) ** -0.5),
        'attn_q_norm_g': 1.0 + nrm(ks[14], (N_LAYERS_B, HEAD_DIM), 0.05),
        'attn_k_norm_g': 1.0 + nrm(ks[15], (N_LAYERS_B, HEAD_DIM), 0.05),
        'lru_w_in': nrm(ks[16], (N_LAYERS_C, D_MODEL, 2 * D_RNN), D_MODEL ** -0.5),
        'lru_conv_w': nrm(ks[17], (N_LAYERS_C, CONV_W, D_RNN), CONV_W ** -0.5),
        'lru_conv_b': nrm(ks[18], (N_LAYERS_C, D_RNN), 0.02),
        'lru_gate_a_w': nrm(ks[19], (N_LAYERS_C, 2, N_LRU_BLOCKS, LRU_BLOCK, LRU_BLOCK), LRU_BLOCK ** -0.5),
        'lru_gate_a_b': nrm(ks[20], (N_LAYERS_C, 2, D_RNN), 0.02),
        'lru_gate_x_w': nrm(ks[21], (N_LAYERS_C, 2, N_LRU_BLOCKS, LRU_BLOCK, LRU_BLOCK), LRU_BLOCK ** -0.5),
        'lru_gate_x_b': nrm(ks[22], (N_LAYERS_C, 2, D_RNN), 0.02),
        'lru_lambda': jnp.log(u) - jnp.log1p(-u),
        'lru_w_out': nrm(ks[23], (N_LAYERS_C, D_RNN, D_MODEL), D_RNN ** -0.5),
        'final_norm_g': 1.0 + nrm(ks[25], (D_MODEL,), 0.05),
    }


def reference(x, c, ctx, c_ctx, ada_w, ada_b, norm_mix_g, norm_mlp_g, mlp_w1, mlp_w2,
              fnet_w_in, fnet_w_out, attn_w_qkv, attn_w_o, attn_q_norm_g, attn_k_norm_g,
              lru_w_in, lru_conv_w, lru_conv_b, lru_gate_a_w, lru_gate_a_b, lru_gate_x_w,
              lru_gate_x_b, lru_lambda, lru_w_out, final_norm_g):
    s_lat = jax.nn.silu(c)
    s_ctx = jax.nn.silu(c_ctx)[None]
    for i in range(DEPTH):
        kind = i % N_MIXERS
        j = i // N_MIXERS
        last = i == DEPTH - 1
        ctx_out = not last
        need_ctx_side = ctx_out or kind != 0

        m_l = (s_lat @ ada_w[i] + ada_b[i])[:, None, :]
        sh1, sc1, g1, sh2, sc2, g2 = jnp.split(m_l, 6, axis=-1)
        h_l = _modulate(_rmsnorm(x, norm_mix_g[i]), sh1, sc1)
        h_c = None
        if need_ctx_side:
            m_c = (s_ctx @ ada_w[i] + ada_b[i])[:, None, :]
            csh1, csc1, cg1, csh2, csc2, cg2 = jnp.split(m_c, 6, axis=-1)
            h_c = _modulate(_rmsnorm(ctx, norm_mix_g[i]), csh1, csc1)

        if kind == 0:
            y_l = _fourier_mixer(h_l, fnet_w_in[j], fnet_w_out[j])
            y_c = _fourier_mixer(h_c, fnet_w_in[j], fnet_w_out[j]) if ctx_out else None
        elif kind == 1:
            y_l, y_c = _attention_mixer(h_l, h_c, attn_w_qkv[j], attn_w_o[j],
                                        attn_q_norm_g[j], attn_k_norm_g[j], ctx_out)
        else:
            y_l, y_c = _lru_mixer(h_l, h_c, lru_w_in[j], lru_conv_w[j], lru_conv_b[j],
                                  lru_gate_a_w[j], lru_gate_a_b[j], lru_gate_x_w[j],
                                  lru_gate_x_b[j], lru_lambda[j], lru_w_out[j], ctx_out)

        x = x + g1 * y_l
        x = x + g2 * _squared_relu_mlp(_modulate(_rmsnorm(x, norm_mlp_g[i]), sh2, sc2), mlp_w1[i], mlp_w2[i])
        if ctx_out:
            ctx = ctx + cg1 * y_c
            ctx = ctx + cg2 * _squared_relu_mlp(_modulate(_rmsnorm(ctx, norm_mlp_g[i]), csh2, csc2),
                                                mlp_w1[i], mlp_w2[i])
    return _rmsnorm(x, final_norm_g)
```

```python
import numpy as np
import ml_dtypes
import concourse.bass as bass
import concourse.mybir as mybir
from concourse.bass_utils import run_bass_kernel_spmd

F32 = mybir.dt.float32
BF16 = mybir.dt.bfloat16
AF = mybir.ActivationFunctionType
ALU = mybir.AluOpType

D = 1024
NSEQ = 2
TC = 256
TL = 4096
TT = TC + TL
DFF = 4096
DRNN = 1280
DEPTH = 4
EPS = 1e-6
SB_BASE = 16640
SB_LIMIT = 229376

ENGS = ("sp", "pool", "act", "dve", "pe")


class Buf:
    __slots__ = ("name", "lastw", "readers", "sems")

    def __init__(self, name):
        self.name = name
        self.lastw = None
        self.readers = {}
        self.sems = {}


class Op:
    __slots__ = ("eng", "fn", "deps", "is_dma", "signal", "sigval", "sem", "semval", "kind", "bar", "epoch")

    def __init__(self, eng, fn, is_dma):
        self.eng = eng
        self.fn = fn
        self.deps = []
        self.is_dma = is_dma
        self.signal = False
        self.sigval = 0
        self.sem = None
        self.semval = 0
        self.kind = "op"
        self.bar = None
        self.epoch = 0


class Prog:
    def __init__(self, nc):
        self.nc = nc
        self.q = {e: [] for e in ENGS}
        self.sb_off = SB_BASE
        self.sb_marks = []
        self.uid = 0
        self.free_sems = {"sp": [], "act": [], "pool": []}
        self.semval = {}
        self.phase_bufs = []
        self.outstanding = {}
        self.nbar = 0
        self.nsem = 0
        self.engsem = {e: nc.alloc_semaphore("eng_" + e) for e in ("pool", "act", "dve", "pe")}
        self.barsem = nc.alloc_semaphore("barrier")
        self.psum = []
        for i in range(8):
            t = nc.alloc_psum_tensor("psb%d" % i, [128, 512], F32)
            self.psum.append((t, Buf("psum%d" % i)))

    def sb(self, shape, dtype, name="t"):
        self.uid += 1
        nbytes = int(np.prod(shape[1:])) * (2 if dtype == BF16 else 4)
        nbytes = (nbytes + 63) // 64 * 64
        off = self.sb_off
        if off + nbytes > SB_LIMIT:
            raise RuntimeError("SBUF overflow allocating %s: %d + %d" % (name, off, nbytes))
        self.sb_off += nbytes
        t = self.nc.alloc_sbuf_tensor_at("%s_%d" % (name, self.uid), list(shape), dtype, offset=off)
        b = Buf(name)
        self.phase_bufs.append(b)
        return t, b

    def newbuf(self, name):
        b = Buf(name)
        self.phase_bufs.append(b)
        return b

    def phase_begin(self):
        self.sb_marks.append(self.sb_off)

    def phase_end(self):
        self.barrier()
        self.sb_off = self.sb_marks.pop()
        for b in self.phase_bufs:
            for qn, sem in b.sems.items():
                self.free_sems[qn].append(sem)
            b.sems = {}
        self.phase_bufs = []

    def _dma_sem(self, buf, queue):
        sem = buf.sems.get(queue)
        if sem is None:
            if self.free_sems[queue]:
                sem = self.free_sems[queue].pop()
            else:
                self.nsem += 1
                sem = self.nc.alloc_semaphore("dma_%s%d" % (queue, self.nsem))
                self.semval[id(sem)] = 0
            buf.sems[queue] = sem
        return sem

    def _add_dep(self, op, dep, war=False):
        if dep is None or dep is op:
            return
        if dep.epoch < self.nbar:
            return
        if not dep.is_dma and dep.eng == op.eng and not op.is_dma:
            if op.eng == "pe":
                return
            if war:
                return
        if not dep.is_dma:
            dep.signal = True
        op.deps.append(dep)

    def op(self, eng, fn, reads=(), writes=(), dma=None):
        o = Op(eng, fn, dma is not None)
        o.epoch = self.nbar
        for b in reads:
            self._add_dep(o, b.lastw)
        for b in writes:
            self._add_dep(o, b.lastw)
            for r in b.readers.values():
                self._add_dep(o, r, war=True)
        if dma is not None:
            sem = self._dma_sem(dma, eng)
            self.semval[id(sem)] += 16
            o.sem = sem
            o.semval = self.semval[id(sem)]
            self.outstanding[id(sem)] = (sem, o.semval)
        for b in reads:
            key = ("dma", id(o.sem)) if o.is_dma else eng
            b.readers[key] = o
        for b in writes:
            b.lastw = o
            b.readers = {}
        self.q[eng].append(o)
        return o

    def barrier(self):
        self.nbar += 1
        waits = list(self.outstanding.values())
        self.outstanding = {}
        lasts = []
        for e in ("pool", "act", "dve", "pe"):
            for o in reversed(self.q[e]):
                if o.kind == "op" and not o.is_dma:
                    o.signal = True
                    lasts.append(o)
                    break
        for e in ENGS:
            o = Op(e, None, False)
            o.kind = "bar"
            o.bar = (self.nbar, waits if e == "pool" else None, lasts if e == "pool" else None)
            self.q[e].append(o)

    def emit(self):
        nc = self.nc
        for e in ("pool", "act", "dve", "pe"):
            n = 0
            for o in self.q[e]:
                if o.kind == "bar":
                    pass
                elif not o.is_dma and o.signal:
                    n += 1
                    o.sigval = n
        engsem = self.engsem
        barsem = self.barsem
        q = self.q

        def run(eng_name, eng):
            seen = {}

            def wait(sem, val):
                k = id(sem)
                if seen.get(k, 0) >= val:
                    return
                seen[k] = val
                eng.wait_ge(sem, val)

            for o in q[eng_name]:
                if o.kind == "bar":
                    k, waits, lasts = o.bar
                    if eng_name == "pool":
                        for sem, val in waits:
                            wait(sem, val)
                        for d in lasts:
                            wait(engsem[d.eng], d.sigval)
                        eng.nop().then_inc(barsem, 1)
                    wait(barsem, k)
                    continue
                for d in o.deps:
                    if d.is_dma:
                        wait(d.sem, d.semval)
                    else:
                        wait(engsem[d.eng], d.sigval)
                ins = o.fn(eng)
                if o.is_dma:
                    ins.then_inc(o.sem, 16)
                elif o.signal:
                    ins.then_inc(engsem[o.eng], 1)

        with nc.Block() as block:
            @block.sync
            def _(e):
                run("sp", e)

            @block.gpsimd
            def _(e):
                run("pool", e)

            @block.scalar
            def _(e):
                run("act", e)

            @block.vector
            def _(e):
                run("dve", e)

            @block.tensor
            def _(e):
                run("pe", e)


def _vec_layout():
    off = {}
    n = 0

    def add(name, cnt):
        nonlocal n
        off[name] = n
        n += cnt

    add("ada_b", DEPTH * 48)
    add("nmix", DEPTH * 8)
    add("nmlp", DEPTH * 8)
    add("fin", 8)
    add("qg", 1)
    add("kg", 1)
    add("conv_w", 4 * 10)
    add("conv_b", 10)
    add("ga_b", 2 * 10)
    add("gx_b", 2 * 10)
    add("lam", 2 * 10)
    off["_n"] = n
    return off


VOFF = _vec_layout()


def seq_tiles(with_ctx=True):
    t = []
    if with_ctx:
        t.append((0, TC, True))
    for i in range(TL // 512):
        t.append((TC + 512 * i, 512, False))
    return t


def build_program(steps, debug_x=False):
    nc = bass.Bass("TRN2", target_bir_lowering=False)
    P = Prog(nc)

    def din(name, shape, dt=F32):
        return nc.dram_tensor(name, list(shape), dt, kind="ExternalInput").ap()

    def dscr(name, shape, dt=F32):
        if debug_x and name in ("FT", "QT", "KT", "VV"):
            return nc.dram_tensor("dbg_" + name, list(shape), dt, kind="ExternalOutput").ap()
        return nc.dram_tensor(name, list(shape), dt, kind="Internal").ap()

    x0 = din("x0", [NSEQ, D, TT])
    cvec = din("cvec", [128, 8, 4])
    vecs = din("vecs", [128, VOFF["_n"]])
    ada_w = din("ada_w", [DEPTH, D, 6 * D])
    mlp_w1 = din("mlp_w1", [DEPTH, D, DFF])
    mlp_w2 = din("mlp_w2", [DEPTH, DFF, D])
    fnet_w_in = din("fnet_w_in", [2, D, D])
    fnet_w_out = din("fnet_w_out", [2, D, D])
    attn_w_qkv = din("attn_w_qkv", [D, 1536])
    attn_w_o = din("attn_w_o", [D, D])
    lru_w_in = din("lru_w_in", [D, 2 * DRNN])
    lru_ga_w = din("lru_ga_w", [2, 10, 128, 128])
    lru_gx_w = din("lru_gx_w", [2, 10, 128, 128])
    lru_w_out = din("lru_w_out", [DRNN, D])
    dft_lat = din("dft_lat", [8, 32, 128, 2, 512], BF16)
    dft_ctx = din("dft_ctx", [1, 2, 128, 2, 256], BF16)
    cdft = din("cdft", [128, 256], BF16)
    rope_cos = din("rope_cos", [128, TL])
    rope_sin = din("rope_sin", [128, TL])
    pswap = din("pswap", [128, 128])

    out = nc.dram_tensor("out", [NSEQ, D, TL], F32, kind="ExternalOutput").ap()
    if debug_x:
        X = nc.dram_tensor("xdbg", [NSEQ, D, TT], F32, kind="ExternalOutput").ap()
    else:
        X = dscr("X", [NSEQ, D, TT])
    AB = dscr("AB", [NSEQ, TT, 2048], BF16)
    FT = dscr("FT", [NSEQ, DRNN, TT], BF16)
    QT = dscr("QT", [NSEQ, 8 * 128, TT], BF16)
    KT = dscr("KT", [NSEQ, 2 * 128, TT], BF16)
    VV = dscr("VV", [NSEQ, TT, 256], BF16)
    GG = dscr("GG", [NSEQ, DRNN, TT])
    XR = dscr("XR", [NSEQ, DRNN, TT])

    psum = P.psum
    ps_rr = [0]

    def next_ps():
        i = ps_rr[0]
        ps_rr[0] = (i + 1) % 8
        return psum[i]

    vec_t, vec_b = P.sb([128, VOFF["_n"]], F32, "vecs")
    mod_t, mod_b = P.sb([128, DEPTH, 48, 4], F32, "mod")
    gp_t, gp_b = P.sb([128, DEPTH, 2, 8, 4], F32, "gp")
    ones_t, ones_b = P.sb([128, 128], BF16, "ones")
    misc_t, misc_b = P.sb([128, 64], F32, "misc")

    def vcol(name, i=0):
        o = VOFF[name] + i
        return vec_t[:, o:o + 1]

    first_x_src = [True]

    def prologue():
        P.phase_begin()
        P.op("sp", lambda e: e.dma_start(out=vec_t[:], in_=vecs[:, :]), writes=[vec_b], dma=vec_b)
        P.op("dve", lambda e: e.memset(ones_t[:], 1.0), writes=[ones_b])
        cv_t, cv_b = P.sb([128, 8, 4], F32, "cv")
        st_t, st_b = P.sb([128, 8, 4], F32, "sT")
        P.op("sp", lambda e: e.dma_start(out=cv_t[:], in_=cvec[:, :, :]), writes=[cv_b], dma=cv_b)
        P.op("act", lambda e: e.activation(out=st_t[:], in_=cv_t[:], func=AF.Silu), reads=[cv_b], writes=[st_b])
        P.op("dve", lambda e: e.tensor_scalar(out=misc_t[:, 0:1], in0=vcol("qg"), scalar1=float(128.0 ** -0.5),
                                              scalar2=None, op0=ALU.mult), reads=[vec_b], writes=[misc_b])
        P.op("dve", lambda e: e.tensor_scalar(out=misc_t[:, 1:2], in0=vcol("kg"), scalar1=1.0,
                                              scalar2=None, op0=ALU.mult), reads=[vec_b, misc_b], writes=[misc_b])
        lo = VOFF["lam"]
        P.op("act", lambda e: e.activation(out=misc_t[:, 2:22], in_=vec_t[:, lo:lo + 20], func=AF.Exp, scale=-1.0),
             reads=[vec_b, misc_b], writes=[misc_b])
        P.op("act", lambda e: e.activation(out=misc_t[:, 2:22], in_=misc_t[:, 2:22], func=AF.Ln, bias=1.0),
             reads=[misc_b], writes=[misc_b])
        P.op("dve", lambda e: e.tensor_scalar(out=misc_t[:, 2:22], in0=misc_t[:, 2:22], scalar1=-8.0,
                                              scalar2=None, op0=ALU.mult), reads=[misc_b], writes=[misc_b])
        aw = [P.sb([128, 8, 512], F32, "aw") for _ in range(2)]
        it = 0
        for l in range(DEPTH):
            for fg in range(12):
                awt, awb = aw[it % 2]
                it += 1
                src = ada_w[l, :, fg * 512:(fg + 1) * 512].rearrange("(c p) f -> p c f", p=128)
                P.op("sp", lambda e, awt=awt, src=src: e.dma_start(out=awt[:], in_=src), writes=[awb], dma=awb)
                for fc in range(4):
                    ch = fg * 4 + fc
                    pt, pb = next_ps()
                    for kc in range(8):
                        P.op("pe", lambda e, pt=pt, awt=awt, fc=fc, kc=kc: e.matmul(
                            pt[:, 0:4], lhsT=awt[:, kc, fc * 128:(fc + 1) * 128], rhs=st_t[:, kc, :],
                            start=(kc == 0), stop=(kc == 7)), reads=[awb, st_b], writes=[pb])
                    P.op("act", lambda e, pt=pt, l=l, ch=ch: e.activation(
                        out=mod_t[:, l, ch, :], in_=pt[:, 0:4], func=AF.Identity,
                        bias=vcol("ada_b", l * 48 + ch)), reads=[pb, vec_b], writes=[mod_b])
        for l in range(DEPTH):
            for which, (nname, j) in enumerate((("nmix", 1), ("nmlp", 4))):
                for c in range(8):
                    P.op("dve", lambda e, l=l, which=which, nname=nname, j=j, c=c: e.tensor_scalar(
                        out=gp_t[:, l, which, c, :], in0=mod_t[:, l, j * 8 + c, :], scalar1=1.0,
                        scalar2=vcol(nname, l * 8 + c), op0=ALU.add, op1=ALU.mult),
                        reads=[mod_b, vec_b], writes=[gp_b])
        P.phase_end()

    def x_src():
        return x0 if first_x_src[0] else X

    def load_w(w_t, w_b, src_ap, kc_n, ncols, piece=1024):
        view = src_ap.rearrange("(c p) f -> p c f", p=128)
        nsp = (ncols + 2047) // 2048
        wcol = ncols // nsp
        bufs = []
        for i in range(nsp):
            b = w_b if i == 0 else P.newbuf("wpart")
            bufs.append(b)
            c0 = i * wcol
            P.op("pool", lambda e, c0=c0: e.dma_start(out=w_t[:, :, c0:c0 + wcol], in_=view[:, :, c0:c0 + wcol]),
                 writes=[b], dma=b)
        return lambda col: bufs[col // wcol]

    def norm_mod(xt, xb, N, l, which, row, sq, sqb, rt, rtb, rs, rsb, h, hb, shj):
        P.op("act", lambda e: e.activation(out=sq[:, :, :N], in_=xt[:, :, :N], func=AF.Square),
             reads=[xb], writes=[sqb])
        pt, pb = next_ps()
        for c in range(8):
            P.op("pe", lambda e, c=c: e.matmul(pt[:, :N], lhsT=ones_t[:], rhs=sq[:, c, :N],
                                               start=(c == 0), stop=(c == 7)), reads=[sqb, ones_b], writes=[pb])
        P.op("act", lambda e: e.activation(out=rt[:, :N], in_=pt[:, :N], func=AF.Sqrt, scale=1.0 / D, bias=EPS),
             reads=[pb], writes=[rtb])
        P.op("dve", lambda e: e.reciprocal(out=rs[:, :N], in_=rt[:, :N]), reads=[rtb], writes=[rsb])
        for c in range(8):
            P.op("dve", lambda e, c=c: e.scalar_tensor_tensor(
                out=rt[:, :N], in0=xt[:, c, :N], scalar=gp_t[:, l, which, c, row:row + 1], in1=rs[:, :N],
                op0=ALU.mult, op1=ALU.mult), reads=[xb, rsb, gp_b], writes=[rtb])
            P.op("act", lambda e, c=c: e.activation(
                out=h[:, c, :N], in_=rt[:, :N], func=AF.Identity,
                bias=mod_t[:, l, shj * 8 + c, row:row + 1]), reads=[rtb, mod_b], writes=[hb])

    def pipelined_tiles(l, with_ctx, body):
        xts = [P.sb([128, 8, 512], F32, "xt") for _ in range(2)]
        hs = [P.sb([128, 8, 512], BF16, "h") for _ in range(2)]
        rt, rtb = P.sb([128, 512], F32, "rt")
        rs, rsb = P.sb([128, 512], F32, "rs")
        tiles = [(s, t0, N, is_ctx) for s in range(NSEQ) for (t0, N, is_ctx) in seq_tiles(with_ctx)]
        nt = len(tiles)

        def load(i):
            s, t0, N, is_ctx = tiles[i]
            xt, xb = xts[i % 2]
            xsrc = x_src()[s, :, t0:t0 + N].rearrange("(c p) t -> p c t", p=128)
            P.op("sp", lambda e: e.dma_start(out=xt[:, :, :N], in_=xsrc), writes=[xb], dma=xb)

        def norm(i):
            s, t0, N, is_ctx = tiles[i]
            xt, xb = xts[i % 2]
            h_t, h_b = hs[i % 2]
            norm_mod(xt, xb, N, l, 0, 2 if is_ctx else s, h_t, h_b, rt, rtb, rs, rsb, h_t, h_b, 0)

        load(0)
        if nt > 1:
            load(1)
        norm(0)
        for i in range(nt):
            if i + 1 < nt:
                norm(i + 1)
            if i + 2 < nt:
                load(i + 2)
            s, t0, N, is_ctx = tiles[i]
            h_t, h_b = hs[i % 2]
            body(i, s, t0, N, is_ctx, 2 if is_ctx else s, h_t, h_b)

    def proj_residual(l, w_src, kc_n, src_dram, with_ctx):
        P.phase_begin()
        w_t, w_b = P.sb([128, kc_n, D], BF16, "wout")
        wl = load_w(w_t, w_b, w_src, kc_n, D)
        xts = [P.sb([128, 8, 512], F32, "xt") for _ in range(2)]
        srs = [P.sb([128, kc_n, 512], BF16, "src") for _ in range(2)]
        it = 0
        for s in range(NSEQ):
            for (t0, N, is_ctx) in seq_tiles(with_ctx):
                row = 2 if is_ctx else s
                xt, xb = xts[it % 2]
                sr, srb = srs[it % 2]
                it += 1
                xsrc = x_src()[s, :, t0:t0 + N].rearrange("(c p) t -> p c t", p=128)
                P.op("sp", lambda e, xt=xt, xsrc=xsrc, N=N: e.dma_start(out=xt[:, :, :N], in_=xsrc),
                     writes=[xb], dma=xb)
                ssrc = src_dram[s, 0:kc_n * 128, t0:t0 + N].rearrange("(c p) t -> p c t", p=128)
                P.op("sp", lambda e, sr=sr, ssrc=ssrc, N=N: e.dma_start(out=sr[:, :, :N], in_=ssrc),
                     writes=[srb], dma=srb)
                for m in range(8):
                    pt, pb = next_ps()
                    for kc in range(kc_n):
                        P.op("pe", lambda e, pt=pt, m=m, kc=kc, sr=sr, N=N: e.matmul(
                            pt[:, :N], lhsT=w_t[:, kc, m * 128:(m + 1) * 128], rhs=sr[:, kc, :N],
                            start=(kc == 0), stop=(kc == kc_n - 1)), reads=[wl(m * 128), srb], writes=[pb])
                    P.op("dve", lambda e, pt=pt, m=m, xt=xt, N=N, row=row: e.scalar_tensor_tensor(
                        out=xt[:, m, :N], in0=pt[:, :N], scalar=mod_t[:, l, 2 * 8 + m, row:row + 1],
                        in1=xt[:, m, :N], op0=ALU.mult, op1=ALU.add), reads=[pb, xb, mod_b], writes=[xb])
                xdst = X[s, :, t0:t0 + N].rearrange("(c p) t -> p c t", p=128)
                P.op("pool", lambda e, xt=xt, xdst=xdst, N=N: e.dma_start(out=xdst, in_=xt[:, :, :N]),
                     reads=[xb], dma=xb)
        P.phase_end()
        first_x_src[0] = False

    def mlp_phase(l, with_ctx, final):
        P.phase_begin()
        w1_t, w1_b = P.sb([128, 8, DFF], BF16, "w1")
        w2_t, w2_b = P.sb([128, 32, D], BF16, "w2")
        wl1 = load_w(w1_t, w1_b, mlp_w1[l], 8, DFF)
        wl2 = load_w(w2_t, w2_b, mlp_w2[l], 32, D)
        xts = [P.sb([128, 8, 512], F32, "xt") for _ in range(2)]
        h_t, h_b = P.sb([128, 8, 512], BF16, "h")
        sq_t, sq_b = P.sb([128, 8, 512], BF16, "sq")
        a_t, a_b = P.sb([128, 16, 512], BF16, "a")
        rt, rtb = P.sb([128, 512], F32, "rt")
        rs, rsb = P.sb([128, 512], F32, "rs")
        rr = [P.sb([128, 512], F32, "relu") for _ in range(2)]
        tiles = [(s, t0, N, is_ctx) for s in range(NSEQ) for (t0, N, is_ctx) in seq_tiles(with_ctx)]
        nt = len(tiles)
        cnt = {"r": 0}

        def tinfo(i):
            s, t0, N, is_ctx = tiles[i]
            xt, xb = xts[i % 2]
            return s, t0, N, (2 if is_ctx else s), xt, xb

        def load(i):
            s, t0, N, row, xt, xb = tinfo(i)
            xsrc = x_src()[s, :, t0:t0 + N].rearrange("(c p) t -> p c t", p=128)
            P.op("sp", lambda e: e.dma_start(out=xt[:, :, :N], in_=xsrc), writes=[xb], dma=xb)

        def rstd_chain(src_t, src_b, N):
            pt, pb = next_ps()
            for c in range(8):
                P.op("pe", lambda e, c=c: e.matmul(pt[:, :N], lhsT=ones_t[:], rhs=sq_t[:, c, :N],
                                                   start=(c == 0), stop=(c == 7)), reads=[sq_b, ones_b], writes=[pb])
            P.op("act", lambda e: e.activation(out=rt[:, :N], in_=pt[:, :N], func=AF.Sqrt, scale=1.0 / D, bias=EPS),
                 reads=[pb], writes=[rtb])
            P.op("dve", lambda e: e.reciprocal(out=rs[:, :N], in_=rt[:, :N]), reads=[rtb], writes=[rsb])

        def norm_a(i):
            s, t0, N, row, xt, xb = tinfo(i)
            P.op("act", lambda e: e.activation(out=sq_t[:, :, :N], in_=xt[:, :, :N], func=AF.Square),
                 reads=[xb], writes=[sq_b])

        def norm_b(i):
            s, t0, N, row, xt, xb = tinfo(i)
            rstd_chain(xt, xb, N)

        def norm_c(i):
            s, t0, N, row, xt, xb = tinfo(i)
            for c in range(8):
                tm_t, tm_b = rr[c % 2]
                P.op("dve", lambda e, c=c, tm_t=tm_t: e.scalar_tensor_tensor(
                    out=tm_t[:, :N], in0=xt[:, c, :N], scalar=gp_t[:, l, 1, c, row:row + 1], in1=rs[:, :N],
                    op0=ALU.mult, op1=ALU.mult), reads=[xb, rsb, gp_b], writes=[tm_b])
                P.op("act", lambda e, c=c, tm_t=tm_t: e.activation(
                    out=h_t[:, c, :N], in_=tm_t[:, :N], func=AF.Identity,
                    bias=mod_t[:, l, 3 * 8 + c, row:row + 1]), reads=[tm_b, mod_b], writes=[h_b])

        def w1(i, half):
            s, t0, N, row, xt, xb = tinfo(i)
            for fi in range(16):
                f = half * 16 + fi
                pt, pb = next_ps()
                for kc in range(8):
                    P.op("pe", lambda e, pt=pt, f=f, kc=kc: e.matmul(
                        pt[:, :N], lhsT=w1_t[:, kc, f * 128:(f + 1) * 128], rhs=h_t[:, kc, :N],
                        start=(kc == 0), stop=(kc == 7)), reads=[wl1(f * 128), h_b], writes=[pb])
                r_t, r_b = rr[cnt["r"] % 2]
                cnt["r"] += 1
                P.op("act", lambda e, pt=pt, r_t=r_t: e.activation(
                    out=r_t[:, :N], in_=pt[:, :N], func=AF.Relu), reads=[pb], writes=[r_b])
                P.op("dve", lambda e, r_t=r_t, fi=fi: e.tensor_tensor(
                    out=a_t[:, fi, :N], in0=r_t[:, :N], in1=r_t[:, :N], op=ALU.mult),
                    reads=[r_b], writes=[a_b])

        def w2(i, half):
            s, t0, N, row, xt, xb = tinfo(i)
            for m in range(8):
                pt, pb = next_ps()
                for fi in range(16):
                    f = half * 16 + fi
                    P.op("pe", lambda e, pt=pt, m=m, f=f, fi=fi: e.matmul(
                        pt[:, :N], lhsT=w2_t[:, f, m * 128:(m + 1) * 128], rhs=a_t[:, fi, :N],
                        start=(fi == 0), stop=(fi == 15)), reads=[wl2(m * 128), a_b], writes=[pb])
                P.op("dve", lambda e, pt=pt, m=m: e.scalar_tensor_tensor(
                    out=xt[:, m, :N], in0=pt[:, :N], scalar=mod_t[:, l, 5 * 8 + m, row:row + 1],
                    in1=xt[:, m, :N], op0=ALU.mult, op1=ALU.add), reads=[pb, xb, mod_b], writes=[xb])

        def fin_a(i):
            s, t0, N, row, xt, xb = tinfo(i)
            P.op("act", lambda e: e.activation(out=sq_t[:, :, :N], in_=xt[:, :, :N], func=AF.Square),
                 reads=[xb], writes=[sq_b])

        def fin_b(i):
            s, t0, N, row, xt, xb = tinfo(i)
            rstd_chain(xt, xb, N)
            for c in range(8):
                P.op("dve", lambda e, c=c: e.scalar_tensor_tensor(
                    out=xt[:, c, :N], in0=xt[:, c, :N], scalar=vcol("fin", c), in1=rs[:, :N],
                    op0=ALU.mult, op1=ALU.mult), reads=[xb, rsb, vec_b], writes=[xb])

        def store(i):
            s, t0, N, row, xt, xb = tinfo(i)
            if final:
                xdst = out[s, :, t0 - TC:t0 - TC + N].rearrange("(c p) t -> p c t", p=128)
            else:
                xdst = X[s, :, t0:t0 + N].rearrange("(c p) t -> p c t", p=128)
            P.op("pool", lambda e: e.dma_start(out=xdst, in_=xt[:, :, :N]), reads=[xb], dma=xb)

        load(0)
        if nt > 1:
            load(1)
        norm_a(0)
        norm_b(0)
        norm_c(0)
        for i in range(nt):
            w1(i, 0)
            if final and i > 0:
                fin_b(i - 1)
                store(i - 1)
                if i + 1 < nt:
                    load(i + 1)
            w2(i, 0)
            if i + 1 < nt:
                norm_a(i + 1)
            w1(i, 1)
            if i + 1 < nt:
                norm_b(i + 1)
                norm_c(i + 1)
            w2(i, 1)
            if final:
                fin_a(i)
                if i == nt - 1:
                    fin_b(i)
                    store(i)
            else:
                store(i)
                if i + 2 < nt:
                    load(i + 2)
        P.phase_end()
        first_x_src[0] = False

    def fnet_phase(l, j, with_ctx):
        P.phase_begin()
        w_t, w_b = P.sb([128, 8, D], BF16, "win")
        wl = load_w(w_t, w_b, fnet_w_in[j], 8, D)
        cd_t, cd_b = P.sb([128, 256], BF16, "cdft")
        P.op("sp", lambda e: e.dma_start(out=cd_t[:], in_=cdft[:, :]), writes=[cd_b], dma=cd_b)
        us = [P.sb([128, 8, 512], BF16, "u") for _ in range(2)]
        abs_ = [P.sb([128, 4, 2048], BF16, "ab") for _ in range(2)]
        cnt = {"ev": 0}

        def evac(pt, pb, out_ap, ob, width):
            cnt["ev"] += 1
            if cnt["ev"] % 2:
                P.op("act", lambda e: e.activation(out=out_ap, in_=pt[:, :width], func=AF.Copy),
                     reads=[pb], writes=[ob])
            else:
                P.op("dve", lambda e: e.tensor_copy(out=out_ap, in_=pt[:, :width]), reads=[pb], writes=[ob])

        def fn1_body(i, s, t0, N, is_ctx, row, h_t, h_b):
            u_t, u_b = us[i % 2]
            ab_t, ab_b = abs_[i % 2]
            for g in range(8):
                pt, pb = next_ps()
                for kc in range(8):
                    P.op("pe", lambda e, pt=pt, g=g, kc=kc: e.matmul(
                        pt[:, :N], lhsT=w_t[:, kc, g * 128:(g + 1) * 128], rhs=h_t[:, kc, :N],
                        start=(kc == 0), stop=(kc == 7)), reads=[wl(g * 128), h_b], writes=[pb])
                evac(pt, pb, u_t[:, g, :N], u_b, N)
            nqc = N // 128
            for qc in range(nqc):
                for gp2 in range(4):
                    pt, pb = next_ps()
                    for gg in range(2):
                        g = gp2 * 2 + gg
                        P.op("pe", lambda e, pt=pt, g=g, gg=gg, qc=qc: e.matmul(
                            pt[:, gg * 256:(gg + 1) * 256], lhsT=u_t[:, g, qc * 128:(qc + 1) * 128],
                            rhs=cd_t[:], start=True, stop=True), reads=[u_b, cd_b], writes=[pb])
                    evac(pt, pb, ab_t[:, qc, gp2 * 512:(gp2 + 1) * 512], ab_b, 512)
            dst = AB[s, t0:t0 + N, :].rearrange("(q p) f -> p q f", p=128)
            P.op("pool", lambda e: e.dma_start(out=dst, in_=ab_t[:, :nqc, :]), reads=[ab_b], dma=ab_b)

        pipelined_tiles(l, with_ctx, fn1_body)
        P.phase_end()

        P.phase_begin()
        abr = [P.sb([128, 8, 2048], BF16, "abr") for _ in range(4)]
        dfr = [P.sb([128, 2, 512], BF16, "dft") for _ in range(6)]
        fts = [P.sb([128, 8, 512], BF16, "ft") for _ in range(2)]
        ndf = 0
        nft = 0
        nev = 0
        for s in range(NSEQ):
            segs = [(TC, TL, dft_lat, 512, 8)]
            if with_ctx:
                segs.append((0, TC, dft_ctx, 256, 1))
            for (t0, T, table, KN, nkt) in segs:
                ntc = T // 128
                nq = (ntc + 7) // 8
                for qi in range(nq):
                    c0 = qi * 8
                    cn = min(8, ntc - c0)
                    a_t, a_b = abr[qi]
                    src = AB[s, t0 + c0 * 128:t0 + (c0 + cn) * 128, :].rearrange("(c p) f -> p c f", p=128)
                    P.op("sp", lambda e, a_t=a_t, src=src, cn=cn: e.dma_start(out=a_t[:, :cn, :], in_=src),
                         writes=[a_b], dma=a_b)
                for kt in range(nkt):
                    banks = psum
                    for tc in range(ntc):
                        d_t, d_b = dfr[ndf % 6]
                        ndf += 1
                        P.op("sp", lambda e, d_t=d_t, table=table, kt=kt, tc=tc, KN=KN: e.dma_start(
                            out=d_t[:, :, :KN], in_=table[kt, tc, :, :, :]), writes=[d_b], dma=d_b)
                        a_t, a_b = abr[tc // 8]
                        ci = tc % 8
                        for g in range(8):
                            pt, pb = banks[g]
                            for ab in range(2):
                                P.op("pe", lambda e, pt=pt, a_t=a_t, ci=ci, g=g, ab=ab, d_t=d_t, tc=tc, KN=KN,
                                     ntc=ntc: e.matmul(
                                    pt[:, :KN], lhsT=a_t[:, ci, g * 256 + ab * 128:g * 256 + ab * 128 + 128],
                                    rhs=d_t[:, ab, :KN], start=(tc == 0 and ab == 0),
                                    stop=(tc == ntc - 1 and ab == 1)), reads=[a_b, d_b], writes=[pb])
                    f_t, f_b = fts[nft % 2]
                    nft += 1
                    for g in range(8):
                        pt, pb = banks[g]
                        nev += 1
                        if nev % 2:
                            P.op("act", lambda e, pt=pt, f_t=f_t, g=g, KN=KN: e.activation(
                                out=f_t[:, g, :KN], in_=pt[:, :KN], func=AF.Copy), reads=[pb], writes=[f_b])
                        else:
                            P.op("dve", lambda e, pt=pt, f_t=f_t, g=g, KN=KN: e.tensor_copy(
                                out=f_t[:, g, :KN], in_=pt[:, :KN]), reads=[pb], writes=[f_b])
                    k0 = t0 + kt * KN
                    dst = FT[s, 0:D, k0:k0 + KN].rearrange("(c p) t -> p c t", p=128)
                    P.op("pool", lambda e, f_t=f_t, dst=dst, KN=KN: e.dma_start(out=dst, in_=f_t[:, :, :KN]),
                         reads=[f_b], dma=f_b)
        P.phase_end()
        proj_residual(l, fnet_w_out[j], 8, FT, with_ctx)

    def attn_phase(l):
        P.phase_begin()
        w_t, w_b = P.sb([128, 8, 1536], BF16, "wqkv")
        wl = load_w(w_t, w_b, attn_w_qkv, 8, 1536)
        cos_t, cos_b = P.sb([128, TL], F32, "cos")
        sin_t, sin_b = P.sb([128, TL], F32, "sin")
        psw_t, psw_b = P.sb([128, 128], F32, "pswap")
        P.op("sp", lambda e: e.dma_start(out=cos_t[:], in_=rope_cos[:, :]), writes=[cos_b], dma=cos_b)
        P.op("sp", lambda e: e.dma_start(out=sin_t[:], in_=rope_sin[:, :]), writes=[sin_b], dma=sin_b)
        P.op("sp", lambda e: e.dma_start(out=psw_t[:], in_=pswap[:, :]), writes=[psw_b], dma=psw_b)
        qks = [P.sb([128, 10, 512], BF16, "qk") for _ in range(2)]
        qk_kst = {id(b_): P.newbuf("qk_kstore") for (_, b_) in qks}
        qk_cb = {id(b_): [P.newbuf("qkc%d" % k) for k in range(10)] for (_, b_) in qks}
        sq_t, sq_b = P.sb([128, 10, 512], BF16, "sqh")
        qf_t, qf_b = P.sb([128, 10, 512], F32, "qf")
        rq_t, rq_b = P.sb([128, 10, 512], F32, "rq")
        t1s = [P.sb([128, 512], F32, "t1") for _ in range(2)]
        t2s = [P.sb([128, 512], F32, "t2") for _ in range(2)]
        vss = [P.sb([128, 4, 256], BF16, "vs") for _ in range(2)]
        sqb = [P.newbuf("sq%d" % k) for k in range(10)]
        qfb = [P.newbuf("qf%d" % k) for k in range(10)]
        rqb = [P.newbuf("rq%d" % k) for k in range(10)]

        def at1_body(i, s, t0, N, is_ctx, row, h_t, h_b):
            qk_t, qk_b = qks[i % 2]
            qkc = qk_cb[id(qk_b)]
            v_t, v_b = vss[i % 2]
            for hc in range(10):
                pq, pqb = next_ps()
                for kc in range(8):
                    P.op("pe", lambda e, pq=pq, hc=hc, kc=kc: e.matmul(
                        pq[:, :N], lhsT=w_t[:, kc, hc * 128:(hc + 1) * 128], rhs=h_t[:, kc, :N],
                        start=(kc == 0), stop=(kc == 7)), reads=[wl(hc * 128), h_b], writes=[pqb])
                P.op("act", lambda e, pq=pq, hc=hc: e.activation(
                    out=sq_t[:, hc, :N], in_=pq[:, :N], func=AF.Square), reads=[pqb], writes=[sqb[hc]])
                P.op("act", lambda e, pq=pq, hc=hc: e.activation(
                    out=qf_t[:, hc, :N], in_=pq[:, :N], func=AF.Copy), reads=[pqb], writes=[qfb[hc]])
            nqc = N // 128
            for qc in range(nqc):
                pv, pvb = next_ps()
                for kc in range(8):
                    P.op("pe", lambda e, pv=pv, kc=kc, qc=qc: e.matmul(
                        pv[:, 0:256], lhsT=h_t[:, kc, qc * 128:(qc + 1) * 128], rhs=w_t[:, kc, 1280:1536],
                        start=(kc == 0), stop=(kc == 7)), reads=[wl(1280), h_b], writes=[pvb])
                P.op("act", lambda e, pv=pv, qc=qc: e.activation(out=v_t[:, qc, :], in_=pv[:, 0:256], func=AF.Copy),
                     reads=[pvb], writes=[v_b])
            vdst = VV[s, t0:t0 + N, :].rearrange("(q p) d -> p q d", p=128)
            P.op("pool", lambda e: e.dma_start(out=vdst, in_=v_t[:, :nqc, :]), reads=[v_b], dma=v_b)
            for hc in range(10):
                p2, p2b = next_ps()
                P.op("pe", lambda e, p2=p2, hc=hc: e.matmul(
                    p2[:, :N], lhsT=ones_t[:], rhs=sq_t[:, hc, :N], start=True, stop=True),
                    reads=[sqb[hc], ones_b], writes=[p2b])
                P.op("act", lambda e, p2=p2, hc=hc: e.activation(
                    out=rq_t[:, hc, :N], in_=p2[:, :N], func=AF.Sqrt, scale=1.0 / 128, bias=EPS),
                    reads=[p2b], writes=[rqb[hc]])
            for hc in range(10):
                gcol = misc_t[:, 0:1] if hc < 8 else misc_t[:, 1:2]
                P.op("dve", lambda e, hc=hc: e.reciprocal(out=rq_t[:, hc, :N], in_=rq_t[:, hc, :N]),
                     reads=[rqb[hc]], writes=[rqb[hc]])
                if is_ctx:
                    P.op("dve", lambda e, hc=hc, gcol=gcol: e.scalar_tensor_tensor(
                        out=qk_t[:, hc, :N], in0=qf_t[:, hc, :N], scalar=gcol, in1=rq_t[:, hc, :N],
                        op0=ALU.mult, op1=ALU.mult), reads=[qfb[hc], rqb[hc], misc_b], writes=[qkc[hc]])
                else:
                    P.op("dve", lambda e, hc=hc, gcol=gcol: e.scalar_tensor_tensor(
                        out=qf_t[:, hc, :N], in0=qf_t[:, hc, :N], scalar=gcol, in1=rq_t[:, hc, :N],
                        op0=ALU.mult, op1=ALU.mult), reads=[qfb[hc], rqb[hc], misc_b], writes=[qfb[hc]])
            if not is_ctx:
                l0 = t0 - TC
                for hc in range(10):
                    t1_t, t1_b = t1s[hc % 2]
                    t2_t, t2_b = t2s[hc % 2]
                    p3, p3b = next_ps()
                    P.op("pe", lambda e, p3=p3, hc=hc: e.matmul(
                        p3[:, :N], lhsT=psw_t[:], rhs=qf_t[:, hc, :N], start=True, stop=True),
                        reads=[qfb[hc], psw_b], writes=[p3b])
                    e1, e2 = ("pool", "dve") if hc % 2 == 0 else ("dve", "pool")
                    P.op(e1, lambda e, t1_t=t1_t, hc=hc: e.tensor_tensor(
                        out=t1_t[:, :N], in0=qf_t[:, hc, :N], in1=cos_t[:, l0:l0 + N], op=ALU.mult),
                        reads=[qfb[hc], cos_b], writes=[t1_b])
                    P.op("dve", lambda e, t2_t=t2_t, p3=p3: e.tensor_tensor(
                        out=t2_t[:, :N], in0=p3[:, :N], in1=sin_t[:, l0:l0 + N], op=ALU.mult),
                        reads=[p3b, sin_b], writes=[t2_b])
                    P.op(e2, lambda e, hc=hc, t1_t=t1_t, t2_t=t2_t: e.tensor_tensor(
                        out=qk_t[:, hc, :N], in0=t1_t[:, :N], in1=t2_t[:, :N], op=ALU.add),
                        reads=[t1_b, t2_b], writes=[qkc[hc]])
            qdst = QT[s, :, t0:t0 + N].rearrange("(c p) t -> p c t", p=128)
            P.op("pool", lambda e: e.dma_start(out=qdst, in_=qk_t[:, 0:8, :N]), reads=qkc[0:8], dma=qk_b)
            kdst = KT[s, :, t0:t0 + N].rearrange("(c p) t -> p c t", p=128)
            P.op("pool", lambda e: e.dma_start(out=kdst, in_=qk_t[:, 8:10, :N]),
                 reads=qkc[8:10], dma=qk_kst[id(qk_b)])

        pipelined_tiles(l, True, at1_body)
        P.phase_end()

        P.phase_begin()
        kts = [P.sb([128, 2, TT], BF16, "kt") for _ in range(1)]
        vts = [P.sb([128, 34, 256], BF16, "vt") for _ in range(1)]
        qrs = [P.sb([128, 512], BF16, "q") for _ in range(3)]
        prs = [P.sb([128, 512], BF16, "p") for _ in range(4)]
        ors = [P.sb([128, 512], BF16, "o") for _ in range(2)]
        rds = [P.sb([128, 512], F32, "rd") for _ in range(2)]
        sbanks = psum[0:4]
        obanks = [(psum[4], psum[5]), (psum[6], psum[7])]
        nq = 0
        npp = 0
        nsb = 0
        for s in range(NSEQ):
            if s > 0:
                P.barrier()
            k_t, k_b = kts[0]
            v_t, v_b = vts[0]
            ksrc = KT[s, :, :].rearrange("(c p) t -> p c t", p=128)
            P.op("sp", lambda e, k_t=k_t, ksrc=ksrc: e.dma_start(out=k_t[:], in_=ksrc), writes=[k_b], dma=k_b)
            vsrc = VV[s, :, :].rearrange("(c p) d -> p c d", p=128)
            P.op("sp", lambda e, v_t=v_t, vsrc=vsrc: e.dma_start(out=v_t[:], in_=vsrc), writes=[v_b], dma=v_b)
            for hk in range(2):
                for hq in range(4):
                    hh = hk * 4 + hq
                    for (t0, N, is_ctx) in seq_tiles(True):
                        chunks = [0, 1] if is_ctx else list(range(34))
                        q_t, q_b = qrs[nq % 3]
                        (po, pob), (pd, pdb) = obanks[nq % 2]
                        o_t, o_b = ors[nq % 2]
                        rd_t, rd_b = rds[nq % 2]
                        nq += 1
                        qsrc = QT[s, hh * 128:(hh + 1) * 128, t0:t0 + N]
                        P.op("sp", lambda e, q_t=q_t, qsrc=qsrc, N=N: e.dma_start(out=q_t[:, :N], in_=qsrc),
                             writes=[q_b], dma=q_b)
                        pend = []

                        def issue_s(kc, q_t=q_t, q_b=q_b, N=N, hk=hk):
                            nonlocal nsb
                            ps_t, ps_b = sbanks[nsb % 4]
                            nsb += 1
                            P.op("pe", lambda e: e.matmul(ps_t[:, :N], lhsT=k_t[:, hk, kc * 128:(kc + 1) * 128],
                                                          rhs=q_t[:, :N], start=True, stop=True),
                                 reads=[k_b, q_b], writes=[ps_b])
                            return (ps_t, ps_b)

                        ahead = 2
                        for i in range(min(ahead, len(chunks))):
                            pend.append(issue_s(chunks[i]))
                        for i, kc in enumerate(chunks):
                            if i + ahead < len(chunks):
                                pend.append(issue_s(chunks[i + ahead]))
                            ps_t, ps_b = pend.pop(0)
                            p_t, p_b = prs[npp % 4]
                            npp += 1
                            P.op("act", lambda e, ps_t=ps_t, p_t=p_t, N=N: e.activation(
                                out=p_t[:, :N], in_=ps_t[:, :N], func=AF.Exp), reads=[ps_b], writes=[p_b])
                            first = (i == 0)
                            last = (i == len(chunks) - 1)
                            P.op("pe", lambda e, po=po, p_t=p_t, kc=kc, hk=hk, N=N, first=first, last=last: e.matmul(
                                po[:, :N], lhsT=v_t[:, kc, hk * 128:(hk + 1) * 128], rhs=p_t[:, :N],
                                start=first, stop=last), reads=[v_b, p_b], writes=[pob])
                            P.op("pe", lambda e, pd=pd, p_t=p_t, N=N, first=first, last=last: e.matmul(
                                pd[:, :N], lhsT=ones_t[:], rhs=p_t[:, :N], start=first, stop=last),
                                reads=[ones_b, p_b], writes=[pdb])
                        P.op("dve", lambda e, pd=pd, rd_t=rd_t, N=N: e.reciprocal(out=rd_t[:, :N], in_=pd[:, :N]),
                             reads=[pdb], writes=[rd_b])
                        P.op("dve", lambda e, po=po, rd_t=rd_t, o_t=o_t, N=N: e.tensor_tensor(
                            out=o_t[:, :N], in0=po[:, :N], in1=rd_t[:, :N], op=ALU.mult),
                            reads=[pob, rd_b], writes=[o_b])
                        odst = FT[s, hh * 128:(hh + 1) * 128, t0:t0 + N]
                        P.op("pool", lambda e, o_t=o_t, odst=odst, N=N: e.dma_start(out=odst, in_=o_t[:, :N]),
                             reads=[o_b], dma=o_b)
        P.phase_end()
        proj_residual(l, attn_w_o, 8, FT, True)

    def lru_phase(l):
        P.phase_begin()
        w_t, w_b = P.sb([128, 8, 2 * DRNN], BF16, "wlin")
        wl = load_w(w_t, w_b, lru_w_in, 8, 2 * DRNN)
        gos = [P.sb([128, 10, 512], F32, "go") for _ in range(1)]
        xos = [P.sb([128, 10, 512], F32, "xo") for _ in range(1)]
        x2s = [P.sb([128, 512], F32, "x2") for _ in range(3)]
        ins_ = [P.sb([128, 512], F32, "inn") for _ in range(3)]
        goc = [P.newbuf("goc%d" % k) for k in range(10)]
        xoc = [P.newbuf("xoc%d" % k) for k in range(10)]

        def lr1_body(i, s, t0, N, is_ctx, row, h_t, h_b):
            go_t, go_b = gos[0]
            xo_t, xo_b = xos[0]

            def proj(oc):
                pt, pb = next_ps()
                for kc in range(8):
                    P.op("pe", lambda e, kc=kc: e.matmul(
                        pt[:, :N], lhsT=w_t[:, kc, oc * 128:(oc + 1) * 128], rhs=h_t[:, kc, :N],
                        start=(kc == 0), stop=(kc == 7)), reads=[wl(oc * 128), h_b], writes=[pb])
                return pt, pb

            for oc in range(10, 20):
                pt, pb = proj(oc)
                P.op("act", lambda e, pt=pt, oc=oc: e.activation(
                    out=xo_t[:, oc - 10, :N], in_=pt[:, :N], func=AF.Copy), reads=[pb], writes=[xoc[oc - 10]])
            xdst = XR[s, :, t0:t0 + N].rearrange("(c p) t -> p c t", p=128)
            P.op("pool", lambda e: e.dma_start(out=xdst, in_=xo_t[:, :, :N]), reads=xoc, dma=xo_b)

            def st_a(oc):
                pt, pb = proj(oc)
                x2_t, x2_b = x2s[oc % 3]
                in_t, in_b = ins_[oc % 3]
                P.op("act", lambda e: e.activation(out=x2_t[:, :N], in_=pt[:, :N], func=AF.Square),
                     reads=[pb], writes=[x2_b])
                P.op("dve", lambda e: e.tensor_scalar(
                    out=x2_t[:, :N], in0=x2_t[:, :N], scalar1=0.044715, scalar2=1.0, op0=ALU.mult, op1=ALU.add),
                    reads=[x2_b], writes=[x2_b])
                P.op("dve", lambda e: e.tensor_tensor(
                    out=in_t[:, :N], in0=pt[:, :N], in1=x2_t[:, :N], op=ALU.mult),
                    reads=[pb, x2_b], writes=[in_b])
                return pt, pb, in_t, in_b

            def st_b(oc, st):
                pt, pb, in_t, in_b = st
                P.op("act", lambda e: e.activation(
                    out=in_t[:, :N], in_=in_t[:, :N], func=AF.Sigmoid, scale=1.5957691216057308),
                    reads=[in_b], writes=[in_b])
                P.op("dve", lambda e: e.tensor_tensor(
                    out=go_t[:, oc, :N], in0=pt[:, :N], in1=in_t[:, :N], op=ALU.mult),
                    reads=[pb, in_b], writes=[goc[oc]])

            prev = st_a(0)
            for oc in range(10):
                nxt = st_a(oc + 1) if oc + 1 < 10 else None
                st_b(oc, prev)
                prev = nxt
            gdst = GG[s, :, t0:t0 + N].rearrange("(c p) t -> p c t", p=128)
            P.op("pool", lambda e: e.dma_start(out=gdst, in_=go_t[:, :, :N]), reads=goc, dma=go_b)

        pipelined_tiles(l, True, lr1_body)
        P.phase_end()

        P.phase_begin()
        ga_t, ga_b = P.sb([128, 20, 128], BF16, "ga")
        gx_t, gx_b = P.sb([128, 20, 128], BF16, "gx")
        P.op("pool", lambda e: e.dma_start(out=ga_t[:], in_=lru_ga_w.rearrange("d n k j -> k (d n) j")),
             writes=[ga_b], dma=ga_b)
        P.op("pool", lambda e: e.dma_start(out=gx_t[:], in_=lru_gx_w.rearrange("d n k j -> k (d n) j")),
             writes=[gx_b], dma=gx_b)
        xr_t, xr_b = P.sb([128, TT], F32, "xr")
        g_t, g_b = P.sb([128, TT], F32, "g")
        xc_t, xc_b = P.sb([128, TT], F32, "xc")
        xcb_t, xcb_b = P.sb([128, TT], BF16, "xcb")
        ra_t, ra_b = P.sb([128, TT], F32, "ra")
        iu_t, iu_b = P.sb([128, TT], F32, "iu")
        s_t, s_b = P.sb([128, TT], F32, "s")
        hf_t, hf_b = P.sb([128, TT], F32, "hf")
        hb_t, hb_b = P.sb([128, TT], F32, "hb")
        o_t, o_b = P.sb([128, TT], BF16, "o")
        segs = [(0, TC), (TC, TL)]
        for s in range(NSEQ):
            for n in range(10):
                P.op("sp", lambda e, s=s, n=n: e.dma_start(out=xr_t[:], in_=XR[s, n * 128:(n + 1) * 128, :]),
                     writes=[xr_b], dma=xr_b)
                P.op("sp", lambda e, s=s, n=n: e.dma_start(out=g_t[:], in_=GG[s, n * 128:(n + 1) * 128, :]),
                     writes=[g_b], dma=g_b)
                P.op("dve", lambda e, n=n: e.tensor_scalar(
                    out=xc_t[:], in0=xr_t[:], scalar1=vcol("conv_w", 2 * 10 + n), scalar2=vcol("conv_b", n),
                    op0=ALU.mult, op1=ALU.add), reads=[xr_b, vec_b], writes=[xc_b])
                for (o0, L) in segs:
                    for k, sh in ((0, -2), (1, -1), (3, 1)):
                        if sh < 0:
                            src_sl = (o0, o0 + L + sh)
                            dst_sl = (o0 - sh, o0 + L)
                        else:
                            src_sl = (o0 + sh, o0 + L)
                            dst_sl = (o0, o0 + L - sh)
                        eng = "pool" if k == 0 else "dve"
                        P.op("dve", lambda e, n=n, k=k, src_sl=src_sl, dst_sl=dst_sl: e.scalar_tensor_tensor(
                            out=xc_t[:, dst_sl[0]:dst_sl[1]], in0=xr_t[:, src_sl[0]:src_sl[1]],
                            scalar=vcol("conv_w", k * 10 + n), in1=xc_t[:, dst_sl[0]:dst_sl[1]],
                            op0=ALU.mult, op1=ALU.add), reads=[xr_b, xc_b, vec_b], writes=[xc_b])
                P.op("act", lambda e: e.activation(out=xcb_t[:], in_=xc_t[:], func=AF.Copy),
                     reads=[xc_b], writes=[xcb_b])
                for d in range(2):
                    for (t0, N, is_ctx) in seq_tiles(True):
                        pr, prb = next_ps()
                        P.op("pe", lambda e, pr=pr, d=d, n=n, t0=t0, N=N: e.matmul(
                            pr[:, :N], lhsT=ga_t[:, d * 10 + n, :], rhs=xcb_t[:, t0:t0 + N], start=True, stop=True),
                            reads=[ga_b, xcb_b], writes=[prb])
                        P.op("act", lambda e, pr=pr, d=d, n=n, t0=t0, N=N: e.activation(
                            out=ra_t[:, t0:t0 + N], in_=pr[:, :N], func=AF.Sigmoid,
                            bias=vcol("ga_b", d * 10 + n)), reads=[prb, vec_b], writes=[ra_b])
                        pi, pib = next_ps()
                        P.op("pe", lambda e, pi=pi, d=d, n=n, t0=t0, N=N: e.matmul(
                            pi[:, :N], lhsT=gx_t[:, d * 10 + n, :], rhs=xcb_t[:, t0:t0 + N], start=True, stop=True),
                            reads=[gx_b, xcb_b], writes=[pib])
                        P.op("act", lambda e, pi=pi, d=d, n=n, t0=t0, N=N: e.activation(
                            out=iu_t[:, t0:t0 + N], in_=pi[:, :N], func=AF.Sigmoid,
                            bias=vcol("gx_b", d * 10 + n)), reads=[pib, vec_b], writes=[iu_b])
                    P.op("act", lambda e, d=d, n=n: e.activation(
                        out=ra_t[:], in_=ra_t[:], func=AF.Exp, scale=misc_t[:, 2 + d * 10 + n:3 + d * 10 + n]),
                        reads=[ra_b, misc_b], writes=[ra_b])
                    P.op("act", lambda e: e.activation(out=s_t[:], in_=ra_t[:], func=AF.Square),
                         reads=[ra_b], writes=[s_b])
                    P.op("act", lambda e: e.activation(out=s_t[:], in_=s_t[:], func=AF.Sqrt, scale=-1.0, bias=1.0),
                         reads=[s_b], writes=[s_b])
                    P.op("pool", lambda e: e.tensor_tensor(out=iu_t[:], in0=iu_t[:], in1=xc_t[:], op=ALU.mult),
                         reads=[iu_b, xc_b], writes=[iu_b])
                    P.op("dve", lambda e: e.tensor_tensor(out=iu_t[:], in0=iu_t[:], in1=s_t[:], op=ALU.mult),
                         reads=[iu_b, s_b], writes=[iu_b])
                    if d == 0:
                        P.op("dve", lambda e: e.tensor_tensor_scan(
                            out=hf_t[:], data0=ra_t[:], data1=iu_t[:], initial=0.0, op0=ALU.mult, op1=ALU.add),
                            reads=[ra_b, iu_b], writes=[hf_b])
                    else:
                        P.op("dve", lambda e: e.tensor_tensor_scan(
                            out=hb_t[:, 0:TC][:, ::-1], data0=ra_t[:, 0:TC][:, ::-1], data1=iu_t[:, 0:TC][:, ::-1],
                            initial=0.0, op0=ALU.mult, op1=ALU.add), reads=[ra_b, iu_b], writes=[hb_b])
                        P.op("dve", lambda e: e.tensor_tensor_scan(
                            out=hb_t[:, TC:TT][:, ::-1], data0=ra_t[:, TC:TT][:, ::-1],
                            data1=iu_t[:, TC:TT][:, ::-1], initial=hb_t[:, 0:1], op0=ALU.mult, op1=ALU.add),
                            reads=[ra_b, iu_b, hb_b], writes=[hb_b])
                P.op("pool", lambda e: e.tensor_tensor(out=hf_t[:], in0=hf_t[:], in1=hb_t[:], op=ALU.add),
                     reads=[hf_b, hb_b], writes=[hf_b])
                P.op("dve", lambda e: e.tensor_tensor(out=o_t[:], in0=hf_t[:], in1=g_t[:], op=ALU.mult),
                     reads=[hf_b, g_b], writes=[o_b])
                P.op("pool", lambda e, s=s, n=n: e.dma_start(out=FT[s, n * 128:(n + 1) * 128, :], in_=o_t[:]),
                     reads=[o_b], dma=o_b)
        P.phase_end()
        proj_residual(l, lru_w_out, 10, FT, True)

    prologue()
    for (l, part) in steps:
        kind = l % 3
        j = l // 3
        last = l == DEPTH - 1
        with_ctx = not last
        if part == "mix":
            if kind == 0:
                fnet_phase(l, j, with_ctx)
            elif kind == 1:
                attn_phase(l)
            else:
                lru_phase(l)
        else:
            mlp_phase(l, with_ctx, final=last)
    P.emit()
    return nc


def _consts():
    bf = ml_dtypes.bfloat16
    c = np.arange(128, dtype=np.float64)
    ang = 2 * np.pi * np.outer(c, c) / 128.0
    cd = np.concatenate([np.cos(ang), np.sin(ang)], axis=1) / np.sqrt(128.0)

    def pos_table(T, KN):
        t = np.arange(T, dtype=np.int64)
        m = np.outer(t, t) % T
        a = 2 * np.pi * m.astype(np.float64) / T
        C = np.cos(a) / np.sqrt(T)
        S = -np.sin(a) / np.sqrt(T)
        nkt = T // KN
        ntc = T // 128
        tab = np.empty((nkt, ntc, 128, 2, KN), dtype=bf)
        Cr = C.reshape(ntc, 128, nkt, KN).transpose(2, 0, 1, 3)
        Sr = S.reshape(ntc, 128, nkt, KN).transpose(2, 0, 1, 3)
        tab[:, :, :, 0, :] = Cr.astype(bf)
        tab[:, :, :, 1, :] = Sr.astype(bf)
        return tab

    dft_lat = pos_table(TL, 512)
    dft_ctx = pos_table(TC, 256)
    n_freq = 32
    inv = (10000.0 ** (-np.arange(n_freq, dtype=np.float32) / n_freq)).astype(np.float32)
    tok = np.arange(TL)
    row = (tok // 64).astype(np.float32)
    col = (tok % 64).astype(np.float32)
    ang = np.concatenate([row[:, None] * inv, col[:, None] * inv], axis=-1).astype(np.float32)
    cs = np.cos(ang).T.astype(np.float32)
    sn = np.sin(ang).T.astype(np.float32)
    rope_cos = np.concatenate([cs, cs], axis=0)
    rope_sin = np.concatenate([-sn, sn], axis=0)
    pswap = np.zeros((128, 128), np.float32)
    for m in range(128):
        pswap[(m + 64) % 128, m] = 1.0
    return dict(cdft=cd.astype(bf), dft_lat=dft_lat, dft_ctx=dft_ctx,
                rope_cos=np.ascontiguousarray(rope_cos), rope_sin=np.ascontiguousarray(rope_sin), pswap=pswap)


_CONSTS = None


def _pcol(v):
    v = np.asarray(v, np.float32)
    return np.ascontiguousarray(v.reshape(-1, 128).T)


def make_in_maps(inp, n_cores=8, cores=None):
    global _CONSTS
    if _CONSTS is None:
        _CONSTS = _consts()
    perm = np.concatenate([np.arange(0, 128, 2), np.arange(1, 128, 2)])
    wqkv = np.asarray(inp["attn_w_qkv"][0], np.float32)
    cols = []
    for hh in range(10):
        cols.append(hh * 128 + perm)
    cols.append(np.arange(1280, 1536))
    wqkv_p = np.ascontiguousarray(wqkv[:, np.concatenate(cols)])

    vecs = np.zeros((128, VOFF["_n"]), np.float32)

    def put(name, arr2d):
        vecs[:, VOFF[name]:VOFF[name] + arr2d.shape[1]] = arr2d

    put("ada_b", np.concatenate([_pcol(inp["ada_b"][l]) for l in range(DEPTH)], axis=1))
    put("nmix", np.concatenate([_pcol(inp["norm_mix_g"][l]) for l in range(DEPTH)], axis=1))
    put("nmlp", np.concatenate([_pcol(inp["norm_mlp_g"][l]) for l in range(DEPTH)], axis=1))
    put("fin", _pcol(inp["final_norm_g"]))
    put("qg", np.asarray(inp["attn_q_norm_g"][0], np.float32)[perm][:, None])
    put("kg", np.asarray(inp["attn_k_norm_g"][0], np.float32)[perm][:, None])
    put("conv_w", np.concatenate([_pcol(inp["lru_conv_w"][0][k]) for k in range(4)], axis=1))
    put("conv_b", _pcol(inp["lru_conv_b"][0]))
    put("ga_b", np.concatenate([_pcol(inp["lru_gate_a_b"][0][d]) for d in range(2)], axis=1))
    put("gx_b", np.concatenate([_pcol(inp["lru_gate_x_b"][0][d]) for d in range(2)], axis=1))
    put("lam", np.concatenate([_pcol(inp["lru_lambda"][0][d]) for d in range(2)], axis=1))

    shared = dict(
        vecs=vecs,
        ada_w=np.ascontiguousarray(inp["ada_w"], np.float32),
        mlp_w1=np.ascontiguousarray(inp["mlp_w1"], np.float32),
        mlp_w2=np.ascontiguousarray(inp["mlp_w2"], np.float32),
        fnet_w_in=np.ascontiguousarray(inp["fnet_w_in"], np.float32),
        fnet_w_out=np.ascontiguousarray(inp["fnet_w_out"], np.float32),
        attn_w_qkv=wqkv_p,
        attn_w_o=np.ascontiguousarray(inp["attn_w_o"][0], np.float32),
        lru_w_in=np.ascontiguousarray(inp["lru_w_in"][0], np.float32),
        lru_ga_w=np.ascontiguousarray(inp["lru_gate_a_w"][0], np.float32),
        lru_gx_w=np.ascontiguousarray(inp["lru_gate_x_w"][0], np.float32),
        lru_w_out=np.ascontiguousarray(inp["lru_w_out"][0], np.float32),
        **_CONSTS,
    )
    x = np.asarray(inp["x"], np.float32)
    ctx = np.asarray(inp["ctx"], np.float32)
    c = np.asarray(inp["c"], np.float32)
    c_ctx = np.asarray(inp["c_ctx"], np.float32)
    maps = []
    for core in (range(n_cores) if cores is None else cores):
        x0 = np.empty((NSEQ, D, TT), np.float32)
        cv = np.zeros((128, 8, 4), np.float32)
        for s in range(NSEQ):
            b = core * NSEQ + s
            x0[s, :, :TC] = ctx[b].T
            x0[s, :, TC:] = x[b].T
            cv[:, :, s] = c[b].reshape(8, 128).T
        cv[:, :, 2] = c_ctx.reshape(8, 128).T
        cv[:, :, 3] = c_ctx.reshape(8, 128).T
        m = dict(shared)
        m["x0"] = x0
        m["cvec"] = cv
        maps.append(m)
    return maps


ALL_STEPS = [(l, p) for l in range(DEPTH) for p in ("mix", "mlp")]


def kernel(**inputs):
    nc = build_program(ALL_STEPS)
    maps = make_in_maps(inputs, 8)
    res = run_bass_kernel_spmd(nc, maps, core_ids=list(range(8)))
    outs = []
    for core in range(8):
        o = res.results[core]["out"]
        for s in range(NSEQ):
            outs.append(np.ascontiguousarray(np.asarray(o[s], np.float32).T))
    return np.stack(outs, axis=0).astype(np.float32)
```

```python
import numpy as np
import ml_dtypes
import concourse.bass as bass
import concourse.mybir as mybir
from concourse.bass_utils import run_bass_kernel_spmd

F32 = mybir.dt.float32
BF16 = mybir.dt.bfloat16
AF = mybir.ActivationFunctionType
ALU = mybir.AluOpType

D = 1024
NSEQ = 2
TC = 256
TL = 4096
TT = TC + TL
DFF = 4096
DRNN = 1280
DEPTH = 4
EPS = 1e-6
SB_BASE = 16640
SB_LIMIT = 229376

ENGS = ("sp", "pool", "act", "dve", "pe")


class Buf:
    __slots__ = ("name", "lastw", "readers", "sems")

    def __init__(self, name):
        self.name = name
        self.lastw = None
        self.readers = {}
        self.sems = {}


class Op:
    __slots__ = ("eng", "fn", "deps", "is_dma", "signal", "sigval", "sem", "semval", "kind", "bar", "epoch")

    def __init__(self, eng, fn, is_dma):
        self.eng = eng
        self.fn = fn
        self.deps = []
        self.is_dma = is_dma
        self.signal = False
        self.sigval = 0
        self.sem = None
        self.semval = 0
        self.kind = "op"
        self.bar = None
        self.epoch = 0


class Prog:
    def __init__(self, nc):
        self.nc = nc
        self.q = {e: [] for e in ENGS}
        self.sb_off = SB_BASE
        self.sb_marks = []
        self.uid = 0
        self.free_sems = {"sp": [], "act": [], "pool": []}
        self.semval = {}
        self.phase_bufs = []
        self.outstanding = {}
        self.nbar = 0
        self.nsem = 0
        self.engsem = {e: nc.alloc_semaphore("eng_" + e) for e in ("pool", "act", "dve", "pe")}
        self.barsem = nc.alloc_semaphore("barrier")
        self.psum = []
        for i in range(8):
            t = nc.alloc_psum_tensor("psb%d" % i, [128, 512], F32)
            self.psum.append((t, Buf("psum%d" % i)))

    def sb(self, shape, dtype, name="t"):
        self.uid += 1
        nbytes = int(np.prod(shape[1:])) * (2 if dtype == BF16 else 4)
        nbytes = (nbytes + 63) // 64 * 64
        off = self.sb_off
        if off + nbytes > SB_LIMIT:
            raise RuntimeError("SBUF overflow allocating %s: %d + %d" % (name, off, nbytes))
        self.sb_off += nbytes
        t = self.nc.alloc_sbuf_tensor_at("%s_%d" % (name, self.uid), list(shape), dtype, offset=off)
        b = Buf(name)
        self.phase_bufs.append(b)
        return t, b

    def newbuf(self, name):
        b = Buf(name)
        self.phase_bufs.append(b)
        return b

    def phase_begin(self):
        self.sb_marks.append(self.sb_off)

    def phase_end(self):
        self.barrier()
        self.sb_off = self.sb_marks.pop()
        for b in self.phase_bufs:
            for qn, sem in b.sems.items():
                self.free_sems[qn].append(sem)
            b.sems = {}
        self.phase_bufs = []

    def _dma_sem(self, buf, queue):
        sem = buf.sems.get(queue)
        if sem is None:
            if self.free_sems[queue]:
                sem = self.free_sems[queue].pop()
            else:
                self.nsem += 1
                sem = self.nc.alloc_semaphore("dma_%s%d" % (queue, self.nsem))
                self.semval[id(sem)] = 0
            buf.sems[queue] = sem
        return sem

    def _add_dep(self, op, dep, war=False):
        if dep is None or dep is op:
            return
        if dep.epoch < self.nbar:
            return
        if not dep.is_dma and dep.eng == op.eng and not op.is_dma:
            if op.eng == "pe":
                return
            if war:
                return
        if not dep.is_dma:
            dep.signal = True
        op.deps.append(dep)

    def op(self, eng, fn, reads=(), writes=(), dma=None):
        o = Op(eng, fn, dma is not None)
        o.epoch = self.nbar
        for b in reads:
            self._add_dep(o, b.lastw)
        for b in writes:
            self._add_dep(o, b.lastw)
            for r in b.readers.values():
                self._add_dep(o, r, war=True)
        if dma is not None:
            sem = self._dma_sem(dma, eng)
            self.semval[id(sem)] += 16
            o.sem = sem
            o.semval = self.semval[id(sem)]
            self.outstanding[id(sem)] = (sem, o.semval)
        for b in reads:
            key = ("dma", id(o.sem)) if o.is_dma else eng
            b.readers[key] = o
        for b in writes:
            b.lastw = o
            b.readers = {}
        self.q[eng].append(o)
        return o

    def barrier(self):
        self.nbar += 1
        waits = list(self.outstanding.values())
        self.outstanding = {}
        lasts = []
        for e in ("pool", "act", "dve", "pe"):
            for o in reversed(self.q[e]):
                if o.kind == "op" and not o.is_dma:
                    o.signal = True
                    lasts.append(o)
                    break
        for e in ENGS:
            o = Op(e, None, False)
            o.kind = "bar"
            o.bar = (self.nbar, waits if e == "pool" else None, lasts if e == "pool" else None)
            self.q[e].append(o)

    def emit(self):
        nc = self.nc
        for e in ("pool", "act", "dve", "pe"):
            n = 0
            for o in self.q[e]:
                if o.kind == "bar":
                    pass
                elif not o.is_dma and o.signal:
                    n += 1
                    o.sigval = n
        engsem = self.engsem
        barsem = self.barsem
        q = self.q

        def run(eng_name, eng):
            seen = {}

            def wait(sem, val):
                k = id(sem)
                if seen.get(k, 0) >= val:
                    return
                seen[k] = val
                eng.wait_ge(sem, val)

            for o in q[eng_name]:
                if o.kind == "bar":
                    k, waits, lasts = o.bar
                    if eng_name == "pool":
                        for sem, val in waits:
                            wait(sem, val)
                        for d in lasts:
                            wait(engsem[d.eng], d.sigval)
                        eng.nop().then_inc(barsem, 1)
                    wait(barsem, k)
                    continue
                for d in o.deps:
                    if d.is_dma:
                        wait(d.sem, d.semval)
                    else:
                        wait(engsem[d.eng], d.sigval)
                ins = o.fn(eng)
                if o.is_dma:
                    ins.then_inc(o.sem, 16)
                elif o.signal:
                    ins.then_inc(engsem[o.eng], 1)

        with nc.Block() as block:
            @block.sync
            def _(e):
                run("sp", e)

            @block.gpsimd
            def _(e):
                run("pool", e)

            @block.scalar
            def _(e):
                run("act", e)

            @block.vector
            def _(e):
                run("dve", e)

            @block.tensor
            def _(e):
                run("pe", e)


def _vec_layout():
    off = {}
    n = 0

    def add(name, cnt):
        nonlocal n
        off[name] = n
        n += cnt

    add("ada_b", DEPTH * 48)
    add("nmix", DEPTH * 8)
    add("nmlp", DEPTH * 8)
    add("fin", 8)
    add("qg", 1)
    add("kg", 1)
    add("conv_w", 4 * 10)
    add("conv_b", 10)
    add("ga_b", 2 * 10)
    add("gx_b", 2 * 10)
    add("lam", 2 * 10)
    off["_n"] = n
    return off


VOFF = _vec_layout()


def seq_tiles(with_ctx=True):
    t = []
    if with_ctx:
        t.append((0, TC, True))
    for i in range(TL // 512):
        t.append((TC + 512 * i, 512, False))
    return t


def build_program(steps, debug_x=False):
    nc = bass.Bass("TRN2", target_bir_lowering=False)
    P = Prog(nc)

    def din(name, shape, dt=F32):
        return nc.dram_tensor(name, list(shape), dt, kind="ExternalInput").ap()

    def dscr(name, shape, dt=F32):
        if debug_x and name in ("FT", "QT", "KT", "VV"):
            return nc.dram_tensor("dbg_" + name, list(shape), dt, kind="ExternalOutput").ap()
        return nc.dram_tensor(name, list(shape), dt, kind="Internal").ap()

    x0 = din("x0", [NSEQ, D, TT])
    cvec = din("cvec", [128, 8, 4])
    vecs = din("vecs", [128, VOFF["_n"]])
    ada_w = din("ada_w", [DEPTH, D, 6 * D])
    mlp_w1 = din("mlp_w1", [DEPTH, D, DFF])
    mlp_w2 = din("mlp_w2", [DEPTH, DFF, D])
    fnet_w_in = din("fnet_w_in", [2, D, D])
    fnet_w_out = din("fnet_w_out", [2, D, D])
    attn_w_qkv = din("attn_w_qkv", [D, 1536])
    attn_w_o = din("attn_w_o", [D, D])
    lru_w_in = din("lru_w_in", [D, 2 * DRNN])
    lru_ga_w = din("lru_ga_w", [2, 10, 128, 128])
    lru_gx_w = din("lru_gx_w", [2, 10, 128, 128])
    lru_w_out = din("lru_w_out", [DRNN, D])
    dft_lat = din("dft_lat", [8, 32, 128, 2, 512], BF16)
    dft_ctx = din("dft_ctx", [1, 2, 128, 2, 256], BF16)
    cdft = din("cdft", [128, 256], BF16)
    rope_cos = din("rope_cos", [128, TL])
    rope_sin = din("rope_sin", [128, TL])
    pswap = din("pswap", [128, 128])

    out = nc.dram_tensor("out", [NSEQ, D, TL], F32, kind="ExternalOutput").ap()
    if debug_x:
        X = nc.dram_tensor("xdbg", [NSEQ, D, TT], F32, kind="ExternalOutput").ap()
    else:
        X = dscr("X", [NSEQ, D, TT])
    AB = dscr("AB", [NSEQ, TT, 2048], BF16)
    FT = dscr("FT", [NSEQ, DRNN, TT], BF16)
    QT = dscr("QT", [NSEQ, 8 * 128, TT], BF16)
    KT = dscr("KT", [NSEQ, 2 * 128, TT], BF16)
    VV = dscr("VV", [NSEQ, TT, 256], BF16)
    GG = dscr("GG", [NSEQ, DRNN, TT])
    XR = dscr("XR", [NSEQ, DRNN, TT])

    psum = P.psum
    ps_rr = [0]

    def next_ps():
        i = ps_rr[0]
        ps_rr[0] = (i + 1) % 8
        return psum[i]

    vec_t, vec_b = P.sb([128, VOFF["_n"]], F32, "vecs")
    mod_t, mod_b = P.sb([128, DEPTH, 48, 4], F32, "mod")
    gp_t, gp_b = P.sb([128, DEPTH, 2, 8, 4], F32, "gp")
    ones_t, ones_b = P.sb([128, 128], BF16, "ones")
    misc_t, misc_b = P.sb([128, 64], F32, "misc")

    def vcol(name, i=0):
        o = VOFF[name] + i
        return vec_t[:, o:o + 1]

    first_x_src = [True]

    def prologue():
        P.phase_begin()
        P.op("sp", lambda e: e.dma_start(out=vec_t[:], in_=vecs[:, :]), writes=[vec_b], dma=vec_b)
        P.op("dve", lambda e: e.memset(ones_t[:], 1.0), writes=[ones_b])
        cv_t, cv_b = P.sb([128, 8, 4], F32, "cv")
        st_t, st_b = P.sb([128, 8, 4], F32, "sT")
        P.op("sp", lambda e: e.dma_start(out=cv_t[:], in_=cvec[:, :, :]), writes=[cv_b], dma=cv_b)
        P.op("act", lambda e: e.activation(out=st_t[:], in_=cv_t[:], func=AF.Silu), reads=[cv_b], writes=[st_b])
        P.op("dve", lambda e: e.tensor_scalar(out=misc_t[:, 0:1], in0=vcol("qg"), scalar1=float(128.0 ** -0.5),
                                              scalar2=None, op0=ALU.mult), reads=[vec_b], writes=[misc_b])
        P.op("dve", lambda e: e.tensor_scalar(out=misc_t[:, 1:2], in0=vcol("kg"), scalar1=1.0,
                                              scalar2=None, op0=ALU.mult), reads=[vec_b, misc_b], writes=[misc_b])
        lo = VOFF["lam"]
        P.op("act", lambda e: e.activation(out=misc_t[:, 2:22], in_=vec_t[:, lo:lo + 20], func=AF.Exp, scale=-1.0),
             reads=[vec_b, misc_b], writes=[misc_b])
        P.op("act", lambda e: e.activation(out=misc_t[:, 2:22], in_=misc_t[:, 2:22], func=AF.Ln, bias=1.0),
             reads=[misc_b], writes=[misc_b])
        P.op("dve", lambda e: e.tensor_scalar(out=misc_t[:, 2:22], in0=misc_t[:, 2:22], scalar1=-8.0,
                                              scalar2=None, op0=ALU.mult), reads=[misc_b], writes=[misc_b])
        aw = [P.sb([128, 8, 512], F32, "aw") for _ in range(2)]
        it = 0
        for l in range(DEPTH):
            for fg in range(12):
                awt, awb = aw[it % 2]
                it += 1
                src = ada_w[l, :, fg * 512:(fg + 1) * 512].rearrange("(c p) f -> p c f", p=128)
                P.op("sp", lambda e, awt=awt, src=src: e.dma_start(out=awt[:], in_=src), writes=[awb], dma=awb)
                for fc in range(4):
                    ch = fg * 4 + fc
                    pt, pb = next_ps()
                    for kc in range(8):
                        P.op("pe", lambda e, pt=pt, awt=awt, fc=fc, kc=kc: e.matmul(
                            pt[:, 0:4], lhsT=awt[:, kc, fc * 128:(fc + 1) * 128], rhs=st_t[:, kc, :],
                            start=(kc == 0), stop=(kc == 7)), reads=[awb, st_b], writes=[pb])
                    P.op("act", lambda e, pt=pt, l=l, ch=ch: e.activation(
                        out=mod_t[:, l, ch, :], in_=pt[:, 0:4], func=AF.Identity,
                        bias=vcol("ada_b", l * 48 + ch)), reads=[pb, vec_b], writes=[mod_b])
        for l in range(DEPTH):
            for which, (nname, j) in enumerate((("nmix", 1), ("nmlp", 4))):
                for c in range(8):
                    P.op("dve", lambda e, l=l, which=which, nname=nname, j=j, c=c: e.tensor_scalar(
                        out=gp_t[:, l, which, c, :], in0=mod_t[:, l, j * 8 + c, :], scalar1=1.0,
                        scalar2=vcol(nname, l * 8 + c), op0=ALU.add, op1=ALU.mult),
                        reads=[mod_b, vec_b], writes=[gp_b])
        P.phase_end()

    def x_src():
        return x0 if first_x_src[0] else X

    def load_w(w_t, w_b, src_ap, kc_n, ncols, piece=1024):
        view = src_ap.rearrange("(c p) f -> p c f", p=128)
        nsp = (ncols + 2047) // 2048
        wcol = ncols // nsp
        bufs = []
        for i in range(nsp):
            b = w_b if i == 0 else P.newbuf("wpart")
            bufs.append(b)
            c0 = i * wcol
            P.op("pool", lambda e, c0=c0: e.dma_start(out=w_t[:, :, c0:c0 + wcol], in_=view[:, :, c0:c0 + wcol]),
                 writes=[b], dma=b)
        return lambda col: bufs[col // wcol]

    def norm_mod(xt, xb, N, l, which, row, sq, sqb, rt, rtb, rs, rsb, h, hb, shj):
        P.op("act", lambda e: e.activation(out=sq[:, :, :N], in_=xt[:, :, :N], func=AF.Square),
             reads=[xb], writes=[sqb])
        pt, pb = next_ps()
        for c in range(8):
            P.op("pe", lambda e, c=c: e.matmul(pt[:, :N], lhsT=ones_t[:], rhs=sq[:, c, :N],
                                               start=(c == 0), stop=(c == 7)), reads=[sqb, ones_b], writes=[pb])
        P.op("act", lambda e: e.activation(out=rt[:, :N], in_=pt[:, :N], func=AF.Sqrt, scale=1.0 / D, bias=EPS),
             reads=[pb], writes=[rtb])
        P.op("dve", lambda e: e.reciprocal(out=rs[:, :N], in_=rt[:, :N]), reads=[rtb], writes=[rsb])
        for c in range(8):
            P.op("dve", lambda e, c=c: e.scalar_tensor_tensor(
                out=rt[:, :N], in0=xt[:, c, :N], scalar=gp_t[:, l, which, c, row:row + 1], in1=rs[:, :N],
                op0=ALU.mult, op1=ALU.mult), reads=[xb, rsb, gp_b], writes=[rtb])
            P.op("act", lambda e, c=c: e.activation(
                out=h[:, c, :N], in_=rt[:, :N], func=AF.Identity,
                bias=mod_t[:, l, shj * 8 + c, row:row + 1]), reads=[rtb, mod_b], writes=[hb])

    def norm_mod_ip(xt, xcb, N, l, which, row, h, hb, rt, rtb, rs, rsb, shj):
        P.op("act", lambda e: e.activation(out=h[:, :, :N], in_=xt[:, :, :N], func=AF.Square),
             reads=xcb, writes=[hb])
        pt, pb = next_ps()
        for c in range(8):
            P.op("pe", lambda e, c=c: e.matmul(pt[:, :N], lhsT=ones_t[:], rhs=h[:, c, :N],
                                               start=(c == 0), stop=(c == 7)), reads=[hb, ones_b], writes=[pb])
        P.op("act", lambda e: e.activation(out=rt[:, :N], in_=pt[:, :N], func=AF.Sqrt, scale=1.0 / D, bias=EPS),
             reads=[pb], writes=[rtb])
        P.op("dve", lambda e: e.reciprocal(out=rs[:, :N], in_=rt[:, :N]), reads=[rtb], writes=[rsb])
        for c in range(8):
            P.op("dve", lambda e, c=c: e.tensor_tensor(
                out=xt[:, c, :N], in0=xt[:, c, :N], in1=rs[:, :N], op=ALU.mult),
                reads=[xcb[c], rsb], writes=[xcb[c]])
            P.op("act", lambda e, c=c: e.activation(
                out=h[:, c, :N], in_=xt[:, c, :N], func=AF.Identity,
                scale=gp_t[:, l, which, c, row:row + 1], bias=mod_t[:, l, shj * 8 + c, row:row + 1]),
                reads=[xcb[c], gp_b, mod_b], writes=[hb])

    def pipelined_tiles(l, with_ctx, body):
        xts = [P.sb([128, 8, 512], F32, "xt") for _ in range(2)]
        xcbs = [[P.newbuf("xtc%d" % c) for c in range(8)] for _ in range(2)]
        hs = [P.sb([128, 8, 512], BF16, "h") for _ in range(2)]
        rt, rtb = P.sb([128, 512], F32, "rt")
        rs, rsb = P.sb([128, 512], F32, "rs")
        tiles = [(s, t0, N, is_ctx) for s in range(NSEQ) for (t0, N, is_ctx) in seq_tiles(with_ctx)]
        nt = len(tiles)

        def load(i):
            s, t0, N, is_ctx = tiles[i]
            xt, xb = xts[i % 2]
            xsrc = x_src()[s, :, t0:t0 + N].rearrange("(c p) t -> p c t", p=128)
            P.op("sp", lambda e: e.dma_start(out=xt[:, :, :N], in_=xsrc), writes=xcbs[i % 2], dma=xb)

        def norm(i):
            s, t0, N, is_ctx = tiles[i]
            xt, xb = xts[i % 2]
            h_t, h_b = hs[i % 2]
            norm_mod_ip(xt, xcbs[i % 2], N, l, 0, 2 if is_ctx else s, h_t, h_b, rt, rtb, rs, rsb, 0)

        load(0)
        if nt > 1:
            load(1)
        norm(0)
        for i in range(nt):
            if i + 1 < nt:
                norm(i + 1)
            if i + 2 < nt:
                load(i + 2)
            s, t0, N, is_ctx = tiles[i]
            h_t, h_b = hs[i % 2]
            body(i, s, t0, N, is_ctx, 2 if is_ctx else s, h_t, h_b)

    def proj_residual(l, w_src, kc_n, src_dram, with_ctx):
        P.phase_begin()
        w_t, w_b = P.sb([128, kc_n, D], BF16, "wout")
        wl = load_w(w_t, w_b, w_src, kc_n, D)
        xts = [P.sb([128, 8, 512], F32, "xt") for _ in range(2)]
        srs = [P.sb([128, kc_n, 512], BF16, "src") for _ in range(2)]
        it = 0
        for s in range(NSEQ):
            for (t0, N, is_ctx) in seq_tiles(with_ctx):
                row = 2 if is_ctx else s
                xt, xb = xts[it % 2]
                sr, srb = srs[it % 2]
                it += 1
                xsrc = x_src()[s, :, t0:t0 + N].rearrange("(c p) t -> p c t", p=128)
                P.op("sp", lambda e, xt=xt, xsrc=xsrc, N=N: e.dma_start(out=xt[:, :, :N], in_=xsrc),
                     writes=[xb], dma=xb)
                ssrc = src_dram[s, 0:kc_n * 128, t0:t0 + N].rearrange("(c p) t -> p c t", p=128)
                P.op("sp", lambda e, sr=sr, ssrc=ssrc, N=N: e.dma_start(out=sr[:, :, :N], in_=ssrc),
                     writes=[srb], dma=srb)
                for m in range(8):
                    pt, pb = next_ps()
                    for kc in range(kc_n):
                        P.op("pe", lambda e, pt=pt, m=m, kc=kc, sr=sr, N=N: e.matmul(
                            pt[:, :N], lhsT=w_t[:, kc, m * 128:(m + 1) * 128], rhs=sr[:, kc, :N],
                            start=(kc == 0), stop=(kc == kc_n - 1)), reads=[wl(m * 128), srb], writes=[pb])
                    P.op("dve", lambda e, pt=pt, m=m, xt=xt, N=N, row=row: e.scalar_tensor_tensor(
                        out=xt[:, m, :N], in0=pt[:, :N], scalar=mod_t[:, l, 2 * 8 + m, row:row + 1],
                        in1=xt[:, m, :N], op0=ALU.mult, op1=ALU.add), reads=[pb, xb, mod_b], writes=[xb])
                xdst = X[s, :, t0:t0 + N].rearrange("(c p) t -> p c t", p=128)
                P.op("pool", lambda e, xt=xt, xdst=xdst, N=N: e.dma_start(out=xdst, in_=xt[:, :, :N]),
                     reads=[xb], dma=xb)
        P.phase_end()
        first_x_src[0] = False

    def mlp_phase(l, with_ctx, final):
        P.phase_begin()
        w1_t, w1_b = P.sb([128, 8, DFF], BF16, "w1")
        w2_t, w2_b = P.sb([128, 32, D], BF16, "w2")
        wl1 = load_w(w1_t, w1_b, mlp_w1[l], 8, DFF)
        wl2 = load_w(w2_t, w2_b, mlp_w2[l], 32, D)
        xts = [P.sb([128, 8, 512], F32, "xt") for _ in range(2)]
        h_t, h_b = P.sb([128, 8, 512], BF16, "h")
        sq_t, sq_b = P.sb([128, 8, 512], BF16, "sq")
        a_t, a_b = P.sb([128, 16, 512], BF16, "a")
        rt, rtb = P.sb([128, 512], F32, "rt")
        rs, rsb = P.sb([128, 512], F32, "rs")
        rr = [P.sb([128, 512], F32, "relu") for _ in range(2)]
        tiles = [(s, t0, N, is_ctx) for s in range(NSEQ) for (t0, N, is_ctx) in seq_tiles(with_ctx)]
        nt = len(tiles)
        cnt = {"r": 0}

        def tinfo(i):
            s, t0, N, is_ctx = tiles[i]
            xt, xb = xts[i % 2]
            return s, t0, N, (2 if is_ctx else s), xt, xb

        def load(i):
            s, t0, N, row, xt, xb = tinfo(i)
            xsrc = x_src()[s, :, t0:t0 + N].rearrange("(c p) t -> p c t", p=128)
            P.op("sp", lambda e: e.dma_start(out=xt[:, :, :N], in_=xsrc), writes=[xb], dma=xb)

        def rstd_chain(src_t, src_b, N):
            pt, pb = next_ps()
            for c in range(8):
                P.op("pe", lambda e, c=c: e.matmul(pt[:, :N], lhsT=ones_t[:], rhs=sq_t[:, c, :N],
                                                   start=(c == 0), stop=(c == 7)), reads=[sq_b, ones_b], writes=[pb])
            P.op("act", lambda e: e.activation(out=rt[:, :N], in_=pt[:, :N], func=AF.Sqrt, scale=1.0 / D, bias=EPS),
                 reads=[pb], writes=[rtb])
            P.op("dve", lambda e: e.reciprocal(out=rs[:, :N], in_=rt[:, :N]), reads=[rtb], writes=[rsb])

        def norm_a(i):
            s, t0, N, row, xt, xb = tinfo(i)
            P.op("act", lambda e: e.activation(out=sq_t[:, :, :N], in_=xt[:, :, :N], func=AF.Square),
                 reads=[xb], writes=[sq_b])

        def norm_b(i):
            s, t0, N, row, xt, xb = tinfo(i)
            rstd_chain(xt, xb, N)

        def norm_c(i):
            s, t0, N, row, xt, xb = tinfo(i)
            for c in range(8):
                tm_t, tm_b = rr[c % 2]
                P.op("dve", lambda e, c=c, tm_t=tm_t: e.scalar_tensor_tensor(
                    out=tm_t[:, :N], in0=xt[:, c, :N], scalar=gp_t[:, l, 1, c, row:row + 1], in1=rs[:, :N],
                    op0=ALU.mult, op1=ALU.mult), reads=[xb, rsb, gp_b], writes=[tm_b])
                P.op("act", lambda e, c=c, tm_t=tm_t: e.activation(
                    out=h_t[:, c, :N], in_=tm_t[:, :N], func=AF.Identity,
                    bias=mod_t[:, l, 3 * 8 + c, row:row + 1]), reads=[tm_b, mod_b], writes=[h_b])

        def w1(i, half):
            s, t0, N, row, xt, xb = tinfo(i)
            for fi in range(16):
                f = half * 16 + fi
                pt, pb = next_ps()
                for kc in range(8):
                    P.op("pe", lambda e, pt=pt, f=f, kc=kc: e.matmul(
                        pt[:, :N], lhsT=w1_t[:, kc, f * 128:(f + 1) * 128], rhs=h_t[:, kc, :N],
                        start=(kc == 0), stop=(kc == 7)), reads=[wl1(f * 128), h_b], writes=[pb])
                r_t, r_b = rr[cnt["r"] % 2]
                cnt["r"] += 1
                P.op("act", lambda e, pt=pt, r_t=r_t: e.activation(
                    out=r_t[:, :N], in_=pt[:, :N], func=AF.Relu), reads=[pb], writes=[r_b])
                P.op("dve", lambda e, r_t=r_t, fi=fi: e.tensor_tensor(
                    out=a_t[:, fi, :N], in0=r_t[:, :N], in1=r_t[:, :N], op=ALU.mult),
                    reads=[r_b], writes=[a_b])

        def w2(i, half):
            s, t0, N, row, xt, xb = tinfo(i)
            for m in range(8):
                pt, pb = next_ps()
                for fi in range(16):
                    f = half * 16 + fi
                    P.op("pe", lambda e, pt=pt, m=m, f=f, fi=fi: e.matmul(
                        pt[:, :N], lhsT=w2_t[:, f, m * 128:(m + 1) * 128], rhs=a_t[:, fi, :N],
                        start=(fi == 0), stop=(fi == 15)), reads=[wl2(m * 128), a_b], writes=[pb])
                P.op("dve", lambda e, pt=pt, m=m: e.scalar_tensor_tensor(
                    out=xt[:, m, :N], in0=pt[:, :N], scalar=mod_t[:, l, 5 * 8 + m, row:row + 1],
                    in1=xt[:, m, :N], op0=ALU.mult, op1=ALU.add), reads=[pb, xb, mod_b], writes=[xb])

        def fin_a(i):
            s, t0, N, row, xt, xb = tinfo(i)
            P.op("act", lambda e: e.activation(out=sq_t[:, :, :N], in_=xt[:, :, :N], func=AF.Square),
                 reads=[xb], writes=[sq_b])

        def fin_b(i):
            s, t0, N, row, xt, xb = tinfo(i)
            rstd_chain(xt, xb, N)
            for c in range(8):
                P.op("dve", lambda e, c=c: e.scalar_tensor_tensor(
                    out=xt[:, c, :N], in0=xt[:, c, :N], scalar=vcol("fin", c), in1=rs[:, :N],
                    op0=ALU.mult, op1=ALU.mult), reads=[xb, rsb, vec_b], writes=[xb])

        def store(i):
            s, t0, N, row, xt, xb = tinfo(i)
            if final:
                xdst = out[s, :, t0 - TC:t0 - TC + N].rearrange("(c p) t -> p c t", p=128)
            else:
                xdst = X[s, :, t0:t0 + N].rearrange("(c p) t -> p c t", p=128)
            P.op("pool", lambda e: e.dma_start(out=xdst, in_=xt[:, :, :N]), reads=[xb], dma=xb)

        load(0)
        if nt > 1:
            load(1)
        norm_a(0)
        norm_b(0)
        norm_c(0)
        for i in range(nt):
            w1(i, 0)
            if final and i > 0:
                fin_b(i - 1)
                store(i - 1)
                if i + 1 < nt:
                    load(i + 1)
            w2(i, 0)
            if i + 1 < nt:
                norm_a(i + 1)
            w1(i, 1)
            if i + 1 < nt:
                norm_b(i + 1)
                norm_c(i + 1)
            w2(i, 1)
            if final:
                fin_a(i)
                if i == nt - 1:
                    fin_b(i)
                    store(i)
            else:
                store(i)
                if i + 2 < nt:
                    load(i + 2)
        P.phase_end()
        first_x_src[0] = False

    def fnet_phase(l, j, with_ctx):
        P.phase_begin()
        w_t, w_b = P.sb([128, 8, D], BF16, "win")
        wl = load_w(w_t, w_b, fnet_w_in[j], 8, D)
        cd_t, cd_b = P.sb([128, 256], BF16, "cdft")
        P.op("sp", lambda e: e.dma_start(out=cd_t[:], in_=cdft[:, :]), writes=[cd_b], dma=cd_b)
        us = [P.sb([128, 8, 512], BF16, "u") for _ in range(2)]
        abs_ = [P.sb([128, 4, 2048], BF16, "ab") for _ in range(2)]
        cnt = {"ev": 0}

        def evac(pt, pb, out_ap, ob, width):
            cnt["ev"] += 1
            if cnt["ev"] % 2:
                P.op("act", lambda e: e.activation(out=out_ap, in_=pt[:, :width], func=AF.Copy),
                     reads=[pb], writes=[ob])
            else:
                P.op("dve", lambda e: e.tensor_copy(out=out_ap, in_=pt[:, :width]), reads=[pb], writes=[ob])

        def fn1_body(i, s, t0, N, is_ctx, row, h_t, h_b):
            u_t, u_b = us[i % 2]
            ab_t, ab_b = abs_[i % 2]
            for g in range(8):
                pt, pb = next_ps()
                for kc in range(8):
                    P.op("pe", lambda e, pt=pt, g=g, kc=kc: e.matmul(
                        pt[:, :N], lhsT=w_t[:, kc, g * 128:(g + 1) * 128], rhs=h_t[:, kc, :N],
                        start=(kc == 0), stop=(kc == 7)), reads=[wl(g * 128), h_b], writes=[pb])
                evac(pt, pb, u_t[:, g, :N], u_b, N)
            nqc = N // 128
            for qc in range(nqc):
                for gp2 in range(4):
                    pt, pb = next_ps()
                    for gg in range(2):
                        g = gp2 * 2 + gg
                        P.op("pe", lambda e, pt=pt, g=g, gg=gg, qc=qc: e.matmul(
                            pt[:, gg * 256:(gg + 1) * 256], lhsT=u_t[:, g, qc * 128:(qc + 1) * 128],
                            rhs=cd_t[:], start=True, stop=True), reads=[u_b, cd_b], writes=[pb])
                    evac(pt, pb, ab_t[:, qc, gp2 * 512:(gp2 + 1) * 512], ab_b, 512)
            dst = AB[s, t0:t0 + N, :].rearrange("(q p) f -> p q f", p=128)
            P.op("pool", lambda e: e.dma_start(out=dst, in_=ab_t[:, :nqc, :]), reads=[ab_b], dma=ab_b)

        pipelined_tiles(l, with_ctx, fn1_body)
        P.phase_end()

        P.phase_begin()
        abr = [P.sb([128, 8, 2048], BF16, "abr") for _ in range(4)]
        dfr = [P.sb([128, 2, 512], BF16, "dft") for _ in range(6)]
        fts = [P.sb([128, 8, 512], BF16, "ft") for _ in range(2)]
        ndf = 0
        nft = 0
        nev = 0
        for s in range(NSEQ):
            segs = [(TC, TL, dft_lat, 512, 8)]
            if with_ctx:
                segs.append((0, TC, dft_ctx, 256, 1))
            for (t0, T, table, KN, nkt) in segs:
                ntc = T // 128
                nq = (ntc + 7) // 8
                for qi in range(nq):
                    c0 = qi * 8
                    cn = min(8, ntc - c0)
                    a_t, a_b = abr[qi]
                    src = AB[s, t0 + c0 * 128:t0 + (c0 + cn) * 128, :].rearrange("(c p) f -> p c f", p=128)
                    P.op("sp", lambda e, a_t=a_t, src=src, cn=cn: e.dma_start(out=a_t[:, :cn, :], in_=src),
                         writes=[a_b], dma=a_b)
                for kt in range(nkt):
                    banks = psum
                    for tc in range(ntc):
                        d_t, d_b = dfr[ndf % 6]
                        ndf += 1
                        P.op("sp", lambda e, d_t=d_t, table=table, kt=kt, tc=tc, KN=KN: e.dma_start(
                            out=d_t[:, :, :KN], in_=table[kt, tc, :, :, :]), writes=[d_b], dma=d_b)
                        a_t, a_b = abr[tc // 8]
                        ci = tc % 8
                        for g in range(8):
                            pt, pb = banks[g]
                            for ab in range(2):
                                P.op("pe", lambda e, pt=pt, a_t=a_t, ci=ci, g=g, ab=ab, d_t=d_t, tc=tc, KN=KN,
                                     ntc=ntc: e.matmul(
                                    pt[:, :KN], lhsT=a_t[:, ci, g * 256 + ab * 128:g * 256 + ab * 128 + 128],
                                    rhs=d_t[:, ab, :KN], start=(tc == 0 and ab == 0),
                                    stop=(tc == ntc - 1 and ab == 1)), reads=[a_b, d_b], writes=[pb])
                    f_t, f_b = fts[nft % 2]
                    nft += 1
                    for g in range(8):
                        pt, pb = banks[g]
                        nev += 1
                        if nev % 2:
                            P.op("act", lambda e, pt=pt, f_t=f_t, g=g, KN=KN: e.activation(
                                out=f_t[:, g, :KN], in_=pt[:, :KN], func=AF.Copy), reads=[pb], writes=[f_b])
                        else:
                            P.op("dve", lambda e, pt=pt, f_t=f_t, g=g, KN=KN: e.tensor_copy(
                                out=f_t[:, g, :KN], in_=pt[:, :KN]), reads=[pb], writes=[f_b])
                    k0 = t0 + kt * KN
                    dst = FT[s, 0:D, k0:k0 + KN].rearrange("(c p) t -> p c t", p=128)
                    P.op("pool", lambda e, f_t=f_t, dst=dst, KN=KN: e.dma_start(out=dst, in_=f_t[:, :, :KN]),
                         reads=[f_b], dma=f_b)
        P.phase_end()
        proj_residual(l, fnet_w_out[j], 8, FT, with_ctx)

    def attn_phase(l):
        P.phase_begin()
        w_t, w_b = P.sb([128, 8, 1536], BF16, "wqkv")
        wl = load_w(w_t, w_b, attn_w_qkv, 8, 1536)
        cos_t, cos_b = P.sb([128, TL], F32, "cos")
        sin_t, sin_b = P.sb([128, TL], F32, "sin")
        psw_t, psw_b = P.sb([128, 128], F32, "pswap")
        P.op("sp", lambda e: e.dma_start(out=cos_t[:], in_=rope_cos[:, :]), writes=[cos_b], dma=cos_b)
        P.op("sp", lambda e: e.dma_start(out=sin_t[:], in_=rope_sin[:, :]), writes=[sin_b], dma=sin_b)
        P.op("sp", lambda e: e.dma_start(out=psw_t[:], in_=pswap[:, :]), writes=[psw_b], dma=psw_b)
        qks = [P.sb([128, 10, 512], BF16, "qk") for _ in range(2)]
        qk_kst = {id(b_): P.newbuf("qk_kstore") for (_, b_) in qks}
        qk_cb = {id(b_): [P.newbuf("qkc%d" % k) for k in range(10)] for (_, b_) in qks}
        sq_t, sq_b = P.sb([128, 10, 512], BF16, "sqh")
        qf_t, qf_b = P.sb([128, 10, 512], F32, "qf")
        rq_t, rq_b = P.sb([128, 10, 512], F32, "rq")
        t1s = [P.sb([128, 512], F32, "t1") for _ in range(2)]
        t2s = [P.sb([128, 512], F32, "t2") for _ in range(2)]
        vss = [P.sb([128, 4, 256], BF16, "vs") for _ in range(2)]
        sqb = [P.newbuf("sq%d" % k) for k in range(10)]
        qfb = [P.newbuf("qf%d" % k) for k in range(10)]
        rqb = [P.newbuf("rq%d" % k) for k in range(10)]

        def at1_body(i, s, t0, N, is_ctx, row, h_t, h_b):
            qk_t, qk_b = qks[i % 2]
            qkc = qk_cb[id(qk_b)]
            v_t, v_b = vss[i % 2]
            for hc in range(10):
                pq, pqb = next_ps()
                for kc in range(8):
                    P.op("pe", lambda e, pq=pq, hc=hc, kc=kc: e.matmul(
                        pq[:, :N], lhsT=w_t[:, kc, hc * 128:(hc + 1) * 128], rhs=h_t[:, kc, :N],
                        start=(kc == 0), stop=(kc == 7)), reads=[wl(hc * 128), h_b], writes=[pqb])
                P.op("act", lambda e, pq=pq, hc=hc: e.activation(
                    out=sq_t[:, hc, :N], in_=pq[:, :N], func=AF.Square), reads=[pqb], writes=[sqb[hc]])
                P.op("act", lambda e, pq=pq, hc=hc: e.activation(
                    out=qf_t[:, hc, :N], in_=pq[:, :N], func=AF.Copy), reads=[pqb], writes=[qfb[hc]])
            nqc = N // 128
            for qc in range(nqc):
                pv, pvb = next_ps()
                for kc in range(8):
                    P.op("pe", lambda e, pv=pv, kc=kc, qc=qc: e.matmul(
                        pv[:, 0:256], lhsT=h_t[:, kc, qc * 128:(qc + 1) * 128], rhs=w_t[:, kc, 1280:1536],
                        start=(kc == 0), stop=(kc == 7)), reads=[wl(1280), h_b], writes=[pvb])
                P.op("act", lambda e, pv=pv, qc=qc: e.activation(out=v_t[:, qc, :], in_=pv[:, 0:256], func=AF.Copy),
                     reads=[pvb], writes=[v_b])
            vdst = VV[s, t0:t0 + N, :].rearrange("(q p) d -> p q d", p=128)
            P.op("pool", lambda e: e.dma_start(out=vdst, in_=v_t[:, :nqc, :]), reads=[v_b], dma=v_b)
            for hc in range(10):
                p2, p2b = next_ps()
                P.op("pe", lambda e, p2=p2, hc=hc: e.matmul(
                    p2[:, :N], lhsT=ones_t[:], rhs=sq_t[:, hc, :N], start=True, stop=True),
                    reads=[sqb[hc], ones_b], writes=[p2b])
                P.op("act", lambda e, p2=p2, hc=hc: e.activation(
                    out=rq_t[:, hc, :N], in_=p2[:, :N], func=AF.Sqrt, scale=1.0 / 128, bias=EPS),
                    reads=[p2b], writes=[rqb[hc]])
            for hc in range(10):
                gcol = misc_t[:, 0:1] if hc < 8 else misc_t[:, 1:2]
                P.op("dve", lambda e, hc=hc: e.reciprocal(out=rq_t[:, hc, :N], in_=rq_t[:, hc, :N]),
                     reads=[rqb[hc]], writes=[rqb[hc]])
                if is_ctx:
                    P.op("dve", lambda e, hc=hc, gcol=gcol: e.scalar_tensor_tensor(
                        out=qk_t[:, hc, :N], in0=qf_t[:, hc, :N], scalar=gcol, in1=rq_t[:, hc, :N],
                        op0=ALU.mult, op1=ALU.mult), reads=[qfb[hc], rqb[hc], misc_b], writes=[qkc[hc]])
                else:
                    P.op("dve", lambda e, hc=hc, gcol=gcol: e.scalar_tensor_tensor(
                        out=qf_t[:, hc, :N], in0=qf_t[:, hc, :N], scalar=gcol, in1=rq_t[:, hc, :N],
                        op0=ALU.mult, op1=ALU.mult), reads=[qfb[hc], rqb[hc], misc_b], writes=[qfb[hc]])
            if not is_ctx:
                l0 = t0 - TC
                for hc in range(10):
                    t1_t, t1_b = t1s[hc % 2]
                    t2_t, t2_b = t2s[hc % 2]
                    p3, p3b = next_ps()
                    P.op("pe", lambda e, p3=p3, hc=hc: e.matmul(
                        p3[:, :N], lhsT=psw_t[:], rhs=qf_t[:, hc, :N], start=True, stop=True),
                        reads=[qfb[hc], psw_b], writes=[p3b])
                    e1, e2 = ("pool", "dve") if hc % 2 == 0 else ("dve", "pool")
                    P.op(e1, lambda e, t1_t=t1_t, hc=hc: e.tensor_tensor(
                        out=t1_t[:, :N], in0=qf_t[:, hc, :N], in1=cos_t[:, l0:l0 + N], op=ALU.mult),
                        reads=[qfb[hc], cos_b], writes=[t1_b])
                    P.op("dve", lambda e, t2_t=t2_t, p3=p3: e.tensor_tensor(
                        out=t2_t[:, :N], in0=p3[:, :N], in1=sin_t[:, l0:l0 + N], op=ALU.mult),
                        reads=[p3b, sin_b], writes=[t2_b])
                    P.op(e2, lambda e, hc=hc, t1_t=t1_t, t2_t=t2_t: e.tensor_tensor(
                        out=qk_t[:, hc, :N], in0=t1_t[:, :N], in1=t2_t[:, :N], op=ALU.add),
                        reads=[t1_b, t2_b], writes=[qkc[hc]])
            qdst = QT[s, :, t0:t0 + N].rearrange("(c p) t -> p c t", p=128)
            P.op("pool", lambda e: e.dma_start(out=qdst, in_=qk_t[:, 0:8, :N]), reads=qkc[0:8], dma=qk_b)
            kdst = KT[s, :, t0:t0 + N].rearrange("(c p) t -> p c t", p=128)
            P.op("pool", lambda e: e.dma_start(out=kdst, in_=qk_t[:, 8:10, :N]),
                 reads=qkc[8:10], dma=qk_kst[id(qk_b)])

        pipelined_tiles(l, True, at1_body)
        P.phase_end()

        P.phase_begin()
        kts = [P.sb([128, 2, TT], BF16, "kt") for _ in range(1)]
        vts = [P.sb([128, 34, 256], BF16, "vt") for _ in range(1)]
        qrs = [P.sb([128, 512], BF16, "q") for _ in range(3)]
        prs = [P.sb([128, 512], BF16, "p") for _ in range(4)]
        ors = [P.sb([128, 512], BF16, "o") for _ in range(2)]
        rds = [P.sb([128, 512], F32, "rd") for _ in range(2)]
        sbanks = psum[0:4]
        obanks = [(psum[4], psum[5]), (psum[6], psum[7])]
        nq = 0
        npp = 0
        nsb = 0
        for s in range(NSEQ):
            if s > 0:
                P.barrier()
            k_t, k_b = kts[0]
            v_t, v_b = vts[0]
            ksrc = KT[s, :, :].rearrange("(c p) t -> p c t", p=128)
            P.op("sp", lambda e, k_t=k_t, ksrc=ksrc: e.dma_start(out=k_t[:], in_=ksrc), writes=[k_b], dma=k_b)
            vsrc = VV[s, :, :].rearrange("(c p) d -> p c d", p=128)
            P.op("sp", lambda e, v_t=v_t, vsrc=vsrc: e.dma_start(out=v_t[:], in_=vsrc), writes=[v_b], dma=v_b)
            for hk in range(2):
                for hq in range(4):
                    hh = hk * 4 + hq
                    for (t0, N, is_ctx) in seq_tiles(True):
                        chunks = [0, 1] if is_ctx else list(range(34))
                        q_t, q_b = qrs[nq % 3]
                        (po, pob), (pd, pdb) = obanks[nq % 2]
                        o_t, o_b = ors[nq % 2]
                        rd_t, rd_b = rds[nq % 2]
                        nq += 1
                        qsrc = QT[s, hh * 128:(hh + 1) * 128, t0:t0 + N]
                        P.op("sp", lambda e, q_t=q_t, qsrc=qsrc, N=N: e.dma_start(out=q_t[:, :N], in_=qsrc),
                             writes=[q_b], dma=q_b)
                        pend = []

                        def issue_s(kc, q_t=q_t, q_b=q_b, N=N, hk=hk):
                            nonlocal nsb
                            ps_t, ps_b = sbanks[nsb % 4]
                            nsb += 1
                            P.op("pe", lambda e: e.matmul(ps_t[:, :N], lhsT=k_t[:, hk, kc * 128:(kc + 1) * 128],
                                                          rhs=q_t[:, :N], start=True, stop=True),
                                 reads=[k_b, q_b], writes=[ps_b])
                            return (ps_t, ps_b)

                        ahead = 2
                        for i in range(min(ahead, len(chunks))):
                            pend.append(issue_s(chunks[i]))
                        for i, kc in enumerate(chunks):
                            if i + ahead < len(chunks):
                                pend.append(issue_s(chunks[i + ahead]))
                            ps_t, ps_b = pend.pop(0)
                            p_t, p_b = prs[npp % 4]
                            npp += 1
                            P.op("act", lambda e, ps_t=ps_t, p_t=p_t, N=N: e.activation(
                                out=p_t[:, :N], in_=ps_t[:, :N], func=AF.Exp), reads=[ps_b], writes=[p_b])
                            first = (i == 0)
                            last = (i == len(chunks) - 1)
                            P.op("pe", lambda e, po=po, p_t=p_t, kc=kc, hk=hk, N=N, first=first, last=last: e.matmul(
                                po[:, :N], lhsT=v_t[:, kc, hk * 128:(hk + 1) * 128], rhs=p_t[:, :N],
                                start=first, stop=last), reads=[v_b, p_b], writes=[pob])
                            P.op("pe", lambda e, pd=pd, p_t=p_t, N=N, first=first, last=last: e.matmul(
                                pd[:, :N], lhsT=ones_t[:], rhs=p_t[:, :N], start=first, stop=last),
                                reads=[ones_b, p_b], writes=[pdb])
                        P.op("dve", lambda e, pd=pd, rd_t=rd_t, N=N: e.reciprocal(out=rd_t[:, :N], in_=pd[:, :N]),
                             reads=[pdb], writes=[rd_b])
                        P.op("dve", lambda e, po=po, rd_t=rd_t, o_t=o_t, N=N: e.tensor_tensor(
                            out=o_t[:, :N], in0=po[:, :N], in1=rd_t[:, :N], op=ALU.mult),
                            reads=[pob, rd_b], writes=[o_b])
                        odst = FT[s, hh * 128:(hh + 1) * 128, t0:t0 + N]
                        P.op("pool", lambda e, o_t=o_t, odst=odst, N=N: e.dma_start(out=odst, in_=o_t[:, :N]),
                             reads=[o_b], dma=o_b)
        P.phase_end()
        proj_residual(l, attn_w_o, 8, FT, True)

    def lru_phase(l):
        P.phase_begin()
        w_t, w_b = P.sb([128, 8, 2 * DRNN], BF16, "wlin")
        wl = load_w(w_t, w_b, lru_w_in, 8, 2 * DRNN)
        gos = [P.sb([128, 10, 512], F32, "go") for _ in range(1)]
        xos = [P.sb([128, 10, 512], F32, "xo") for _ in range(1)]
        x2s = [P.sb([128, 512], F32, "x2") for _ in range(3)]
        ins_ = [P.sb([128, 512], F32, "inn") for _ in range(3)]
        goc = [P.newbuf("goc%d" % k) for k in range(10)]
        xoc = [P.newbuf("xoc%d" % k) for k in range(10)]

        def lr1_body(i, s, t0, N, is_ctx, row, h_t, h_b):
            go_t, go_b = gos[0]
            xo_t, xo_b = xos[0]

            def proj(oc):
                pt, pb = next_ps()
                for kc in range(8):
                    P.op("pe", lambda e, kc=kc: e.matmul(
                        pt[:, :N], lhsT=w_t[:, kc, oc * 128:(oc + 1) * 128], rhs=h_t[:, kc, :N],
                        start=(kc == 0), stop=(kc == 7)), reads=[wl(oc * 128), h_b], writes=[pb])
                return pt, pb

            for oc in range(10, 20):
                pt, pb = proj(oc)
                P.op("act", lambda e, pt=pt, oc=oc: e.activation(
                    out=xo_t[:, oc - 10, :N], in_=pt[:, :N], func=AF.Copy), reads=[pb], writes=[xoc[oc - 10]])
            xdst = XR[s, :, t0:t0 + N].rearrange("(c p) t -> p c t", p=128)
            P.op("pool", lambda e: e.dma_start(out=xdst, in_=xo_t[:, :, :N]), reads=xoc, dma=xo_b)

            def st_a(oc):
                pt, pb = proj(oc)
                x2_t, x2_b = x2s[oc % 3]
                in_t, in_b = ins_[oc % 3]
                P.op("act", lambda e: e.activation(out=x2_t[:, :N], in_=pt[:, :N], func=AF.Square),
                     reads=[pb], writes=[x2_b])
                P.op("dve", lambda e: e.tensor_scalar(
                    out=x2_t[:, :N], in0=x2_t[:, :N], scalar1=0.044715, scalar2=1.0, op0=ALU.mult, op1=ALU.add),
                    reads=[x2_b], writes=[x2_b])
                P.op("dve", lambda e: e.tensor_tensor(
                    out=in_t[:, :N], in0=pt[:, :N], in1=x2_t[:, :N], op=ALU.mult),
                    reads=[pb, x2_b], writes=[in_b])
                return pt, pb, in_t, in_b

            def st_b(oc, st):
                pt, pb, in_t, in_b = st
                P.op("act", lambda e: e.activation(
                    out=in_t[:, :N], in_=in_t[:, :N], func=AF.Sigmoid, scale=1.5957691216057308),
                    reads=[in_b], writes=[in_b])
                P.op("dve", lambda e: e.tensor_tensor(
                    out=go_t[:, oc, :N], in0=pt[:, :N], in1=in_t[:, :N], op=ALU.mult),
                    reads=[pb, in_b], writes=[goc[oc]])

            prev = st_a(0)
            for oc in range(10):
                nxt = st_a(oc + 1) if oc + 1 < 10 else None
                st_b(oc, prev)
                prev = nxt
            gdst = GG[s, :, t0:t0 + N].rearrange("(c p) t -> p c t", p=128)
            P.op("pool", lambda e: e.dma_start(out=gdst, in_=go_t[:, :, :N]), reads=goc, dma=go_b)

        pipelined_tiles(l, True, lr1_body)
        P.phase_end()

        P.phase_begin()
        ga_t, ga_b = P.sb([128, 20, 128], BF16, "ga")
        gx_t, gx_b = P.sb([128, 20, 128], BF16, "gx")
        P.op("pool", lambda e: e.dma_start(out=ga_t[:], in_=lru_ga_w.rearrange("d n k j -> k (d n) j")),
             writes=[ga_b], dma=ga_b)
        P.op("pool", lambda e: e.dma_start(out=gx_t[:], in_=lru_gx_w.rearrange("d n k j -> k (d n) j")),
             writes=[gx_b], dma=gx_b)
        xr_t, xr_b = P.sb([128, TT], F32, "xr")
        g_t, g_b = P.sb([128, TT], F32, "g")
        xc_t, xc_b = P.sb([128, TT], F32, "xc")
        xcb_t, xcb_b = P.sb([128, TT], BF16, "xcb")
        ra_t, ra_b = P.sb([128, TT], F32, "ra")
        iu_t, iu_b = P.sb([128, TT], F32, "iu")
        s_t, s_b = P.sb([128, TT], F32, "s")
        hf_t, hf_b = P.sb([128, TT], F32, "hf")
        hb_t, hb_b = P.sb([128, TT], F32, "hb")
        o_t, o_b = P.sb([128, TT], BF16, "o")
        segs = [(0, TC), (TC, TL)]
        for s in range(NSEQ):
            for n in range(10):
                P.op("sp", lambda e, s=s, n=n: e.dma_start(out=xr_t[:], in_=XR[s, n * 128:(n + 1) * 128, :]),
                     writes=[xr_b], dma=xr_b)
                P.op("sp", lambda e, s=s, n=n: e.dma_start(out=g_t[:], in_=GG[s, n * 128:(n + 1) * 128, :]),
                     writes=[g_b], dma=g_b)
                P.op("dve", lambda e, n=n: e.tensor_scalar(
                    out=xc_t[:], in0=xr_t[:], scalar1=vcol("conv_w", 2 * 10 + n), scalar2=vcol("conv_b", n),
                    op0=ALU.mult, op1=ALU.add), reads=[xr_b, vec_b], writes=[xc_b])
                for (o0, L) in segs:
                    for k, sh in ((0, -2), (1, -1), (3, 1)):
                        if sh < 0:
                            src_sl = (o0, o0 + L + sh)
                            dst_sl = (o0 - sh, o0 + L)
                        else:
                            src_sl = (o0 + sh, o0 + L)
                            dst_sl = (o0, o0 + L - sh)
                        eng = "pool" if k == 0 else "dve"
                        P.op("dve", lambda e, n=n, k=k, src_sl=src_sl, dst_sl=dst_sl: e.scalar_tensor_tensor(
                            out=xc_t[:, dst_sl[0]:dst_sl[1]], in0=xr_t[:, src_sl[0]:src_sl[1]],
                            scalar=vcol("conv_w", k * 10 + n), in1=xc_t[:, dst_sl[0]:dst_sl[1]],
                            op0=ALU.mult, op1=ALU.add), reads=[xr_b, xc_b, vec_b], writes=[xc_b])
                P.op("act", lambda e: e.activation(out=xcb_t[:], in_=xc_t[:], func=AF.Copy),
                     reads=[xc_b], writes=[xcb_b])
                for d in range(2):
                    for (t0, N, is_ctx) in seq_tiles(True):
                        pr, prb = next_ps()
                        P.op("pe", lambda e, pr=pr, d=d, n=n, t0=t0, N=N: e.matmul(
                            pr[:, :N], lhsT=ga_t[:, d * 10 + n, :], rhs=xcb_t[:, t0:t0 + N], start=True, stop=True),
                            reads=[ga_b, xcb_b], writes=[prb])
                        P.op("act", lambda e, pr=pr, d=d, n=n, t0=t0, N=N: e.activation(
                            out=ra_t[:, t0:t0 + N], in_=pr[:, :N], func=AF.Sigmoid,
                            bias=vcol("ga_b", d * 10 + n)), reads=[prb, vec_b], writes=[ra_b])
                        pi, pib = next_ps()
                        P.op("pe", lambda e, pi=pi, d=d, n=n, t0=t0, N=N: e.matmul(
                            pi[:, :N], lhsT=gx_t[:, d * 10 + n, :], rhs=xcb_t[:, t0:t0 + N], start=True, stop=True),
                            reads=[gx_b, xcb_b], writes=[pib])
                        P.op("act", lambda e, pi=pi, d=d, n=n, t0=t0, N=N: e.activation(
                            out=iu_t[:, t0:t0 + N], in_=pi[:, :N], func=AF.Sigmoid,
                            bias=vcol("gx_b", d * 10 + n)), reads=[pib, vec_b], writes=[iu_b])
                    P.op("act", lambda e, d=d, n=n: e.activation(
                        out=ra_t[:], in_=ra_t[:], func=AF.Exp, scale=misc_t[:, 2 + d * 10 + n:3 + d * 10 + n]),
                        reads=[ra_b, misc_b], writes=[ra_b])
                    P.op("act", lambda e: e.activation(out=s_t[:], in_=ra_t[:], func=AF.Square),
                         reads=[ra_b], writes=[s_b])
                    P.op("act", lambda e: e.activation(out=s_t[:], in_=s_t[:], func=AF.Sqrt, scale=-1.0, bias=1.0),
                         reads=[s_b], writes=[s_b])
                    P.op("pool", lambda e: e.tensor_tensor(out=iu_t[:], in0=iu_t[:], in1=xc_t[:], op=ALU.mult),
                         reads=[iu_b, xc_b], writes=[iu_b])
                    P.op("dve", lambda e: e.tensor_tensor(out=iu_t[:], in0=iu_t[:], in1=s_t[:], op=ALU.mult),
                         reads=[iu_b, s_b], writes=[iu_b])
                    if d == 0:
                        P.op("dve", lambda e: e.tensor_tensor_scan(
                            out=hf_t[:], data0=ra_t[:], data1=iu_t[:], initial=0.0, op0=ALU.mult, op1=ALU.add),
                            reads=[ra_b, iu_b], writes=[hf_b])
                    else:
                        P.op("dve", lambda e: e.tensor_tensor_scan(
                            out=hb_t[:, 0:TC][:, ::-1], data0=ra_t[:, 0:TC][:, ::-1], data1=iu_t[:, 0:TC][:, ::-1],
                            initial=0.0, op0=ALU.mult, op1=ALU.add), reads=[ra_b, iu_b], writes=[hb_b])
                        P.op("dve", lambda e: e.tensor_tensor_scan(
                            out=hb_t[:, TC:TT][:, ::-1], data0=ra_t[:, TC:TT][:, ::-1],
                            data1=iu_t[:, TC:TT][:, ::-1], initial=hb_t[:, 0:1], op0=ALU.mult, op1=ALU.add),
                            reads=[ra_b, iu_b, hb_b], writes=[hb_b])
                P.op("pool", lambda e: e.tensor_tensor(out=hf_t[:], in0=hf_t[:], in1=hb_t[:], op=ALU.add),
                     reads=[hf_b, hb_b], writes=[hf_b])
                P.op("dve", lambda e: e.tensor_tensor(out=o_t[:], in0=hf_t[:], in1=g_t[:], op=ALU.mult),
                     reads=[hf_b, g_b], writes=[o_b])
                P.op("pool", lambda e, s=s, n=n: e.dma_start(out=FT[s, n * 128:(n + 1) * 128, :], in_=o_t[:]),
                     reads=[o_b], dma=o_b)
        P.phase_end()
        proj_residual(l, lru_w_out, 10, FT, True)

    prologue()
    for (l, part) in steps:
        kind = l % 3
        j = l // 3
        last = l == DEPTH - 1
        with_ctx = not last
        if part == "mix":
            if kind == 0:
                fnet_phase(l, j, with_ctx)
            elif kind == 1:
                attn_phase(l)
            else:
                lru_phase(l)
        else:
            mlp_phase(l, with_ctx, final=last)
    P.emit()
    return nc


def _consts():
    bf = ml_dtypes.bfloat16
    c = np.arange(128, dtype=np.float64)
    ang = 2 * np.pi * np.outer(c, c) / 128.0
    cd = np.concatenate([np.cos(ang), np.sin(ang)], axis=1) / np.sqrt(128.0)

    def pos_table(T, KN):
        t = np.arange(T, dtype=np.int64)
        m = np.outer(t, t) % T
        a = 2 * np.pi * m.astype(np.float64) / T
        C = np.cos(a) / np.sqrt(T)
        S = -np.sin(a) / np.sqrt(T)
        nkt = T // KN
        ntc = T // 128
        tab = np.empty((nkt, ntc, 128, 2, KN), dtype=bf)
        Cr = C.reshape(ntc, 128, nkt, KN).transpose(2, 0, 1, 3)
        Sr = S.reshape(ntc, 128, nkt, KN).transpose(2, 0, 1, 3)
        tab[:, :, :, 0, :] = Cr.astype(bf)
        tab[:, :, :, 1, :] = Sr.astype(bf)
        return tab

    dft_lat = pos_table(TL, 512)
    dft_ctx = pos_table(TC, 256)
    n_freq = 32
    inv = (10000.0 ** (-np.arange(n_freq, dtype=np.float32) / n_freq)).astype(np.float32)
    tok = np.arange(TL)
    row = (tok // 64).astype(np.float32)
    col = (tok % 64).astype(np.float32)
    ang = np.concatenate([row[:, None] * inv, col[:, None] * inv], axis=-1).astype(np.float32)
    cs = np.cos(ang).T.astype(np.float32)
    sn = np.sin(ang).T.astype(np.float32)
    rope_cos = np.concatenate([cs, cs], axis=0)
    rope_sin = np.concatenate([-sn, sn], axis=0)
    pswap = np.zeros((128, 128), np.float32)
    for m in range(128):
        pswap[(m + 64) % 128, m] = 1.0
    return dict(cdft=cd.astype(bf), dft_lat=dft_lat, dft_ctx=dft_ctx,
                rope_cos=np.ascontiguousarray(rope_cos), rope_sin=np.ascontiguousarray(rope_sin), pswap=pswap)


_CONSTS = None


def _pcol(v):
    v = np.asarray(v, np.float32)
    return np.ascontiguousarray(v.reshape(-1, 128).T)


def make_in_maps(inp, n_cores=8, cores=None):
    global _CONSTS
    if _CONSTS is None:
        _CONSTS = _consts()
    perm = np.concatenate([np.arange(0, 128, 2), np.arange(1, 128, 2)])
    wqkv = np.asarray(inp["attn_w_qkv"][0], np.float32)
    cols = []
    for hh in range(10):
        cols.append(hh * 128 + perm)
    cols.append(np.arange(1280, 1536))
    wqkv_p = np.ascontiguousarray(wqkv[:, np.concatenate(cols)])

    vecs = np.zeros((128, VOFF["_n"]), np.float32)

    def put(name, arr2d):
        vecs[:, VOFF[name]:VOFF[name] + arr2d.shape[1]] = arr2d

    put("ada_b", np.concatenate([_pcol(inp["ada_b"][l]) for l in range(DEPTH)], axis=1))
    put("nmix", np.concatenate([_pcol(inp["norm_mix_g"][l]) for l in range(DEPTH)], axis=1))
    put("nmlp", np.concatenate([_pcol(inp["norm_mlp_g"][l]) for l in range(DEPTH)], axis=1))
    put("fin", _pcol(inp["final_norm_g"]))
    put("qg", np.asarray(inp["attn_q_norm_g"][0], np.float32)[perm][:, None])
    put("kg", np.asarray(inp["attn_k_norm_g"][0], np.float32)[perm][:, None])
    put("conv_w", np.concatenate([_pcol(inp["lru_conv_w"][0][k]) for k in range(4)], axis=1))
    put("conv_b", _pcol(inp["lru_conv_b"][0]))
    put("ga_b", np.concatenate([_pcol(inp["lru_gate_a_b"][0][d]) for d in range(2)], axis=1))
    put("gx_b", np.concatenate([_pcol(inp["lru_gate_x_b"][0][d]) for d in range(2)], axis=1))
    put("lam", np.concatenate([_pcol(inp["lru_lambda"][0][d]) for d in range(2)], axis=1))

    shared = dict(
        vecs=vecs,
        ada_w=np.ascontiguousarray(inp["ada_w"], np.float32),
        mlp_w1=np.ascontiguousarray(inp["mlp_w1"], np.float32),
        mlp_w2=np.ascontiguousarray(inp["mlp_w2"], np.float32),
        fnet_w_in=np.ascontiguousarray(inp["fnet_w_in"], np.float32),
        fnet_w_out=np.ascontiguousarray(inp["fnet_w_out"], np.float32),
        attn_w_qkv=wqkv_p,
        attn_w_o=np.ascontiguousarray(inp["attn_w_o"][0], np.float32),
        lru_w_in=np.ascontiguousarray(inp["lru_w_in"][0], np.float32),
        lru_ga_w=np.ascontiguousarray(inp["lru_gate_a_w"][0], np.float32),
        lru_gx_w=np.ascontiguousarray(inp["lru_gate_x_w"][0], np.float32),
        lru_w_out=np.ascontiguousarray(inp["lru_w_out"][0], np.float32),
        **_CONSTS,
    )
    x = np.asarray(inp["x"], np.float32)
    ctx = np.asarray(inp["ctx"], np.float32)
    c = np.asarray(inp["c"], np.float32)
    c_ctx = np.asarray(inp["c_ctx"], np.float32)
    maps = []
    for core in (range(n_cores) if cores is None else cores):
        x0 = np.empty((NSEQ, D, TT), np.float32)
        cv = np.zeros((128, 8, 4), np.float32)
        for s in range(NSEQ):
            b = core * NSEQ + s
            x0[s, :, :TC] = ctx[b].T
            x0[s, :, TC:] = x[b].T
            cv[:, :, s] = c[b].reshape(8, 128).T
        cv[:, :, 2] = c_ctx.reshape(8, 128).T
        cv[:, :, 3] = c_ctx.reshape(8, 128).T
        m = dict(shared)
        m["x0"] = x0
        m["cvec"] = cv
        maps.append(m)
    return maps


ALL_STEPS = [(l, p) for l in range(DEPTH) for p in ("mix", "mlp")]


def kernel(**inputs):
    nc = build_program(ALL_STEPS)
    maps = make_in_maps(inputs, 8)
    res = run_bass_kernel_spmd(nc, maps, core_ids=list(range(8)))
    outs = []
    for core in range(8):
        o = res.results[core]["out"]
        for s in range(NSEQ):
            outs.append(np.ascontiguousarray(np.asarray(o[s], np.float32).T))
    return np.stack(outs, axis=0).astype(np.float32)
```

```python
import numpy as np
import ml_dtypes
import concourse.bass as bass
import concourse.mybir as mybir
from concourse.bass_utils import run_bass_kernel_spmd

F32 = mybir.dt.float32
BF16 = mybir.dt.bfloat16
AF = mybir.ActivationFunctionType
ALU = mybir.AluOpType

D = 1024
NSEQ = 2
TC = 256
TL = 4096
TT = TC + TL
DFF = 4096
DRNN = 1280
DEPTH = 4
EPS = 1e-6
SB_BASE = 16640
SB_LIMIT = 229376

ENGS = ("sp", "pool", "act", "dve", "pe")


class Buf:
    __slots__ = ("name", "lastw", "readers", "sems")

    def __init__(self, name):
        self.name = name
        self.lastw = None
        self.readers = {}
        self.sems = {}


class Op:
    __slots__ = ("eng", "fn", "deps", "is_dma", "signal", "sigval", "sem", "semval", "kind", "bar", "epoch")

    def __init__(self, eng, fn, is_dma):
        self.eng = eng
        self.fn = fn
        self.deps = []
        self.is_dma = is_dma
        self.signal = False
        self.sigval = 0
        self.sem = None
        self.semval = 0
        self.kind = "op"
        self.bar = None
        self.epoch = 0


class Prog:
    def __init__(self, nc):
        self.nc = nc
        self.q = {e: [] for e in ENGS}
        self.sb_off = SB_BASE
        self.sb_marks = []
        self.uid = 0
        self.free_sems = {"sp": [], "act": [], "pool": []}
        self.semval = {}
        self.phase_bufs = []
        self.outstanding = {}
        self.nbar = 0
        self.nsem = 0
        self.engsem = {e: nc.alloc_semaphore("eng_" + e) for e in ("pool", "act", "dve", "pe")}
        self.barsem = nc.alloc_semaphore("barrier")
        self.psum = []
        for i in range(8):
            t = nc.alloc_psum_tensor("psb%d" % i, [128, 512], F32)
            self.psum.append((t, Buf("psum%d" % i)))

    def sb(self, shape, dtype, name="t"):
        self.uid += 1
        nbytes = int(np.prod(shape[1:])) * (2 if dtype == BF16 else 4)
        nbytes = (nbytes + 63) // 64 * 64
        off = self.sb_off
        if off + nbytes > SB_LIMIT:
            raise RuntimeError("SBUF overflow allocating %s: %d + %d" % (name, off, nbytes))
        self.sb_off += nbytes
        t = self.nc.alloc_sbuf_tensor_at("%s_%d" % (name, self.uid), list(shape), dtype, offset=off)
        b = Buf(name)
        self.phase_bufs.append(b)
        return t, b

    def newbuf(self, name):
        b = Buf(name)
        self.phase_bufs.append(b)
        return b

    def phase_begin(self):
        self.sb_marks.append(self.sb_off)

    def phase_end(self):
        self.barrier()
        self.sb_off = self.sb_marks.pop()
        for b in self.phase_bufs:
            for qn, sem in b.sems.items():
                self.free_sems[qn].append(sem)
            b.sems = {}
        self.phase_bufs = []

    def _dma_sem(self, buf, queue):
        sem = buf.sems.get(queue)
        if sem is None:
            if self.free_sems[queue]:
                sem = self.free_sems[queue].pop()
            else:
                self.nsem += 1
                sem = self.nc.alloc_semaphore("dma_%s%d" % (queue, self.nsem))
                self.semval[id(sem)] = 0
            buf.sems[queue] = sem
        return sem

    def _add_dep(self, op, dep, war=False):
        if dep is None or dep is op:
            return
        if dep.epoch < self.nbar:
            return
        if not dep.is_dma and dep.eng == op.eng and not op.is_dma:
            if op.eng == "pe":
                return
            if war:
                return
        if not dep.is_dma:
            dep.signal = True
        op.deps.append(dep)

    def op(self, eng, fn, reads=(), writes=(), dma=None):
        o = Op(eng, fn, dma is not None)
        o.epoch = self.nbar
        for b in reads:
            self._add_dep(o, b.lastw)
        for b in writes:
            self._add_dep(o, b.lastw)
            for r in b.readers.values():
                self._add_dep(o, r, war=True)
        if dma is not None:
            sem = self._dma_sem(dma, eng)
            self.semval[id(sem)] += 16
            o.sem = sem
            o.semval = self.semval[id(sem)]
            self.outstanding[id(sem)] = (sem, o.semval)
        for b in reads:
            key = ("dma", id(o.sem)) if o.is_dma else eng
            b.readers[key] = o
        for b in writes:
            b.lastw = o
            b.readers = {}
        self.q[eng].append(o)
        return o

    def barrier(self):
        self.nbar += 1
        waits = list(self.outstanding.values())
        self.outstanding = {}
        lasts = []
        for e in ("pool", "act", "dve", "pe"):
            for o in reversed(self.q[e]):
                if o.kind == "op" and not o.is_dma:
                    o.signal = True
                    lasts.append(o)
                    break
        for e in ENGS:
            o = Op(e, None, False)
            o.kind = "bar"
            o.bar = (self.nbar, waits if e == "pool" else None, lasts if e == "pool" else None)
            self.q[e].append(o)

    def emit(self):
        nc = self.nc
        for e in ("pool", "act", "dve", "pe"):
            n = 0
            for o in self.q[e]:
                if o.kind == "bar":
                    pass
                elif not o.is_dma and o.signal:
                    n += 1
                    o.sigval = n
        engsem = self.engsem
        barsem = self.barsem
        q = self.q

        def run(eng_name, eng):
            seen = {}

            def wait(sem, val):
                k = id(sem)
                if seen.get(k, 0) >= val:
                    return
                seen[k] = val
                eng.wait_ge(sem, val)

            for o in q[eng_name]:
                if o.kind == "bar":
                    k, waits, lasts = o.bar
                    if eng_name == "pool":
                        for sem, val in waits:
                            wait(sem, val)
                        for d in lasts:
                            wait(engsem[d.eng], d.sigval)
                        eng.nop().then_inc(barsem, 1)
                    wait(barsem, k)
                    continue
                for d in o.deps:
                    if d.is_dma:
                        wait(d.sem, d.semval)
                    else:
                        wait(engsem[d.eng], d.sigval)
                ins = o.fn(eng)
                if o.is_dma:
                    ins.then_inc(o.sem, 16)
                elif o.signal:
                    ins.then_inc(engsem[o.eng], 1)

        with nc.Block() as block:
            @block.sync
            def _(e):
                run("sp", e)

            @block.gpsimd
            def _(e):
                run("pool", e)

            @block.scalar
            def _(e):
                run("act", e)

            @block.vector
            def _(e):
                run("dve", e)

            @block.tensor
            def _(e):
                run("pe", e)


def _vec_layout():
    off = {}
    n = 0

    def add(name, cnt):
        nonlocal n
        off[name] = n
        n += cnt

    add("ada_b", DEPTH * 48)
    add("nmix", DEPTH * 8)
    add("nmlp", DEPTH * 8)
    add("fin", 8)
    add("qg", 1)
    add("kg", 1)
    add("conv_w", 4 * 10)
    add("conv_b", 10)
    add("ga_b", 2 * 10)
    add("gx_b", 2 * 10)
    add("lam", 2 * 10)
    off["_n"] = n
    return off


VOFF = _vec_layout()


def seq_tiles(with_ctx=True):
    t = []
    if with_ctx:
        t.append((0, TC, True))
    for i in range(TL // 512):
        t.append((TC + 512 * i, 512, False))
    return t


def build_program(steps, debug_x=False):
    nc = bass.Bass("TRN2", target_bir_lowering=False)
    P = Prog(nc)

    def din(name, shape, dt=F32):
        return nc.dram_tensor(name, list(shape), dt, kind="ExternalInput").ap()

    def dscr(name, shape, dt=F32):
        if debug_x and name in ("FT", "QT", "KT", "VV"):
            return nc.dram_tensor("dbg_" + name, list(shape), dt, kind="ExternalOutput").ap()
        return nc.dram_tensor(name, list(shape), dt, kind="Internal").ap()

    x0 = din("x0", [NSEQ, D, TT])
    cvec = din("cvec", [128, 8, 4])
    vecs = din("vecs", [128, VOFF["_n"]])
    ada_w = din("ada_w", [DEPTH, D, 6 * D])
    mlp_w1 = din("mlp_w1", [DEPTH, D, DFF])
    mlp_w2 = din("mlp_w2", [DEPTH, DFF, D])
    fnet_w_in = din("fnet_w_in", [2, D, D])
    fnet_w_out = din("fnet_w_out", [2, D, D])
    attn_w_qkv = din("attn_w_qkv", [D, 1536])
    attn_w_o = din("attn_w_o", [D, D])
    lru_w_in = din("lru_w_in", [D, 2 * DRNN])
    lru_ga_w = din("lru_ga_w", [2, 10, 128, 128])
    lru_gx_w = din("lru_gx_w", [2, 10, 128, 128])
    lru_w_out = din("lru_w_out", [DRNN, D])
    dft_lat = din("dft_lat", [8, 32, 128, 2, 512], BF16)
    dft_ctx = din("dft_ctx", [1, 2, 128, 2, 256], BF16)
    cdft = din("cdft", [128, 256], BF16)
    rope_cos = din("rope_cos", [128, TL])
    rope_sin = din("rope_sin", [128, TL])
    pswap = din("pswap", [128, 128])

    out = nc.dram_tensor("out", [NSEQ, D, TL], F32, kind="ExternalOutput").ap()
    if debug_x:
        X = nc.dram_tensor("xdbg", [NSEQ, D, TT], F32, kind="ExternalOutput").ap()
    else:
        X = dscr("X", [NSEQ, D, TT])
    AB = dscr("AB", [NSEQ, TT, 2048], BF16)
    FT = dscr("FT", [NSEQ, DRNN, TT], BF16)
    QT = dscr("QT", [NSEQ, 8 * 128, TT], BF16)
    KT = dscr("KT", [NSEQ, 2 * 128, TT], BF16)
    VV = dscr("VV", [NSEQ, TT, 256], BF16)
    GG = dscr("GG", [NSEQ, DRNN, TT])
    XR = dscr("XR", [NSEQ, DRNN, TT])

    psum = P.psum
    ps_rr = [0]

    def next_ps():
        i = ps_rr[0]
        ps_rr[0] = (i + 1) % 8
        return psum[i]

    vec_t, vec_b = P.sb([128, VOFF["_n"]], F32, "vecs")
    mod_t, mod_b = P.sb([128, DEPTH, 48, 4], F32, "mod")
    gp_t, gp_b = P.sb([128, DEPTH, 2, 8, 4], F32, "gp")
    ones_t, ones_b = P.sb([128, 128], BF16, "ones")
    misc_t, misc_b = P.sb([128, 64], F32, "misc")

    def vcol(name, i=0):
        o = VOFF[name] + i
        return vec_t[:, o:o + 1]

    first_x_src = [True]

    def prologue():
        P.phase_begin()
        P.op("sp", lambda e: e.dma_start(out=vec_t[:], in_=vecs[:, :]), writes=[vec_b], dma=vec_b)
        P.op("dve", lambda e: e.memset(ones_t[:], 1.0), writes=[ones_b])
        cv_t, cv_b = P.sb([128, 8, 4], F32, "cv")
        st_t, st_b = P.sb([128, 8, 4], F32, "sT")
        P.op("sp", lambda e: e.dma_start(out=cv_t[:], in_=cvec[:, :, :]), writes=[cv_b], dma=cv_b)
        P.op("act", lambda e: e.activation(out=st_t[:], in_=cv_t[:], func=AF.Silu), reads=[cv_b], writes=[st_b])
        P.op("dve", lambda e: e.tensor_scalar(out=misc_t[:, 0:1], in0=vcol("qg"), scalar1=float(128.0 ** -0.5),
                                              scalar2=None, op0=ALU.mult), reads=[vec_b], writes=[misc_b])
        P.op("dve", lambda e: e.tensor_scalar(out=misc_t[:, 1:2], in0=vcol("kg"), scalar1=1.0,
                                              scalar2=None, op0=ALU.mult), reads=[vec_b, misc_b], writes=[misc_b])
        lo = VOFF["lam"]
        P.op("act", lambda e: e.activation(out=misc_t[:, 2:22], in_=vec_t[:, lo:lo + 20], func=AF.Exp, scale=-1.0),
             reads=[vec_b, misc_b], writes=[misc_b])
        P.op("act", lambda e: e.activation(out=misc_t[:, 2:22], in_=misc_t[:, 2:22], func=AF.Ln, bias=1.0),
             reads=[misc_b], writes=[misc_b])
        P.op("dve", lambda e: e.tensor_scalar(out=misc_t[:, 2:22], in0=misc_t[:, 2:22], scalar1=-8.0,
                                              scalar2=None, op0=ALU.mult), reads=[misc_b], writes=[misc_b])
        aw = [P.sb([128, 8, 512], F32, "aw") for _ in range(2)]
        it = 0
        for l in range(DEPTH):
            for fg in range(12):
                awt, awb = aw[it % 2]
                it += 1
                src = ada_w[l, :, fg * 512:(fg + 1) * 512].rearrange("(c p) f -> p c f", p=128)
                P.op("sp", lambda e, awt=awt, src=src: e.dma_start(out=awt[:], in_=src), writes=[awb], dma=awb)
                for fc in range(4):
                    ch = fg * 4 + fc
                    pt, pb = next_ps()
                    for kc in range(8):
                        P.op("pe", lambda e, pt=pt, awt=awt, fc=fc, kc=kc: e.matmul(
                            pt[:, 0:4], lhsT=awt[:, kc, fc * 128:(fc + 1) * 128], rhs=st_t[:, kc, :],
                            start=(kc == 0), stop=(kc == 7)), reads=[awb, st_b], writes=[pb])
                    P.op("act", lambda e, pt=pt, l=l, ch=ch: e.activation(
                        out=mod_t[:, l, ch, :], in_=pt[:, 0:4], func=AF.Identity,
                        bias=vcol("ada_b", l * 48 + ch)), reads=[pb, vec_b], writes=[mod_b])
        for l in range(DEPTH):
            for which, (nname, j) in enumerate((("nmix", 1), ("nmlp", 4))):
                for c in range(8):
                    P.op("dve", lambda e, l=l, which=which, nname=nname, j=j, c=c: e.tensor_scalar(
                        out=gp_t[:, l, which, c, :], in0=mod_t[:, l, j * 8 + c, :], scalar1=1.0,
                        scalar2=vcol(nname, l * 8 + c), op0=ALU.add, op1=ALU.mult),
                        reads=[mod_b, vec_b], writes=[gp_b])
        P.phase_end()

    def x_src():
        return x0 if first_x_src[0] else X

    def load_w(w_t, w_b, src_ap, kc_n, ncols, piece=1024):
        view = src_ap.rearrange("(c p) f -> p c f", p=128)
        nsp = (ncols + 2047) // 2048
        wcol = ncols // nsp
        bufs = []
        for i in range(nsp):
            b = w_b if i == 0 else P.newbuf("wpart")
            bufs.append(b)
            c0 = i * wcol
            P.op("pool", lambda e, c0=c0: e.dma_start(out=w_t[:, :, c0:c0 + wcol], in_=view[:, :, c0:c0 + wcol]),
                 writes=[b], dma=b)
        return lambda col: bufs[col // wcol]

    def norm_mod(xt, xb, N, l, which, row, sq, sqb, rt, rtb, rs, rsb, h, hb, shj):
        P.op("act", lambda e: e.activation(out=sq[:, :, :N], in_=xt[:, :, :N], func=AF.Square),
             reads=[xb], writes=[sqb])
        pt, pb = next_ps()
        for c in range(8):
            P.op("pe", lambda e, c=c: e.matmul(pt[:, :N], lhsT=ones_t[:], rhs=sq[:, c, :N],
                                               start=(c == 0), stop=(c == 7)), reads=[sqb, ones_b], writes=[pb])
        P.op("act", lambda e: e.activation(out=rt[:, :N], in_=pt[:, :N], func=AF.Sqrt, scale=1.0 / D, bias=EPS),
             reads=[pb], writes=[rtb])
        P.op("dve", lambda e: e.reciprocal(out=rs[:, :N], in_=rt[:, :N]), reads=[rtb], writes=[rsb])
        for c in range(8):
            P.op("dve", lambda e, c=c: e.scalar_tensor_tensor(
                out=rt[:, :N], in0=xt[:, c, :N], scalar=gp_t[:, l, which, c, row:row + 1], in1=rs[:, :N],
                op0=ALU.mult, op1=ALU.mult), reads=[xb, rsb, gp_b], writes=[rtb])
            P.op("act", lambda e, c=c: e.activation(
                out=h[:, c, :N], in_=rt[:, :N], func=AF.Identity,
                bias=mod_t[:, l, shj * 8 + c, row:row + 1]), reads=[rtb, mod_b], writes=[hb])

    def norm_mod_ip(xt, xcb, N, l, which, row, h, hb, rt, rtb, rs, rsb, shj):
        P.op("act", lambda e: e.activation(out=h[:, :, :N], in_=xt[:, :, :N], func=AF.Square),
             reads=xcb, writes=[hb])
        pt, pb = next_ps()
        for c in range(8):
            P.op("pe", lambda e, c=c: e.matmul(pt[:, :N], lhsT=ones_t[:], rhs=h[:, c, :N],
                                               start=(c == 0), stop=(c == 7)), reads=[hb, ones_b], writes=[pb])
        P.op("act", lambda e: e.activation(out=rt[:, :N], in_=pt[:, :N], func=AF.Sqrt, scale=1.0 / D, bias=EPS),
             reads=[pb], writes=[rtb])
        P.op("dve", lambda e: e.reciprocal(out=rs[:, :N], in_=rt[:, :N]), reads=[rtb], writes=[rsb])
        for c in range(8):
            P.op("dve", lambda e, c=c: e.tensor_tensor(
                out=xt[:, c, :N], in0=xt[:, c, :N], in1=rs[:, :N], op=ALU.mult),
                reads=[xcb[c], rsb], writes=[xcb[c]])
            P.op("act", lambda e, c=c: e.activation(
                out=h[:, c, :N], in_=xt[:, c, :N], func=AF.Identity,
                scale=gp_t[:, l, which, c, row:row + 1], bias=mod_t[:, l, shj * 8 + c, row:row + 1]),
                reads=[xcb[c], gp_b, mod_b], writes=[hb])

    def pipelined_tiles(l, with_ctx, body):
        xts = [P.sb([128, 8, 512], F32, "xt") for _ in range(2)]
        xcbs = [[P.newbuf("xtc%d" % c) for c in range(8)] for _ in range(2)]
        hs = [P.sb([128, 8, 512], BF16, "h") for _ in range(2)]
        rt, rtb = P.sb([128, 512], F32, "rt")
        rs, rsb = P.sb([128, 512], F32, "rs")
        tiles = [(s, t0, N, is_ctx) for s in range(NSEQ) for (t0, N, is_ctx) in seq_tiles(with_ctx)]
        nt = len(tiles)

        def load(i):
            s, t0, N, is_ctx = tiles[i]
            xt, xb = xts[i % 2]
            xsrc = x_src()[s, :, t0:t0 + N].rearrange("(c p) t -> p c t", p=128)
            P.op("sp", lambda e: e.dma_start(out=xt[:, :, :N], in_=xsrc), writes=xcbs[i % 2], dma=xb)

        def norm(i):
            s, t0, N, is_ctx = tiles[i]
            xt, xb = xts[i % 2]
            h_t, h_b = hs[i % 2]
            norm_mod_ip(xt, xcbs[i % 2], N, l, 0, 2 if is_ctx else s, h_t, h_b, rt, rtb, rs, rsb, 0)

        load(0)
        if nt > 1:
            load(1)
        norm(0)
        for i in range(nt):
            if i + 1 < nt:
                norm(i + 1)
            if i + 2 < nt:
                load(i + 2)
            s, t0, N, is_ctx = tiles[i]
            h_t, h_b = hs[i % 2]
            body(i, s, t0, N, is_ctx, 2 if is_ctx else s, h_t, h_b)

    def proj_residual(l, w_src, kc_n, src_dram, with_ctx):
        P.phase_begin()
        w_t, w_b = P.sb([128, kc_n, D], BF16, "wout")
        wl = load_w(w_t, w_b, w_src, kc_n, D)
        xts = [P.sb([128, 8, 512], F32, "xt") for _ in range(2)]
        srs = [P.sb([128, kc_n, 512], BF16, "src") for _ in range(2)]
        it = 0
        for s in range(NSEQ):
            for (t0, N, is_ctx) in seq_tiles(with_ctx):
                row = 2 if is_ctx else s
                xt, xb = xts[it % 2]
                sr, srb = srs[it % 2]
                it += 1
                xsrc = x_src()[s, :, t0:t0 + N].rearrange("(c p) t -> p c t", p=128)
                P.op("sp", lambda e, xt=xt, xsrc=xsrc, N=N: e.dma_start(out=xt[:, :, :N], in_=xsrc),
                     writes=[xb], dma=xb)
                ssrc = src_dram[s, 0:kc_n * 128, t0:t0 + N].rearrange("(c p) t -> p c t", p=128)
                P.op("sp", lambda e, sr=sr, ssrc=ssrc, N=N: e.dma_start(out=sr[:, :, :N], in_=ssrc),
                     writes=[srb], dma=srb)
                for m in range(8):
                    pt, pb = next_ps()
                    for kc in range(kc_n):
                        P.op("pe", lambda e, pt=pt, m=m, kc=kc, sr=sr, N=N: e.matmul(
                            pt[:, :N], lhsT=w_t[:, kc, m * 128:(m + 1) * 128], rhs=sr[:, kc, :N],
                            start=(kc == 0), stop=(kc == kc_n - 1)), reads=[wl(m * 128), srb], writes=[pb])
                    P.op("dve", lambda e, pt=pt, m=m, xt=xt, N=N, row=row: e.scalar_tensor_tensor(
                        out=xt[:, m, :N], in0=pt[:, :N], scalar=mod_t[:, l, 2 * 8 + m, row:row + 1],
                        in1=xt[:, m, :N], op0=ALU.mult, op1=ALU.add), reads=[pb, xb, mod_b], writes=[xb])
                xdst = X[s, :, t0:t0 + N].rearrange("(c p) t -> p c t", p=128)
                P.op("pool", lambda e, xt=xt, xdst=xdst, N=N: e.dma_start(out=xdst, in_=xt[:, :, :N]),
                     reads=[xb], dma=xb)
        P.phase_end()
        first_x_src[0] = False

    def mlp_phase(l, with_ctx, final):
        P.phase_begin()
        w1_t, w1_b = P.sb([128, 8, DFF], BF16, "w1")
        w2_t, w2_b = P.sb([128, 32, D], BF16, "w2")
        wl1 = load_w(w1_t, w1_b, mlp_w1[l], 8, DFF)
        wl2 = load_w(w2_t, w2_b, mlp_w2[l], 32, D)
        xts = [P.sb([128, 8, 512], F32, "xt") for _ in range(2)]
        h_t, h_b = P.sb([128, 8, 512], BF16, "h")
        sq_t, sq_b = P.sb([128, 8, 512], BF16, "sq")
        a_t, a_b = P.sb([128, 16, 512], BF16, "a")
        rt, rtb = P.sb([128, 512], F32, "rt")
        rs, rsb = P.sb([128, 512], F32, "rs")
        rr = [P.sb([128, 512], F32, "relu") for _ in range(2)]
        tiles = [(s, t0, N, is_ctx) for s in range(NSEQ) for (t0, N, is_ctx) in seq_tiles(with_ctx)]
        nt = len(tiles)
        cnt = {"r": 0}

        def tinfo(i):
            s, t0, N, is_ctx = tiles[i]
            xt, xb = xts[i % 2]
            return s, t0, N, (2 if is_ctx else s), xt, xb

        def load(i):
            s, t0, N, row, xt, xb = tinfo(i)
            xsrc = x_src()[s, :, t0:t0 + N].rearrange("(c p) t -> p c t", p=128)
            P.op("sp", lambda e: e.dma_start(out=xt[:, :, :N], in_=xsrc), writes=[xb], dma=xb)

        def rstd_chain(src_t, src_b, N):
            pt, pb = next_ps()
            for c in range(8):
                P.op("pe", lambda e, c=c: e.matmul(pt[:, :N], lhsT=ones_t[:], rhs=sq_t[:, c, :N],
                                                   start=(c == 0), stop=(c == 7)), reads=[sq_b, ones_b], writes=[pb])
            P.op("act", lambda e: e.activation(out=rt[:, :N], in_=pt[:, :N], func=AF.Sqrt, scale=1.0 / D, bias=EPS),
                 reads=[pb], writes=[rtb])
            P.op("dve", lambda e: e.reciprocal(out=rs[:, :N], in_=rt[:, :N]), reads=[rtb], writes=[rsb])

        def norm_a(i):
            s, t0, N, row, xt, xb = tinfo(i)
            P.op("act", lambda e: e.activation(out=sq_t[:, :, :N], in_=xt[:, :, :N], func=AF.Square),
                 reads=[xb], writes=[sq_b])

        def norm_b(i):
            s, t0, N, row, xt, xb = tinfo(i)
            rstd_chain(xt, xb, N)

        def norm_c(i):
            s, t0, N, row, xt, xb = tinfo(i)
            for c in range(8):
                tm_t, tm_b = rr[c % 2]
                P.op("dve", lambda e, c=c, tm_t=tm_t: e.scalar_tensor_tensor(
                    out=tm_t[:, :N], in0=xt[:, c, :N], scalar=gp_t[:, l, 1, c, row:row + 1], in1=rs[:, :N],
                    op0=ALU.mult, op1=ALU.mult), reads=[xb, rsb, gp_b], writes=[tm_b])
                P.op("act", lambda e, c=c, tm_t=tm_t: e.activation(
                    out=h_t[:, c, :N], in_=tm_t[:, :N], func=AF.Identity,
                    bias=mod_t[:, l, 3 * 8 + c, row:row + 1]), reads=[tm_b, mod_b], writes=[h_b])

        def w1(i, half):
            s, t0, N, row, xt, xb = tinfo(i)
            for fi in range(16):
                f = half * 16 + fi
                pt, pb = next_ps()
                for kc in range(8):
                    P.op("pe", lambda e, pt=pt, f=f, kc=kc: e.matmul(
                        pt[:, :N], lhsT=w1_t[:, kc, f * 128:(f + 1) * 128], rhs=h_t[:, kc, :N],
                        start=(kc == 0), stop=(kc == 7)), reads=[wl1(f * 128), h_b], writes=[pb])
                r_t, r_b = rr[cnt["r"] % 2]
                cnt["r"] += 1
                P.op("act", lambda e, pt=pt, r_t=r_t: e.activation(
                    out=r_t[:, :N], in_=pt[:, :N], func=AF.Relu), reads=[pb], writes=[r_b])
                P.op("dve", lambda e, r_t=r_t, fi=fi: e.tensor_tensor(
                    out=a_t[:, fi, :N], in0=r_t[:, :N], in1=r_t[:, :N], op=ALU.mult),
                    reads=[r_b], writes=[a_b])

        def w2(i, half):
            s, t0, N, row, xt, xb = tinfo(i)
            for m in range(8):
                pt, pb = next_ps()
                for fi in range(16):
                    f = half * 16 + fi
                    P.op("pe", lambda e, pt=pt, m=m, f=f, fi=fi: e.matmul(
                        pt[:, :N], lhsT=w2_t[:, f, m * 128:(m + 1) * 128], rhs=a_t[:, fi, :N],
                        start=(fi == 0), stop=(fi == 15)), reads=[wl2(m * 128), a_b], writes=[pb])
                P.op("dve", lambda e, pt=pt, m=m: e.scalar_tensor_tensor(
                    out=xt[:, m, :N], in0=pt[:, :N], scalar=mod_t[:, l, 5 * 8 + m, row:row + 1],
                    in1=xt[:, m, :N], op0=ALU.mult, op1=ALU.add), reads=[pb, xb, mod_b], writes=[xb])

        def fin_a(i):
            s, t0, N, row, xt, xb = tinfo(i)
            P.op("act", lambda e: e.activation(out=sq_t[:, :, :N], in_=xt[:, :, :N], func=AF.Square),
                 reads=[xb], writes=[sq_b])

        def fin_b(i):
            s, t0, N, row, xt, xb = tinfo(i)
            rstd_chain(xt, xb, N)
            for c in range(8):
                P.op("dve", lambda e, c=c: e.scalar_tensor_tensor(
                    out=xt[:, c, :N], in0=xt[:, c, :N], scalar=vcol("fin", c), in1=rs[:, :N],
                    op0=ALU.mult, op1=ALU.mult), reads=[xb, rsb, vec_b], writes=[xb])

        def store(i):
            s, t0, N, row, xt, xb = tinfo(i)
            if final:
                xdst = out[s, :, t0 - TC:t0 - TC + N].rearrange("(c p) t -> p c t", p=128)
            else:
                xdst = X[s, :, t0:t0 + N].rearrange("(c p) t -> p c t", p=128)
            P.op("pool", lambda e: e.dma_start(out=xdst, in_=xt[:, :, :N]), reads=[xb], dma=xb)

        load(0)
        if nt > 1:
            load(1)
        norm_a(0)
        norm_b(0)
        norm_c(0)
        for i in range(nt):
            w1(i, 0)
            if final and i > 0:
                fin_b(i - 1)
                store(i - 1)
                if i + 1 < nt:
                    load(i + 1)
            w2(i, 0)
            if i + 1 < nt:
                norm_a(i + 1)
            w1(i, 1)
            if i + 1 < nt:
                norm_b(i + 1)
                norm_c(i + 1)
            w2(i, 1)
            if final:
                fin_a(i)
                if i == nt - 1:
                    fin_b(i)
                    store(i)
            else:
                store(i)
                if i + 2 < nt:
                    load(i + 2)
        P.phase_end()
        first_x_src[0] = False

    def fnet_phase(l, j, with_ctx):
        P.phase_begin()
        w_t, w_b = P.sb([128, 8, D], BF16, "win")
        wl = load_w(w_t, w_b, fnet_w_in[j], 8, D)
        cd_t, cd_b = P.sb([128, 256], BF16, "cdft")
        P.op("sp", lambda e: e.dma_start(out=cd_t[:], in_=cdft[:, :]), writes=[cd_b], dma=cd_b)
        us = [P.sb([128, 8, 512], BF16, "u") for _ in range(2)]
        abs_ = [P.sb([128, 4, 2048], BF16, "ab") for _ in range(2)]
        cnt = {"ev": 0}

        def evac(pt, pb, out_ap, ob, width):
            cnt["ev"] += 1
            if cnt["ev"] % 2:
                P.op("act", lambda e: e.activation(out=out_ap, in_=pt[:, :width], func=AF.Copy),
                     reads=[pb], writes=[ob])
            else:
                P.op("dve", lambda e: e.tensor_copy(out=out_ap, in_=pt[:, :width]), reads=[pb], writes=[ob])

        def fn1_body(i, s, t0, N, is_ctx, row, h_t, h_b):
            u_t, u_b = us[i % 2]
            ab_t, ab_b = abs_[i % 2]
            for g in range(8):
                pt, pb = next_ps()
                for kc in range(8):
                    P.op("pe", lambda e, pt=pt, g=g, kc=kc: e.matmul(
                        pt[:, :N], lhsT=w_t[:, kc, g * 128:(g + 1) * 128], rhs=h_t[:, kc, :N],
                        start=(kc == 0), stop=(kc == 7)), reads=[wl(g * 128), h_b], writes=[pb])
                evac(pt, pb, u_t[:, g, :N], u_b, N)
            nqc = N // 128
            for qc in range(nqc):
                for gp2 in range(4):
                    pt, pb = next_ps()
                    for gg in range(2):
                        g = gp2 * 2 + gg
                        P.op("pe", lambda e, pt=pt, g=g, gg=gg, qc=qc: e.matmul(
                            pt[:, gg * 256:(gg + 1) * 256], lhsT=u_t[:, g, qc * 128:(qc + 1) * 128],
                            rhs=cd_t[:], start=True, stop=True), reads=[u_b, cd_b], writes=[pb])
                    evac(pt, pb, ab_t[:, qc, gp2 * 512:(gp2 + 1) * 512], ab_b, 512)
            dst = AB[s, t0:t0 + N, :].rearrange("(q p) f -> p q f", p=128)
            P.op("pool", lambda e: e.dma_start(out=dst, in_=ab_t[:, :nqc, :]), reads=[ab_b], dma=ab_b)

        pipelined_tiles(l, with_ctx, fn1_body)
        P.phase_end()

        P.phase_begin()
        abr = [P.sb([128, 8, 2048], BF16, "abr") for _ in range(4)]
        dfr = [P.sb([128, 2, 512], BF16, "dft") for _ in range(6)]
        fts = [P.sb([128, 8, 512], BF16, "ft") for _ in range(2)]
        ndf = 0
        nft = 0
        nev = 0
        for s in range(NSEQ):
            segs = [(TC, TL, dft_lat, 512, 8)]
            if with_ctx:
                segs.append((0, TC, dft_ctx, 256, 1))
            for (t0, T, table, KN, nkt) in segs:
                ntc = T // 128
                nq = (ntc + 7) // 8
                for qi in range(nq):
                    c0 = qi * 8
                    cn = min(8, ntc - c0)
                    a_t, a_b = abr[qi]
                    src = AB[s, t0 + c0 * 128:t0 + (c0 + cn) * 128, :].rearrange("(c p) f -> p c f", p=128)
                    P.op("sp", lambda e, a_t=a_t, src=src, cn=cn: e.dma_start(out=a_t[:, :cn, :], in_=src),
                         writes=[a_b], dma=a_b)
                for kt in range(nkt):
                    banks = psum
                    for tc in range(ntc):
                        d_t, d_b = dfr[ndf % 6]
                        ndf += 1
                        P.op("sp", lambda e, d_t=d_t, table=table, kt=kt, tc=tc, KN=KN: e.dma_start(
                            out=d_t[:, :, :KN], in_=table[kt, tc, :, :, :]), writes=[d_b], dma=d_b)
                        a_t, a_b = abr[tc // 8]
                        ci = tc % 8
                        for g in range(8):
                            pt, pb = banks[g]
                            for ab in range(2):
                                P.op("pe", lambda e, pt=pt, a_t=a_t, ci=ci, g=g, ab=ab, d_t=d_t, tc=tc, KN=KN,
                                     ntc=ntc: e.matmul(
                                    pt[:, :KN], lhsT=a_t[:, ci, g * 256 + ab * 128:g * 256 + ab * 128 + 128],
                                    rhs=d_t[:, ab, :KN], start=(tc == 0 and ab == 0),
                                    stop=(tc == ntc - 1 and ab == 1)), reads=[a_b, d_b], writes=[pb])
                    f_t, f_b = fts[nft % 2]
                    nft += 1
                    for g in range(8):
                        pt, pb = banks[g]
                        nev += 1
                        if nev % 2:
                            P.op("act", lambda e, pt=pt, f_t=f_t, g=g, KN=KN: e.activation(
                                out=f_t[:, g, :KN], in_=pt[:, :KN], func=AF.Copy), reads=[pb], writes=[f_b])
                        else:
                            P.op("dve", lambda e, pt=pt, f_t=f_t, g=g, KN=KN: e.tensor_copy(
                                out=f_t[:, g, :KN], in_=pt[:, :KN]), reads=[pb], writes=[f_b])
                    k0 = t0 + kt * KN
                    dst = FT[s, 0:D, k0:k0 + KN].rearrange("(c p) t -> p c t", p=128)
                    P.op("pool", lambda e, f_t=f_t, dst=dst, KN=KN: e.dma_start(out=dst, in_=f_t[:, :, :KN]),
                         reads=[f_b], dma=f_b)
        P.phase_end()
        proj_residual(l, fnet_w_out[j], 8, FT, with_ctx)

    def attn_phase(l):
        P.phase_begin()
        w_t, w_b = P.sb([128, 8, 1536], BF16, "wqkv")
        wl = load_w(w_t, w_b, attn_w_qkv, 8, 1536)
        cos_t, cos_b = P.sb([128, TL], F32, "cos")
        sin_t, sin_b = P.sb([128, TL], F32, "sin")
        psw_t, psw_b = P.sb([128, 128], F32, "pswap")
        P.op("sp", lambda e: e.dma_start(out=cos_t[:], in_=rope_cos[:, :]), writes=[cos_b], dma=cos_b)
        P.op("sp", lambda e: e.dma_start(out=sin_t[:], in_=rope_sin[:, :]), writes=[sin_b], dma=sin_b)
        P.op("sp", lambda e: e.dma_start(out=psw_t[:], in_=pswap[:, :]), writes=[psw_b], dma=psw_b)
        qks = [P.sb([128, 10, 512], BF16, "qk") for _ in range(2)]
        qk_kst = {id(b_): P.newbuf("qk_kstore") for (_, b_) in qks}
        qk_cb = {id(b_): [P.newbuf("qkc%d" % k) for k in range(10)] for (_, b_) in qks}
        sq_t, sq_b = P.sb([128, 10, 512], BF16, "sqh")
        qf_t, qf_b = P.sb([128, 10, 512], F32, "qf")
        rq_t, rq_b = P.sb([128, 10, 512], F32, "rq")
        t1s = [P.sb([128, 512], F32, "t1") for _ in range(2)]
        t2s = [P.sb([128, 512], F32, "t2") for _ in range(2)]
        vss = [P.sb([128, 4, 256], BF16, "vs") for _ in range(2)]
        sqb = [P.newbuf("sq%d" % k) for k in range(10)]
        qfb = [P.newbuf("qf%d" % k) for k in range(10)]
        rqb = [P.newbuf("rq%d" % k) for k in range(10)]

        def at1_body(i, s, t0, N, is_ctx, row, h_t, h_b):
            qk_t, qk_b = qks[i % 2]
            qkc = qk_cb[id(qk_b)]
            v_t, v_b = vss[i % 2]
            for hc in range(10):
                pq, pqb = next_ps()
                for kc in range(8):
                    P.op("pe", lambda e, pq=pq, hc=hc, kc=kc: e.matmul(
                        pq[:, :N], lhsT=w_t[:, kc, hc * 128:(hc + 1) * 128], rhs=h_t[:, kc, :N],
                        start=(kc == 0), stop=(kc == 7)), reads=[wl(hc * 128), h_b], writes=[pqb])
                P.op("act", lambda e, pq=pq, hc=hc: e.activation(
                    out=sq_t[:, hc, :N], in_=pq[:, :N], func=AF.Square), reads=[pqb], writes=[sqb[hc]])
                P.op("act", lambda e, pq=pq, hc=hc: e.activation(
                    out=qf_t[:, hc, :N], in_=pq[:, :N], func=AF.Copy), reads=[pqb], writes=[qfb[hc]])
            nqc = N // 128
            for qc in range(nqc):
                pv, pvb = next_ps()
                for kc in range(8):
                    P.op("pe", lambda e, pv=pv, kc=kc, qc=qc: e.matmul(
                        pv[:, 0:256], lhsT=h_t[:, kc, qc * 128:(qc + 1) * 128], rhs=w_t[:, kc, 1280:1536],
                        start=(kc == 0), stop=(kc == 7)), reads=[wl(1280), h_b], writes=[pvb])
                P.op("act", lambda e, pv=pv, qc=qc: e.activation(out=v_t[:, qc, :], in_=pv[:, 0:256], func=AF.Copy),
                     reads=[pvb], writes=[v_b])
            vdst = VV[s, t0:t0 + N, :].rearrange("(q p) d -> p q d", p=128)
            P.op("pool", lambda e: e.dma_start(out=vdst, in_=v_t[:, :nqc, :]), reads=[v_b], dma=v_b)
            for hc in range(10):
                p2, p2b = next_ps()
                P.op("pe", lambda e, p2=p2, hc=hc: e.matmul(
                    p2[:, :N], lhsT=ones_t[:], rhs=sq_t[:, hc, :N], start=True, stop=True),
                    reads=[sqb[hc], ones_b], writes=[p2b])
                P.op("act", lambda e, p2=p2, hc=hc: e.activation(
                    out=rq_t[:, hc, :N], in_=p2[:, :N], func=AF.Sqrt, scale=1.0 / 128, bias=EPS),
                    reads=[p2b], writes=[rqb[hc]])
            for hc in range(10):
                gcol = misc_t[:, 0:1] if hc < 8 else misc_t[:, 1:2]
                P.op("dve", lambda e, hc=hc: e.reciprocal(out=rq_t[:, hc, :N], in_=rq_t[:, hc, :N]),
                     reads=[rqb[hc]], writes=[rqb[hc]])
                if is_ctx:
                    P.op("dve", lambda e, hc=hc, gcol=gcol: e.scalar_tensor_tensor(
                        out=qk_t[:, hc, :N], in0=qf_t[:, hc, :N], scalar=gcol, in1=rq_t[:, hc, :N],
                        op0=ALU.mult, op1=ALU.mult), reads=[qfb[hc], rqb[hc], misc_b], writes=[qkc[hc]])
                else:
                    P.op("dve", lambda e, hc=hc, gcol=gcol: e.scalar_tensor_tensor(
                        out=qf_t[:, hc, :N], in0=qf_t[:, hc, :N], scalar=gcol, in1=rq_t[:, hc, :N],
                        op0=ALU.mult, op1=ALU.mult), reads=[qfb[hc], rqb[hc], misc_b], writes=[qfb[hc]])
            if not is_ctx:
                l0 = t0 - TC
                for hc in range(10):
                    t1_t, t1_b = t1s[hc % 2]
                    t2_t, t2_b = t2s[hc % 2]
                    p3, p3b = next_ps()
                    P.op("pe", lambda e, p3=p3, hc=hc: e.matmul(
                        p3[:, :N], lhsT=psw_t[:], rhs=qf_t[:, hc, :N], start=True, stop=True),
                        reads=[qfb[hc], psw_b], writes=[p3b])
                    e1, e2 = ("pool", "dve") if hc % 2 == 0 else ("dve", "pool")
                    P.op(e1, lambda e, t1_t=t1_t, hc=hc: e.tensor_tensor(
                        out=t1_t[:, :N], in0=qf_t[:, hc, :N], in1=cos_t[:, l0:l0 + N], op=ALU.mult),
                        reads=[qfb[hc], cos_b], writes=[t1_b])
                    P.op("dve", lambda e, t2_t=t2_t, p3=p3: e.tensor_tensor(
                        out=t2_t[:, :N], in0=p3[:, :N], in1=sin_t[:, l0:l0 + N], op=ALU.mult),
                        reads=[p3b, sin_b], writes=[t2_b])
                    P.op(e2, lambda e, hc=hc, t1_t=t1_t, t2_t=t2_t: e.tensor_tensor(
                        out=qk_t[:, hc, :N], in0=t1_t[:, :N], in1=t2_t[:, :N], op=ALU.add),
                        reads=[t1_b, t2_b], writes=[qkc[hc]])
            qdst = QT[s, :, t0:t0 + N].rearrange("(c p) t -> p c t", p=128)
            P.op("pool", lambda e: e.dma_start(out=qdst, in_=qk_t[:, 0:8, :N]), reads=qkc[0:8], dma=qk_b)
            kdst = KT[s, :, t0:t0 + N].rearrange("(c p) t -> p c t", p=128)
            P.op("pool", lambda e: e.dma_start(out=kdst, in_=qk_t[:, 8:10, :N]),
                 reads=qkc[8:10], dma=qk_kst[id(qk_b)])

        pipelined_tiles(l, True, at1_body)
        P.phase_end()

        P.phase_begin()
        kts = [P.sb([128, 2, TT], BF16, "kt") for _ in range(1)]
        vts = [P.sb([128, 34, 256], BF16, "vt") for _ in range(1)]
        qrs = [P.sb([128, 512], BF16, "q") for _ in range(3)]
        prs = [P.sb([128, 512], BF16, "p") for _ in range(4)]
        ors = [P.sb([128, 512], BF16, "o") for _ in range(2)]
        rds = [P.sb([128, 512], F32, "rd") for _ in range(2)]
        sbanks = psum[0:4]
        obanks = [(psum[4], psum[5]), (psum[6], psum[7])]
        nq = 0
        npp = 0
        nsb = 0
        for s in range(NSEQ):
            if s > 0:
                P.barrier()
            k_t, k_b = kts[0]
            v_t, v_b = vts[0]
            ksrc = KT[s, :, :].rearrange("(c p) t -> p c t", p=128)
            P.op("sp", lambda e, k_t=k_t, ksrc=ksrc: e.dma_start(out=k_t[:], in_=ksrc), writes=[k_b], dma=k_b)
            vsrc = VV[s, :, :].rearrange("(c p) d -> p c d", p=128)
            P.op("sp", lambda e, v_t=v_t, vsrc=vsrc: e.dma_start(out=v_t[:], in_=vsrc), writes=[v_b], dma=v_b)
            for hk in range(2):
                for hq in range(4):
                    hh = hk * 4 + hq
                    for (t0, N, is_ctx) in seq_tiles(True):
                        chunks = [0, 1] if is_ctx else list(range(34))
                        q_t, q_b = qrs[nq % 3]
                        (po, pob), (pd, pdb) = obanks[nq % 2]
                        o_t, o_b = ors[nq % 2]
                        rd_t, rd_b = rds[nq % 2]
                        nq += 1
                        qsrc = QT[s, hh * 128:(hh + 1) * 128, t0:t0 + N]
                        P.op("sp", lambda e, q_t=q_t, qsrc=qsrc, N=N: e.dma_start(out=q_t[:, :N], in_=qsrc),
                             writes=[q_b], dma=q_b)
                        pend = []

                        def issue_s(kc, q_t=q_t, q_b=q_b, N=N, hk=hk):
                            nonlocal nsb
                            ps_t, ps_b = sbanks[nsb % 4]
                            nsb += 1
                            P.op("pe", lambda e: e.matmul(ps_t[:, :N], lhsT=k_t[:, hk, kc * 128:(kc + 1) * 128],
                                                          rhs=q_t[:, :N], start=True, stop=True),
                                 reads=[k_b, q_b], writes=[ps_b])
                            return (ps_t, ps_b)

                        ahead = 2
                        for i in range(min(ahead, len(chunks))):
                            pend.append(issue_s(chunks[i]))
                        for i, kc in enumerate(chunks):
                            if i + ahead < len(chunks):
                                pend.append(issue_s(chunks[i + ahead]))
                            ps_t, ps_b = pend.pop(0)
                            p_t, p_b = prs[npp % 4]
                            npp += 1
                            P.op("act", lambda e, ps_t=ps_t, p_t=p_t, N=N: e.activation(
                                out=p_t[:, :N], in_=ps_t[:, :N], func=AF.Exp), reads=[ps_b], writes=[p_b])
                            first = (i == 0)
                            last = (i == len(chunks) - 1)
                            P.op("pe", lambda e, po=po, p_t=p_t, kc=kc, hk=hk, N=N, first=first, last=last: e.matmul(
                                po[:, :N], lhsT=v_t[:, kc, hk * 128:(hk + 1) * 128], rhs=p_t[:, :N],
                                start=first, stop=last), reads=[v_b, p_b], writes=[pob])
                            P.op("pe", lambda e, pd=pd, p_t=p_t, N=N, first=first, last=last: e.matmul(
                                pd[:, :N], lhsT=ones_t[:], rhs=p_t[:, :N], start=first, stop=last),
                                reads=[ones_b, p_b], writes=[pdb])
                        P.op("dve", lambda e, pd=pd, rd_t=rd_t, N=N: e.reciprocal(out=rd_t[:, :N], in_=pd[:, :N]),
                             reads=[pdb], writes=[rd_b])
                        P.op("dve", lambda e, po=po, rd_t=rd_t, o_t=o_t, N=N: e.tensor_tensor(
                            out=o_t[:, :N], in0=po[:, :N], in1=rd_t[:, :N], op=ALU.mult),
                            reads=[pob, rd_b], writes=[o_b])
                        odst = FT[s, hh * 128:(hh + 1) * 128, t0:t0 + N]
                        P.op("pool", lambda e, o_t=o_t, odst=odst, N=N: e.dma_start(out=odst, in_=o_t[:, :N]),
                             reads=[o_b], dma=o_b)
        P.phase_end()
        proj_residual(l, attn_w_o, 8, FT, True)

    def lru_phase(l):
        P.phase_begin()
        w_t, w_b = P.sb([128, 8, 2 * DRNN], BF16, "wlin")
        wl = load_w(w_t, w_b, lru_w_in, 8, 2 * DRNN)
        gos = [P.sb([128, 10, 512], F32, "go") for _ in range(1)]
        xos = [P.sb([128, 10, 512], F32, "xo") for _ in range(1)]
        x2s = [P.sb([128, 512], F32, "x2") for _ in range(3)]
        ins_ = [P.sb([128, 512], F32, "inn") for _ in range(3)]
        goc = [P.newbuf("goc%d" % k) for k in range(10)]
        xoc = [P.newbuf("xoc%d" % k) for k in range(10)]

        def lr1_body(i, s, t0, N, is_ctx, row, h_t, h_b):
            go_t, go_b = gos[0]
            xo_t, xo_b = xos[0]

            def proj(oc):
                pt, pb = next_ps()
                for kc in range(8):
                    P.op("pe", lambda e, kc=kc: e.matmul(
                        pt[:, :N], lhsT=w_t[:, kc, oc * 128:(oc + 1) * 128], rhs=h_t[:, kc, :N],
                        start=(kc == 0), stop=(kc == 7)), reads=[wl(oc * 128), h_b], writes=[pb])
                return pt, pb

            for oc in range(10, 20):
                pt, pb = proj(oc)
                P.op("act", lambda e, pt=pt, oc=oc: e.activation(
                    out=xo_t[:, oc - 10, :N], in_=pt[:, :N], func=AF.Copy), reads=[pb], writes=[xoc[oc - 10]])
            xdst = XR[s, :, t0:t0 + N].rearrange("(c p) t -> p c t", p=128)
            P.op("pool", lambda e: e.dma_start(out=xdst, in_=xo_t[:, :, :N]), reads=xoc, dma=xo_b)

            def st_a(oc):
                pt, pb = proj(oc)
                x2_t, x2_b = x2s[oc % 3]
                in_t, in_b = ins_[oc % 3]
                P.op("act", lambda e: e.activation(out=x2_t[:, :N], in_=pt[:, :N], func=AF.Square),
                     reads=[pb], writes=[x2_b])
                P.op("dve", lambda e: e.tensor_scalar(
                    out=x2_t[:, :N], in0=x2_t[:, :N], scalar1=0.044715, scalar2=1.0, op0=ALU.mult, op1=ALU.add),
                    reads=[x2_b], writes=[x2_b])
                P.op("dve", lambda e: e.tensor_tensor(
                    out=in_t[:, :N], in0=pt[:, :N], in1=x2_t[:, :N], op=ALU.mult),
                    reads=[pb, x2_b], writes=[in_b])
                return pt, pb, in_t, in_b

            def st_b(oc, st):
                pt, pb, in_t, in_b = st
                P.op("act", lambda e: e.activation(
                    out=in_t[:, :N], in_=in_t[:, :N], func=AF.Sigmoid, scale=1.5957691216057308),
                    reads=[in_b], writes=[in_b])
                P.op("dve", lambda e: e.tensor_tensor(
                    out=go_t[:, oc, :N], in0=pt[:, :N], in1=in_t[:, :N], op=ALU.mult),
                    reads=[pb, in_b], writes=[goc[oc]])

            prev = st_a(0)
            for oc in range(10):
                nxt = st_a(oc + 1) if oc + 1 < 10 else None
                st_b(oc, prev)
                prev = nxt
            gdst = GG[s, :, t0:t0 + N].rearrange("(c p) t -> p c t", p=128)
            P.op("pool", lambda e: e.dma_start(out=gdst, in_=go_t[:, :, :N]), reads=goc, dma=go_b)

        pipelined_tiles(l, True, lr1_body)
        P.phase_end()

        P.phase_begin()
        ga_t, ga_b = P.sb([128, 20, 128], BF16, "ga")
        gx_t, gx_b = P.sb([128, 20, 128], BF16, "gx")
        P.op("pool", lambda e: e.dma_start(out=ga_t[:], in_=lru_ga_w.rearrange("d n k j -> k (d n) j")),
             writes=[ga_b], dma=ga_b)
        P.op("pool", lambda e: e.dma_start(out=gx_t[:], in_=lru_gx_w.rearrange("d n k j -> k (d n) j")),
             writes=[gx_b], dma=gx_b)
        xr_t, xr_b = P.sb([128, TT], F32, "xr")
        g_t, g_b = P.sb([128, TT], F32, "g")
        xcs = [P.sb([128, TT], F32, "xc") for _ in range(2)]
        xcbs_ = [P.sb([128, TT], BF16, "xcb") for _ in range(2)]
        ra_t, ra_b = P.sb([128, TT], F32, "ra")
        iu_t, iu_b = P.sb([128, TT], F32, "iu")
        s_t, s_b = P.sb([128, TT], F32, "s")
        hf_t, hf_b = P.sb([128, TT], F32, "hf")
        hb_t, hb_b = P.sb([128, TT], F32, "hb")
        o_t, o_b = P.sb([128, TT], BF16, "o")
        segs = [(0, TC), (TC, TL)]
        items = [(s, n) for s in range(NSEQ) for n in range(10)]

        def ld_xr(k):
            s, n = items[k]
            P.op("sp", lambda e: e.dma_start(out=xr_t[:], in_=XR[s, n * 128:(n + 1) * 128, :]),
                 writes=[xr_b], dma=xr_b)

        def ld_g(k):
            s, n = items[k]
            P.op("sp", lambda e: e.dma_start(out=g_t[:], in_=GG[s, n * 128:(n + 1) * 128, :]),
                 writes=[g_b], dma=g_b)

        def conv(k):
            s, n = items[k]
            xc_t, xc_b = xcs[k % 2]
            P.op("dve", lambda e: e.tensor_scalar(
                out=xc_t[:], in0=xr_t[:], scalar1=vcol("conv_w", 2 * 10 + n), scalar2=vcol("conv_b", n),
                op0=ALU.mult, op1=ALU.add), reads=[xr_b, vec_b], writes=[xc_b])
            for (o0, L) in segs:
                for kk, sh in ((0, -2), (1, -1), (3, 1)):
                    if sh < 0:
                        src_sl = (o0, o0 + L + sh)
                        dst_sl = (o0 - sh, o0 + L)
                    else:
                        src_sl = (o0 + sh, o0 + L)
                        dst_sl = (o0, o0 + L - sh)
                    P.op("dve", lambda e, kk=kk, src_sl=src_sl, dst_sl=dst_sl: e.scalar_tensor_tensor(
                        out=xc_t[:, dst_sl[0]:dst_sl[1]], in0=xr_t[:, src_sl[0]:src_sl[1]],
                        scalar=vcol("conv_w", kk * 10 + n), in1=xc_t[:, dst_sl[0]:dst_sl[1]],
                        op0=ALU.mult, op1=ALU.add), reads=[xr_b, xc_b, vec_b], writes=[xc_b])

        def cast(k):
            xc_t, xc_b = xcs[k % 2]
            xcb_t, xcb_b = xcbs_[k % 2]
            P.op("act", lambda e: e.activation(out=xcb_t[:], in_=xc_t[:], func=AF.Copy),
                 reads=[xc_b], writes=[xcb_b])

        def scan_dir(k, d):
            s, n = items[k]
            xc_t, xc_b = xcs[k % 2]
            xcb_t, xcb_b = xcbs_[k % 2]
            for (t0, N, is_ctx) in seq_tiles(True):
                pr, prb = next_ps()
                P.op("pe", lambda e, pr=pr, t0=t0, N=N: e.matmul(
                    pr[:, :N], lhsT=ga_t[:, d * 10 + n, :], rhs=xcb_t[:, t0:t0 + N], start=True, stop=True),
                    reads=[ga_b, xcb_b], writes=[prb])
                P.op("act", lambda e, pr=pr, t0=t0, N=N: e.activation(
                    out=ra_t[:, t0:t0 + N], in_=pr[:, :N], func=AF.Sigmoid,
                    bias=vcol("ga_b", d * 10 + n)), reads=[prb, vec_b], writes=[ra_b])
                pi, pib = next_ps()
                P.op("pe", lambda e, pi=pi, t0=t0, N=N: e.matmul(
                    pi[:, :N], lhsT=gx_t[:, d * 10 + n, :], rhs=xcb_t[:, t0:t0 + N], start=True, stop=True),
                    reads=[gx_b, xcb_b], writes=[pib])
                P.op("act", lambda e, pi=pi, t0=t0, N=N: e.activation(
                    out=iu_t[:, t0:t0 + N], in_=pi[:, :N], func=AF.Sigmoid,
                    bias=vcol("gx_b", d * 10 + n)), reads=[pib, vec_b], writes=[iu_b])
            P.op("act", lambda e: e.activation(
                out=ra_t[:], in_=ra_t[:], func=AF.Exp, scale=misc_t[:, 2 + d * 10 + n:3 + d * 10 + n]),
                reads=[ra_b, misc_b], writes=[ra_b])
            P.op("act", lambda e: e.activation(out=s_t[:], in_=ra_t[:], func=AF.Square),
                 reads=[ra_b], writes=[s_b])
            P.op("act", lambda e: e.activation(out=s_t[:], in_=s_t[:], func=AF.Sqrt, scale=-1.0, bias=1.0),
                 reads=[s_b], writes=[s_b])
            P.op("pool", lambda e: e.tensor_tensor(out=iu_t[:], in0=iu_t[:], in1=xc_t[:], op=ALU.mult),
                 reads=[iu_b, xc_b], writes=[iu_b])
            P.op("dve", lambda e: e.tensor_tensor(out=iu_t[:], in0=iu_t[:], in1=s_t[:], op=ALU.mult),
                 reads=[iu_b, s_b], writes=[iu_b])
            if d == 0:
                P.op("dve", lambda e: e.tensor_tensor_scan(
                    out=hf_t[:], data0=ra_t[:], data1=iu_t[:], initial=0.0, op0=ALU.mult, op1=ALU.add),
                    reads=[ra_b, iu_b], writes=[hf_b])
            else:
                P.op("dve", lambda e: e.tensor_tensor_scan(
                    out=hb_t[:, 0:TC][:, ::-1], data0=ra_t[:, 0:TC][:, ::-1], data1=iu_t[:, 0:TC][:, ::-1],
                    initial=0.0, op0=ALU.mult, op1=ALU.add), reads=[ra_b, iu_b], writes=[hb_b])
                P.op("dve", lambda e: e.tensor_tensor_scan(
                    out=hb_t[:, TC:TT][:, ::-1], data0=ra_t[:, TC:TT][:, ::-1],
                    data1=iu_t[:, TC:TT][:, ::-1], initial=hb_t[:, 0:1], op0=ALU.mult, op1=ALU.add),
                    reads=[ra_b, iu_b, hb_b], writes=[hb_b])

        def finish(k):
            s, n = items[k]
            P.op("pool", lambda e: e.tensor_tensor(out=hf_t[:], in0=hf_t[:], in1=hb_t[:], op=ALU.add),
                 reads=[hf_b, hb_b], writes=[hf_b])
            P.op("dve", lambda e: e.tensor_tensor(out=o_t[:], in0=hf_t[:], in1=g_t[:], op=ALU.mult),
                 reads=[hf_b, g_b], writes=[o_b])
            P.op("pool", lambda e: e.dma_start(out=FT[s, n * 128:(n + 1) * 128, :], in_=o_t[:]),
                 reads=[o_b], dma=o_b)

        nk = len(items)
        ld_xr(0)
        ld_g(0)
        conv(0)
        cast(0)
        for k in range(nk):
            scan_dir(k, 0)
            if k + 1 < nk:
                ld_xr(k + 1)
                conv(k + 1)
            scan_dir(k, 1)
            if k + 1 < nk:
                cast(k + 1)
            finish(k)
            if k + 1 < nk:
                ld_g(k + 1)
        P.phase_end()
        proj_residual(l, lru_w_out, 10, FT, True)

    prologue()
    for (l, part) in steps:
        kind = l % 3
        j = l // 3
        last = l == DEPTH - 1
        with_ctx = not last
        if part == "mix":
            if kind == 0:
                fnet_phase(l, j, with_ctx)
            elif kind == 1:
                attn_phase(l)
            else:
                lru_phase(l)
        else:
            mlp_phase(l, with_ctx, final=last)
    P.emit()
    return nc


def _consts():
    bf = ml_dtypes.bfloat16
    c = np.arange(128, dtype=np.float64)
    ang = 2 * np.pi * np.outer(c, c) / 128.0
    cd = np.concatenate([np.cos(ang), np.sin(ang)], axis=1) / np.sqrt(128.0)

    def pos_table(T, KN):
        t = np.arange(T, dtype=np.int64)
        m = np.outer(t, t) % T
        a = 2 * np.pi * m.astype(np.float64) / T
        C = np.cos(a) / np.sqrt(T)
        S = -np.sin(a) / np.sqrt(T)
        nkt = T // KN
        ntc = T // 128
        tab = np.empty((nkt, ntc, 128, 2, KN), dtype=bf)
        Cr = C.reshape(ntc, 128, nkt, KN).transpose(2, 0, 1, 3)
        Sr = S.reshape(ntc, 128, nkt, KN).transpose(2, 0, 1, 3)
        tab[:, :, :, 0, :] = Cr.astype(bf)
        tab[:, :, :, 1, :] = Sr.astype(bf)
        return tab

    dft_lat = pos_table(TL, 512)
    dft_ctx = pos_table(TC, 256)
    n_freq = 32
    inv = (10000.0 ** (-np.arange(n_freq, dtype=np.float32) / n_freq)).astype(np.float32)
    tok = np.arange(TL)
    row = (tok // 64).astype(np.float32)
    col = (tok % 64).astype(np.float32)
    ang = np.concatenate([row[:, None] * inv, col[:, None] * inv], axis=-1).astype(np.float32)
    cs = np.cos(ang).T.astype(np.float32)
    sn = np.sin(ang).T.astype(np.float32)
    rope_cos = np.concatenate([cs, cs], axis=0)
    rope_sin = np.concatenate([-sn, sn], axis=0)
    pswap = np.zeros((128, 128), np.float32)
    for m in range(128):
        pswap[(m + 64) % 128, m] = 1.0
    return dict(cdft=cd.astype(bf), dft_lat=dft_lat, dft_ctx=dft_ctx,
                rope_cos=np.ascontiguousarray(rope_cos), rope_sin=np.ascontiguousarray(rope_sin), pswap=pswap)


_CONSTS = None


def _pcol(v):
    v = np.asarray(v, np.float32)
    return np.ascontiguousarray(v.reshape(-1, 128).T)


def make_in_maps(inp, n_cores=8, cores=None):
    global _CONSTS
    if _CONSTS is None:
        _CONSTS = _consts()
    perm = np.concatenate([np.arange(0, 128, 2), np.arange(1, 128, 2)])
    wqkv = np.asarray(inp["attn_w_qkv"][0], np.float32)
    cols = []
    for hh in range(10):
        cols.append(hh * 128 + perm)
    cols.append(np.arange(1280, 1536))
    wqkv_p = np.ascontiguousarray(wqkv[:, np.concatenate(cols)])

    vecs = np.zeros((128, VOFF["_n"]), np.float32)

    def put(name, arr2d):
        vecs[:, VOFF[name]:VOFF[name] + arr2d.shape[1]] = arr2d

    put("ada_b", np.concatenate([_pcol(inp["ada_b"][l]) for l in range(DEPTH)], axis=1))
    put("nmix", np.concatenate([_pcol(inp["norm_mix_g"][l]) for l in range(DEPTH)], axis=1))
    put("nmlp", np.concatenate([_pcol(inp["norm_mlp_g"][l]) for l in range(DEPTH)], axis=1))
    put("fin", _pcol(inp["final_norm_g"]))
    put("qg", np.asarray(inp["attn_q_norm_g"][0], np.float32)[perm][:, None])
    put("kg", np.asarray(inp["attn_k_norm_g"][0], np.float32)[perm][:, None])
    put("conv_w", np.concatenate([_pcol(inp["lru_conv_w"][0][k]) for k in range(4)], axis=1))
    put("conv_b", _pcol(inp["lru_conv_b"][0]))
    put("ga_b", np.concatenate([_pcol(inp["lru_gate_a_b"][0][d]) for d in range(2)], axis=1))
    put("gx_b", np.concatenate([_pcol(inp["lru_gate_x_b"][0][d]) for d in range(2)], axis=1))
    put("lam", np.concatenate([_pcol(inp["lru_lambda"][0][d]) for d in range(2)], axis=1))

    shared = dict(
        vecs=vecs,
        ada_w=np.ascontiguousarray(inp["ada_w"], np.float32),
        mlp_w1=np.ascontiguousarray(inp["mlp_w1"], np.float32),
        mlp_w2=np.ascontiguousarray(inp["mlp_w2"], np.float32),
        fnet_w_in=np.ascontiguousarray(inp["fnet_w_in"], np.float32),
        fnet_w_out=np.ascontiguousarray(inp["fnet_w_out"], np.float32),
        attn_w_qkv=wqkv_p,
        attn_w_o=np.ascontiguousarray(inp["attn_w_o"][0], np.float32),
        lru_w_in=np.ascontiguousarray(inp["lru_w_in"][0], np.float32),
        lru_ga_w=np.ascontiguousarray(inp["lru_gate_a_w"][0], np.float32),
        lru_gx_w=np.ascontiguousarray(inp["lru_gate_x_w"][0], np.float32),
        lru_w_out=np.ascontiguousarray(inp["lru_w_out"][0], np.float32),
        **_CONSTS,
    )
    x = np.asarray(inp["x"], np.float32)
    ctx = np.asarray(inp["ctx"], np.float32)
    c = np.asarray(inp["c"], np.float32)
    c_ctx = np.asarray(inp["c_ctx"], np.float32)
    maps = []
    for core in (range(n_cores) if cores is None else cores):
        x0 = np.empty((NSEQ, D, TT), np.float32)
        cv = np.zeros((128, 8, 4), np.float32)
        for s in range(NSEQ):
            b = core * NSEQ + s
            x0[s, :, :TC] = ctx[b].T
            x0[s, :, TC:] = x[b].T
            cv[:, :, s] = c[b].reshape(8, 128).T
        cv[:, :, 2] = c_ctx.reshape(8, 128).T
        cv[:, :, 3] = c_ctx.reshape(8, 128).T
        m = dict(shared)
        m["x0"] = x0
        m["cvec"] = cv
        maps.append(m)
    return maps


ALL_STEPS = [(l, p) for l in range(DEPTH) for p in ("mix", "mlp")]


def kernel(**inputs):
    nc = build_program(ALL_STEPS)
    maps = make_in_maps(inputs, 8)
    res = run_bass_kernel_spmd(nc, maps, core_ids=list(range(8)))
    outs = []
    for core in range(8):
        o = res.results[core]["out"]
        for s in range(NSEQ):
            outs.append(np.ascontiguousarray(np.asarray(o[s], np.float32).T))
    return np.stack(outs, axis=0).astype(np.float32)
```
